# Optimizing a Trainium2 kernel written in Bass

```python
import math
import jax, jax.numpy as jnp
from jax import lax
import numpy as np

D_MODEL = 1024
BATCH = 32
SEQ = 2048
DEPTH = 4
DEC_BATCH = 2
DEC_SEQ = 16384
PAST_LEN = 128

N_MIXERS = 3
EXPAND = 2
BRANCH_WIDTH = EXPAND * D_MODEL
NORM_EPS = 1e-6

A_HEAD_DIM = 64
A_Q_HEADS = BRANCH_WIDTH // A_HEAD_DIM
A_KV_HEADS = max(1, A_Q_HEADS // 8)
A_GROUP = A_Q_HEADS // A_KV_HEADS
A_ROT_DIM = A_HEAD_DIM // 4
A_ROPE_THETA = 500000.0
A_WINDOW = 128
A_BLOCK = 128
A_KEY_SPAN = A_BLOCK + 2 * A_WINDOW
A_IN = A_Q_HEADS * A_HEAD_DIM + 2 * A_KV_HEADS * A_HEAD_DIM + BRANCH_WIDTH

B_QK_DIM = 256
B_HEADS = D_MODEL // B_QK_DIM
B_V_DIM = BRANCH_WIDTH // B_HEADS
B_CHUNK = 128
B_ROPE_THETA = 10000.0
B_IN = 2 * B_HEADS * B_QK_DIM + 2 * BRANCH_WIDTH
B_DECAY_HI = 1.0 / 32.0
B_DECAY_LO = 1.0 / 512.0

C_GROUPS = 4
C_GROUP_DIM = BRANCH_WIDTH // C_GROUPS
C_IN = 2 * BRANCH_WIDTH

kernel_name = "hybrid_bidir_swa_retention_fnet_encoder"


def rms_norm(x, g):
    x32 = x.astype(jnp.float32)
    y = x32 * lax.rsqrt(jnp.mean(x32 * x32, axis=-1, keepdims=True) + NORM_EPS)
    return (y * g.astype(jnp.float32)).astype(x.dtype)


def apply_rope(x, rot_dim, theta):
    seq = x.shape[1]
    half = rot_dim // 2
    inv_freq = jnp.exp(-(jnp.arange(half, dtype=jnp.float32) * (2.0 / rot_dim)) * math.log(theta))
    ang = jnp.arange(seq, dtype=jnp.float32)[:, None] * inv_freq[None, :]
    cos = jnp.cos(ang)[None, :, None, :]
    sin = jnp.sin(ang)[None, :, None, :]
    xr = x[..., :rot_dim].astype(jnp.float32)
    x1, x2 = xr[..., :half], xr[..., half:]
    rot = jnp.concatenate([x1 * cos - x2 * sin, x2 * cos + x1 * sin], axis=-1).astype(x.dtype)
    if rot_dim == x.shape[-1]:
        return rot
    return jnp.concatenate([rot, x[..., rot_dim:]], axis=-1)


def window_attention(h, w_in, sink, w_out):
    bsz, seq, _ = h.shape
    proj = h @ w_in
    nq = A_Q_HEADS * A_HEAD_DIM
    nkv = A_KV_HEADS * A_HEAD_DIM
    q = proj[..., :nq].reshape(bsz, seq, A_Q_HEADS, A_HEAD_DIM)
    k = proj[..., nq:nq + nkv].reshape(bsz, seq, A_KV_HEADS, A_HEAD_DIM)
    v = proj[..., nq + nkv:nq + 2 * nkv].reshape(bsz, seq, A_KV_HEADS, A_HEAD_DIM)
    gate = proj[..., nq + 2 * nkv:]
    q = apply_rope(q, A_ROT_DIM, A_ROPE_THETA) * (A_HEAD_DIM ** -0.5)
    k = apply_rope(k, A_ROT_DIM, A_ROPE_THETA)
    nb = seq // A_BLOCK
    q_blocks = q.reshape(bsz, nb, A_BLOCK, A_KV_HEADS, A_GROUP, A_HEAD_DIM).transpose(1, 0, 2, 3, 4, 5)
    kp = jnp.pad(k, ((0, 0), (A_WINDOW, A_WINDOW), (0, 0), (0, 0)))
    vp = jnp.pad(v, ((0, 0), (A_WINDOW, A_WINDOW), (0, 0), (0, 0)))
    kvalid = jnp.pad(jnp.ones((seq,), dtype=bool), (A_WINDOW, A_WINDOW))
    s_idx = jnp.arange(A_BLOCK)[:, None]
    t_idx = jnp.arange(A_KEY_SPAN)[None, :]
    band = jnp.abs(s_idx + A_WINDOW - t_idx) <= A_WINDOW
    sink_l = sink.astype(jnp.float32).reshape(A_KV_HEADS, A_GROUP)[None, :, :, None, None]

    def one_block(args):
        qb, start = args
        kb = lax.dynamic_slice_in_dim(kp, start, A_KEY_SPAN, axis=1)
        vb = lax.dynamic_slice_in_dim(vp, start, A_KEY_SPAN, axis=1)
        mb = lax.dynamic_slice_in_dim(kvalid, start, A_KEY_SPAN, axis=0)
        scores = jnp.einsum('bqkgd,btkd->bkgqt', qb, kb).astype(jnp.float32)
        mask = (band & mb[None, :])[None, None, None]
        scores = jnp.where(mask, scores, -1e30)
        mx = jnp.maximum(jnp.max(scores, axis=-1, keepdims=True), sink_l)
        p = jnp.exp(scores - mx)
        p = p / (jnp.sum(p, axis=-1, keepdims=True) + jnp.exp(sink_l - mx))
        out = jnp.einsum('bkgqt,btkd->bqkgd', p.astype(vb.dtype), vb)
        return out.reshape(bsz, A_BLOCK, A_Q_HEADS * A_HEAD_DIM)

    starts = jnp.arange(nb, dtype=jnp.int32) * A_BLOCK
    out = lax.map(one_block, (q_blocks, starts))
    out = out.transpose(1, 0, 2, 3).reshape(bsz, seq, A_Q_HEADS * A_HEAD_DIM)
    return (out * jax.nn.silu(gate)) @ w_out


def retention_scan(q, k, v, log_g, include_diag):
    bsz, seq, nh, dk = q.shape
    dv = v.shape[-1]
    nc = seq // B_CHUNK
    idx = jnp.arange(B_CHUNK, dtype=jnp.float32)
    diff = idx[:, None] - idx[None, :]
    causal = (diff >= 0) if include_diag else (diff > 0)
    intra = jnp.where(causal[None], jnp.exp(jnp.where(causal, diff, 0.0)[None] * log_g[:, None, None]), 0.0)
    q_decay = jnp.exp((idx + 1.0)[:, None] * log_g[None, :])[None, :, :, None]
    k_decay = jnp.exp((B_CHUNK - 1.0 - idx)[:, None] * log_g[None, :])[None, :, :, None]
    chunk_decay = jnp.exp(B_CHUNK * log_g)[None, :, None, None]

    def to_chunks(t):
        return t.reshape(bsz, nc, B_CHUNK, nh, t.shape[-1]).transpose(1, 0, 2, 3, 4)

    def step(state, inp):
        qc, kc, vc = inp
        scores = jnp.einsum('bihd,bjhd->bhij', qc, kc) * intra[None]
        inner = jnp.einsum('bhij,bjhe->bihe', scores, vc)
        cross = jnp.einsum('bihd,bhde->bihe', qc, state) * q_decay
        new_state = state * chunk_decay + jnp.einsum('bjhd,bjhe->bhde', kc * k_decay, vc)
        return new_state, inner + cross

    state0 = jnp.zeros((bsz, nh, dk, dv), jnp.float32)
    _, out = lax.scan(step, state0, (to_chunks(q), to_chunks(k), to_chunks(v)))
    return out.transpose(1, 0, 2, 3, 4).reshape(bsz, seq, nh, dv)


def retention(h, w_in, decay_logit, w_out):
    bsz, seq, _ = h.shape
    proj = h @ w_in
    nqk = B_HEADS * B_QK_DIM
    q = proj[..., :nqk].reshape(bsz, seq, B_HEADS, B_QK_DIM)
    k = proj[..., nqk:2 * nqk].reshape(bsz, seq, B_HEADS, B_QK_DIM)
    v = proj[..., 2 * nqk:2 * nqk + BRANCH_WIDTH].reshape(bsz, seq, B_HEADS, B_V_DIM)
    gate = proj[..., 2 * nqk + BRANCH_WIDTH:]
    q = apply_rope(q, B_QK_DIM, B_ROPE_THETA).astype(jnp.float32)
    k = apply_rope(k, B_QK_DIM, B_ROPE_THETA).astype(jnp.float32) * (B_QK_DIM ** -0.5)
    v = v.astype(jnp.float32)
    log_g = jax.nn.log_sigmoid(decay_logit.astype(jnp.float32))
    fwd = retention_scan(q, k, v, log_g[0], True)
    bwd = jnp.flip(retention_scan(jnp.flip(q, 1), jnp.flip(k, 1), jnp.flip(v, 1), log_g[1], False), 1)
    o = fwd + bwd
    o = o * lax.rsqrt(jnp.mean(o * o, axis=-1, keepdims=True) + NORM_EPS)
    o = o.reshape(bsz, seq, BRANCH_WIDTH).astype(h.dtype)
    return (o * jax.nn.silu(gate)) @ w_out


def fourier_mix(h, w_in, w_out):
    bsz, seq, _ = h.shape
    proj = h @ w_in
    u = proj[..., :BRANCH_WIDTH].reshape(bsz, seq, C_GROUPS, C_GROUP_DIM).astype(jnp.float32)
    gate = proj[..., BRANCH_WIDTH:]
    mixed = jnp.real(jnp.fft.fftn(u, axes=(1, 3), norm='ortho'))
    mixed = mixed.reshape(bsz, seq, BRANCH_WIDTH).astype(h.dtype)
    return (mixed * jax.nn.silu(gate)) @ w_out


def trunk(x, norm_g, final_norm_g, a_w_in, a_sink, a_w_out, b_w_in, b_decay, b_w_out, c_w_in, c_w_out):
    counts = [0, 0, 0]
    for layer in range(DEPTH):
        kind = layer % N_MIXERS
        j = counts[kind]
        counts[kind] += 1
        h = rms_norm(x, norm_g[layer])
        if kind == 0:
            d = window_attention(h, a_w_in[j], a_sink[j], a_w_out[j])
        elif kind == 1:
            d = retention(h, b_w_in[j], b_decay[j], b_w_out[j])
        else:
            d = fourier_mix(h, c_w_in[j], c_w_out[j])
        x = x + d.astype(x.dtype)
    return rms_norm(x, final_norm_g)


def setup_inputs(seed: int = 0) -> dict:
    key = jax.random.key(seed)
    ks = jax.random.split(key, 16)
    n_a = sum(1 for l in range(DEPTH) if l % N_MIXERS == 0)
    n_b = sum(1 for l in range(DEPTH) if l % N_MIXERS == 1)
    n_c = sum(1 for l in range(DEPTH) if l % N_MIXERS == 2)
    f32 = jnp.float32
    x_prompt = jax.random.normal(ks[0], (BATCH, SEQ, D_MODEL), f32)
    x_sample = jax.random.normal(ks[1], (DEC_BATCH, DEC_SEQ, D_MODEL), f32)
    norm_g = 1.0 + 0.02 * jax.random.normal(ks[2], (DEPTH, D_MODEL), f32)
    final_norm_g = 1.0 + 0.02 * jax.random.normal(ks[3], (D_MODEL,), f32)
    a_w_in = jax.random.normal(ks[4], (n_a, D_MODEL, A_IN), f32) * D_MODEL ** -0.5
    a_sink = jax.random.normal(ks[5], (n_a, A_Q_HEADS), f32)
    a_w_out = jax.random.normal(ks[6], (n_a, A_Q_HEADS * A_HEAD_DIM, D_MODEL), f32) * (A_Q_HEADS * A_HEAD_DIM) ** -0.5
    b_w_in = jax.random.normal(ks[7], (n_b, D_MODEL, B_IN), f32) * D_MODEL ** -0.5
    lin = jnp.linspace(math.log(B_DECAY_HI), math.log(B_DECAY_LO), B_HEADS, dtype=f32)
    base_logit = jnp.log1p(-jnp.exp(lin)) - lin
    b_decay = base_logit[None, None, :] + 0.01 * jax.random.normal(ks[8], (n_b, 2, B_HEADS), f32)
    b_w_out = jax.random.normal(ks[9], (n_b, BRANCH_WIDTH, D_MODEL), f32) * BRANCH_WIDTH ** -0.5
    c_w_in = jax.random.normal(ks[10], (n_c, D_MODEL, C_IN), f32) * D_MODEL ** -0.5
    c_w_out = jax.random.normal(ks[11], (n_c, BRANCH_WIDTH, D_MODEL), f32) * BRANCH_WIDTH ** -0.5
    return {"x_prompt": x_prompt, "x_sample": x_sample, "norm_g": norm_g, "final_norm_g": final_norm_g,
            "a_w_in": a_w_in, "a_sink": a_sink, "a_w_out": a_w_out,
            "b_w_in": b_w_in, "b_decay": b_decay, "b_w_out": b_w_out,
            "c_w_in": c_w_in, "c_w_out": c_w_out}


def reference(x_prompt, x_sample, norm_g, final_norm_g, a_w_in, a_sink, a_w_out,
              b_w_in, b_decay, b_w_out, c_w_in, c_w_out):
    y_prompt = trunk(x_prompt, norm_g, final_norm_g, a_w_in, a_sink, a_w_out,
                     b_w_in, b_decay, b_w_out, c_w_in, c_w_out)
    y_sample = trunk(x_sample, norm_g, final_norm_g, a_w_in, a_sink, a_w_out,
                     b_w_in, b_decay, b_w_out, c_w_in, c_w_out)
    return (y_prompt, y_sample)
```

```python
import math
from contextlib import ExitStack

import numpy as np
import ml_dtypes

import concourse.bass as bass
import concourse.mybir as mybir
from concourse.bass_utils import run_bass_kernel_spmd

F32 = mybir.dt.float32
BF16 = mybir.dt.bfloat16
AF = mybir.ActivationFunctionType
ALU = mybir.AluOpType
NPBF = ml_dtypes.bfloat16

D = 1024
E = 2048
EPS = 1e-6
A_IN = 4608
B_IN = 6144
C_IN = 4096
LAYERS = ("A", "B", "C", "A")


class Em:
    ENGS = ("pe", "act", "dve", "pool", "sp")

    def __init__(self, nc, stack):
        self.nc = nc
        self.eng = {"pe": nc.tensor, "act": nc.scalar, "dve": nc.vector, "pool": nc.gpsimd, "sp": nc.sync}
        self.stack = stack
        self.sem = {e: stack.enter_context(nc.semaphore("s_" + e)) for e in self.ENGS}
        self.semval = {e: 0 for e in self.ENGS}
        self.dsem = {}
        self.dval = {}
        self.ops = []
        self.n_instr = 0

    def op(self, e, fn, reads=(), writes=()):
        self.ops.append(("c", e, fn, tuple(reads), tuple(writes), None))

    def dma(self, out, in_, reads=(), writes=(), semkey=None, q="sp", **kw):
        assert semkey is not None
        self.ops.append(("d", q, (out, in_, kw), tuple(reads), tuple(writes), semkey))

    @staticmethod
    def _hoist(ops):
        pos = [0.0] * len(ops)
        last_touch = {}
        last_sem = {}
        nh = 0
        for i, o in enumerate(ops):
            kind, e, fn, reads, writes, semkey = o
            p = float(i)
            if kind == "d" and not reads and writes and not str(writes[0][0]).startswith("D_"):
                q = max([last_touch.get(k, -1.0) for k in writes] + [last_sem.get(semkey, -1.0)])
                nh += 1
                p = min(p, q + 1e-7 * nh)
            pos[i] = p
            for k in reads:
                last_touch[k] = max(last_touch.get(k, -1.0), p)
            for k in writes:
                last_touch[k] = max(last_touch.get(k, -1.0), p)
            if kind == "d":
                last_sem[semkey] = max(last_sem.get(semkey, -1.0), p)
        order = sorted(range(len(ops)), key=lambda i: (pos[i], i))
        return [ops[i] for i in order]

    def flush(self):
        ops = self._hoist(self.ops)
        self.ops = []
        n = len(ops)
        last_w = {}
        readers = {}
        eidx = {e: 0 for e in self.ENGS}
        op_eidx = [0] * n
        known = {e: {} for e in self.ENGS}
        waits = [None] * n
        needed = [False] * n
        for i, o in enumerate(ops):
            kind, e, fn, reads, writes, semkey = o
            eidx[e] += 1
            op_eidx[i] = eidx[e]
            deps = set()
            for k in reads:
                j = last_w.get(k)
                if j is not None:
                    deps.add(j)
            for k in writes:
                j = last_w.get(k)
                if j is not None:
                    deps.add(j)
                for r in readers.get(k, ()):
                    deps.add(r)
            deps.discard(i)
            w = []
            kn = known[e]
            for j in sorted(deps):
                oj = ops[j]
                if oj[0] == "c":
                    ej = oj[1]
                    if ej == e and e == "pe":
                        continue
                    kk = ej
                    if kn.get(kk, -1) >= op_eidx[j]:
                        continue
                    kn[kk] = op_eidx[j]
                else:
                    kk = ("d", oj[5])
                    if kn.get(kk, -1) >= j:
                        continue
                    kn[kk] = j
                w.append(j)
                needed[j] = True
            waits[i] = w
            for k in writes:
                last_w[k] = i
                readers[k] = []
            for k in reads:
                readers.setdefault(k, []).append(i)
        last_of = {}
        for i, o in enumerate(ops):
            if o[0] == "c":
                last_of[o[1]] = i
            else:
                needed[i] = True
        for i in last_of.values():
            needed[i] = True
        val = [None] * n
        for i, o in enumerate(ops):
            if not needed[i]:
                continue
            if o[0] == "c":
                self.semval[o[1]] += 1
                val[i] = (self.sem[o[1]], self.semval[o[1]])
            else:
                sk = o[5]
                if sk not in self.dsem:
                    self.dsem[sk] = self.stack.enter_context(self.nc.semaphore("d%d" % len(self.dsem)))
                    self.dval[sk] = 0
                self.dval[sk] += 16
                val[i] = (self.dsem[sk], self.dval[sk])
        for i, o in enumerate(ops):
            kind, e, fn, reads, writes, semkey = o
            eng = self.eng[e]
            for j in waits[i]:
                s, v = val[j]
                eng.wait_ge(s, v)
            if kind == "c":
                ins = fn()
                if needed[i]:
                    ins.then_inc(val[i][0], 1)
            else:
                out, in_, kw = fn
                eng.dma_start(out=out, in_=in_, **kw).then_inc(val[i][0], 16)
            self.n_instr += 1 + len(waits[i])
        self.barrier()

    def barrier(self):
        sp = self.eng["sp"]
        for sk, s in self.dsem.items():
            if self.dval[sk] > 0:
                sp.wait_ge(s, self.dval[sk])
        for e in ("pe", "act", "dve", "pool"):
            if self.semval[e] > 0:
                sp.wait_ge(self.sem[e], self.semval[e])
        self.semval["sp"] += 1
        sp.nop().then_inc(self.sem["sp"], 1)
        for e in ("pe", "act", "dve", "pool"):
            self.eng[e].wait_ge(self.sem["sp"], self.semval["sp"])


def const_tables(NT, TS, fa, fb, TSA, TSB):
    T = NT * 128
    pos = (np.arange(T) % (TS * 128)).astype(np.float32)
    t = {}
    t["ident"] = np.eye(128, dtype=np.float32).astype(NPBF)
    invA = np.exp(-(np.arange(8, dtype=np.float32) * (2.0 / 16)) * np.float32(math.log(500000.0))).astype(np.float32)
    angA = (pos[:, None] * invA[None, :]).astype(np.float32)
    t["ropeA"] = np.concatenate([np.cos(angA), np.sin(angA)], 1).astype(np.float32)
    invB = np.exp(-(np.arange(128, dtype=np.float32) * (2.0 / 256)) * np.float32(math.log(10000.0))).astype(np.float32)
    angB = (pos[:, None] * invB[None, :]).astype(np.float32)
    t["ropeB"] = np.concatenate([np.cos(angB), np.sin(angB)], 1).astype(np.float32)
    tiles = np.arange(NT)
    fl = np.zeros((2, NT), np.float32)
    fl[0] = (tiles % TS != 0)
    fl[1] = (tiles % TS != TS - 1)
    t["flagsA"] = np.broadcast_to(fl[None], (128, 2, NT)).astype(np.float32).copy()
    kk = np.arange(128)
    mk = np.zeros((128, 2, 128), np.float32)
    mk[:, 0, :] = (kk[:, None] >= kk[None, :])
    mk[:, 1, :] = (kk[:, None] <= kk[None, :])
    t["maskLR"] = mk.astype(NPBF)
    jj = kk[:, None].astype(np.float32)
    ii = kk[None, :].astype(np.float32)
    rt = np.zeros((128, 4, 128), np.float32)
    rt[:, 0, :] = np.where(jj <= ii, ii - jj, 0.0)
    rt[:, 1, :] = np.where(jj > ii, jj - ii, 0.0)
    rt[:, 2, :] = ii + 1.0
    rt[:, 3, :] = 128.0 - ii
    t["retT"] = rt
    rp = np.zeros((128, 2), np.float32)
    rp[:, 0] = 127.0 - kk
    rp[:, 1] = kk
    t["retP"] = rp
    fb_ = np.zeros((2, NT), np.float32)
    fb_[0] = ((tiles + 1) % TS != 0)
    fb_[1] = (tiles % TS != 0)
    t["flagsB"] = np.broadcast_to(fb_[None], (128, 2, NT)).astype(np.float32).copy()
    c = np.arange(512)
    ang = 2 * np.pi * np.outer(c, c) / 512.0
    fc = np.stack([np.cos(ang), -np.sin(ang)], 1) / math.sqrt(512.0)
    t["FC"] = fc.reshape(4, 128, 2, 512).transpose(1, 0, 2, 3).astype(NPBF).copy()
    NS = NT // TS
    n1 = np.arange(NT)
    s1, m1 = n1 // TS, n1 % TS
    q = np.arange(NT)
    sq, k1 = q // TS, q % TS
    n2 = np.arange(128)
    ph = (m1[:, None, None] * k1[None, None, :] / TS) + (n2[None, :, None] * k1[None, None, :] / (128.0 * TS))
    Gc = np.exp(-2j * np.pi * ph) * (s1[:, None, None] == sq[None, None, :])
    G = np.stack([Gc.real, Gc.imag, -Gc.imag], 2)
    t["G"] = G.astype(NPBF)
    R = 128 // TS
    k2 = np.arange(128)
    sig = TS * (k2 % R) + k2 // R
    Hc = np.exp(-2j * np.pi * np.outer(n2, k2) / 128.0) / math.sqrt(128.0 * TS)
    H = np.zeros((128, 2, 128), np.float64)
    H[:, 0, sig] = Hc.real
    H[:, 1, sig] = -Hc.imag
    t["H"] = H.astype(NPBF)
    pi = np.arange(128)
    P = np.zeros((128, 2, 128), np.float32)
    for v, (f, ts) in enumerate(((fa, TSA), (fb, TSB))):
        r = 128 // ts
        w = ts * (pi % r) + pi // r
        P[pi, v, w] = f
    t["PAB"] = P.astype(NPBF)
    return t


TABLE_SPECS = lambda NT: {
    "ident": ([128, 128], BF16), "ropeA": ([NT * 128, 16], F32), "ropeB": ([NT * 128, 256], F32),
    "flagsA": ([128, 2, NT], F32), "maskLR": ([128, 2, 128], BF16), "retT": ([128, 4, 128], F32),
    "retP": ([128, 2], F32), "flagsB": ([128, 2, NT], F32), "FC": ([128, 4, 2, 512], BF16),
    "G": ([NT, 128, 3, NT], BF16), "H": ([128, 2, 128], BF16), "PAB": ([128, 2, 128], BF16),
}


def build(NT, TSA, TSB, layers=LAYERS, final_norm=True, debug=False, bstop=0):
    T = NT * 128
    nc = bass.Bass("TRN2", target_bir_lowering=False)
    din = lambda name, shape, dt=F32: nc.dram_tensor(name, shape, dt, kind="ExternalInput").ap()
    x_in = din("x", [T, D])
    norm_g = din("norm_g", [4, D])
    fin_g = din("final_norm_g", [1, D])
    a_w_in = din("a_w_in", [2, D, A_IN])
    a_sink = din("a_sink", [2, 32])
    a_w_out = din("a_w_out", [2, E, D])
    b_w_in = din("b_w_in", [1, D, B_IN])
    b_decay = din("b_decay", [1, 8])
    b_w_out = din("b_w_out", [1, E, D])
    c_w_in = din("c_w_in", [1, D, C_IN])
    c_w_out = din("c_w_out", [1, E, D])
    tb = {k: din(k, sh, dt) for k, (sh, dt) in TABLE_SPECS(NT).items()}
    y_out = nc.dram_tensor("y", [T, D], F32, kind="ExternalOutput").ap()
    dscr = lambda name, shape, dt: nc.dram_tensor(name, shape, dt, kind="Internal").ap()
    xscr = dscr("xscr", [T, D], F32)
    KT = dscr("KT", [NT, 128, 1024], BF16)
    VA = dscr("VA", [NT, 128, 1024], BF16)
    SG = dscr("SG", [NT, 128, 2048], BF16)
    KB = dscr("KBs", [NT, 128, 1024], BF16)
    VB = dscr("VBs", [NT, 128, 2048], BF16)
    Sscr = dscr("Sscr", [NT, 128, 4096], BF16)
    Zscr = dscr("Zscr", [T, 4096], BF16)
    Bscr = dscr("Bscr", [NT, 128, 4096], BF16)
    Mscr = dscr("Mscr", [T, E], BF16)

    with ExitStack() as top:
        em = Em(nc, top)
        V, S, G, P = nc.vector, nc.scalar, nc.gpsimd, nc.tensor

        uid = [0]

        def sbuf(st, name, shape, dt):
            uid[0] += 1
            return st.enter_context(nc.sbuf_tensor("sb%d_%s" % (uid[0], name), shape, dt))

        def psum(st, name, shape, dt):
            uid[0] += 1
            return st.enter_context(nc.psum_tensor("ps%d_%s" % (uid[0], name), shape, dt))

        ident = sbuf(top, "ident", [128, 128], BF16)
        gl = sbuf(top, "gl", [128, D], F32)
        gfin_box = [None]
        em.dma(ident[:], tb["ident"], writes=[("ident",)], semkey="c0")
        em.flush()

        def load_weights(st, Wd, ncols, name):
            K = Wd.shape[0]
            Wb = sbuf(st, name, [128, K // 128, ncols], BF16)
            with ExitStack() as st2:
                stg = [sbuf(st2, "stg%d" % i, [128, ncols], F32) for i in range(2)]
                h = ncols // 2
                for kc in range(K // 128):
                    s = kc % 2
                    em.dma(stg[s][:], Wd[kc * 128:(kc + 1) * 128, :], writes=[("stg", s)], semkey=("stg", s))
                    em.op("act", lambda kc=kc, s=s: S.copy(out=Wb[:, kc, 0:h], in_=stg[s][:, 0:h]),
                          reads=[("stg", s)], writes=[(name, kc, 0)])
                    em.op("dve", lambda kc=kc, s=s: V.tensor_copy(out=Wb[:, kc, h:ncols], in_=stg[s][:, h:ncols]),
                          reads=[("stg", s)], writes=[(name, kc, 1)])
                em.flush()
            return Wb

        def load_gain(li, st=None, last=False):
            em.dma(gl[:], norm_g[li:li + 1, :].partition_broadcast(128), writes=[("gl",)], semkey="gl")
            if last:
                gfin_box[0] = sbuf(st, "gfin", [128, D], F32)
                em.dma(gfin_box[0][:], fin_g.partition_broadcast(128), writes=[("gfin",)], semkey="c2")

        class Work:
            pass

        def alloc_common(st, w, nslot=2):
            w.nslot = nslot
            w.xt = [sbuf(st, "xt%d" % i, [128, D], F32) for i in range(nslot)]
            w.hb = sbuf(st, "hb", [128, D], BF16)
            w.junk = w.hb
            w.st = [sbuf(st, "stat%d" % i, [128, 4], F32) for i in range(nslot)]
            w.hT = [sbuf(st, "hT%d" % i, [128, 8, 128], BF16) for i in range(min(nslot, 2))]
            w.tp = psum(st, "tp", [128, 16, 128], BF16)

        def load_x(w, t, src):
            s = t % w.nslot
            em.dma(w.xt[s][:], src[t * 128:(t + 1) * 128, :], writes=[("xt", s)], semkey=("xt", s))

        def prologue(w, t, li, src, hdst=None, hkey=None, preloaded=False):
            s = t % w.nslot
            xt, stt = w.xt[s], w.st[s]
            sh = s % len(w.hT)
            hT = w.hT[sh][:] if hdst is None else hdst
            hkey = ("hT", sh) if hkey is None else hkey
            if not preloaded:
                em.dma(xt[:], src[t * 128:(t + 1) * 128, :], writes=[("xt", s)], semkey=("xt", s))
            em.op("act", lambda: S.activation(out=w.junk[:], in_=xt[:], func=AF.Square, accum_out=stt[:, 0:1]),
                  reads=[("xt", s)], writes=[("hb",), ("st", s)])
            em.op("dve", lambda: V.tensor_scalar(out=stt[:, 1:2], in0=stt[:, 0:1], scalar1=1.0 / D, scalar2=EPS,
                                                 op0=ALU.mult, op1=ALU.add), reads=[("st", s)], writes=[("st", s)])
            em.op("act", lambda: S.activation(out=stt[:, 2:3], in_=stt[:, 1:2], func=AF.Ln),
                  reads=[("st", s)], writes=[("st", s)])
            em.op("act", lambda: S.activation(out=stt[:, 3:4], in_=stt[:, 2:3], func=AF.Exp, scale=-0.5),
                  reads=[("st", s)], writes=[("st", s)])
            em.op("dve", lambda: V.scalar_tensor_tensor(out=w.hb[:], in0=xt[:], scalar=stt[:, 3:4], in1=gl[:],
                                                        op0=ALU.mult, op1=ALU.mult),
                  reads=[("xt", s), ("st", s)], writes=[("hb",)])
            split_at[0] = len(em.ops)
            for c in range(8):
                em.op("pe", lambda c=c: P.transpose(out=w.tp[:, c, :], in_=w.hb[:, c * 128:(c + 1) * 128], identity=ident[:]),
                      reads=[("hb",)], writes=[("tp",)])
            em.op("dve", lambda: V.tensor_copy(out=hT[:, 0:4, :], in_=w.tp[:, 0:4, :]), reads=[("tp",)], writes=[hkey])
            em.op("dve", lambda: V.tensor_copy(out=hT[:, 4:8, :], in_=w.tp[:, 4:8, :]), reads=[("tp",)], writes=[hkey])
            return s

        def epilogue(w, t, s, gatedT, gkey, Wo, mm, dst, last):
            xt, stt = w.xt[s], w.st[s]
            def gk(m):
                if gkey == "per_group":
                    return [("gatedT", m // 4)]
                return list(gkey) if isinstance(gkey, list) else [gkey]
            for (m0, m1) in ((0, 12), (12, 16)):
                for dh in range(2):
                    bank, bk = mm[dh]
                    for m in range(m0, m1):
                        em.op("pe", lambda m=m, dh=dh, bank=bank: P.matmul(bank[:], lhsT=gatedT[:, m, :],
                                                                          rhs=Wo[:, m, dh * 512:(dh + 1) * 512],
                                                                          start=(m == 0), stop=(m == 15)),
                              reads=gk(m), writes=[bk])
            for dh in range(2):
                bank, bk = mm[dh]
                em.op("dve", lambda dh=dh, bank=bank: V.tensor_tensor(out=xt[:, dh * 512:(dh + 1) * 512], in0=bank[:],
                                                                     in1=xt[:, dh * 512:(dh + 1) * 512], op=ALU.add),
                      reads=[bk, ("xt", s)], writes=[("xt", s)])
            if last:
                em.op("act", lambda: S.activation(out=w.junk[:], in_=xt[:], func=AF.Square, accum_out=stt[:, 0:1]),
                      reads=[("xt", s)], writes=[("hb",) if w.junk is w.hb else ("junk",), ("st", s)])
                em.op("dve", lambda: V.tensor_scalar(out=stt[:, 1:2], in0=stt[:, 0:1], scalar1=1.0 / D, scalar2=EPS,
                                                     op0=ALU.mult, op1=ALU.add), reads=[("st", s)], writes=[("st", s)])
                em.op("act", lambda: S.activation(out=stt[:, 2:3], in_=stt[:, 1:2], func=AF.Ln),
                      reads=[("st", s)], writes=[("st", s)])
                em.op("act", lambda: S.activation(out=stt[:, 3:4], in_=stt[:, 2:3], func=AF.Exp, scale=-0.5),
                      reads=[("st", s)], writes=[("st", s)])
                em.op("dve", lambda: V.scalar_tensor_tensor(out=xt[:], in0=xt[:], scalar=stt[:, 3:4], in1=gfin_box[0][:],
                                                            op0=ALU.mult, op1=ALU.mult),
                      reads=[("xt", s), ("st", s)], writes=[("xt", s)])
            em.dma(dst[t * 128:(t + 1) * 128, :], xt[:], reads=[("xt", s)], writes=[("D_x", t)], semkey=("xst", s))

        def dump(name, ap, key):
            if not debug:
                return
            d = nc.dram_tensor("dbg_" + name, list(ap.shape), ap.dtype, kind="ExternalOutput").ap()
            em.dma(d, ap, reads=[key], writes=[("D_dbg", name)], semkey=("dbg", name))

        split_at = [0]

        def capture(fn, *args):
            saved = em.ops
            em.ops = []
            fn(*args)
            out = em.ops
            em.ops = saved
            return out

        def merge(a, b):
            out, i, j = [], 0, 0
            na, nb = len(a), len(b)
            while i < na or j < nb:
                if j >= nb or (i < na and i * nb <= j * na):
                    out.append(a[i]); i += 1
                else:
                    out.append(b[j]); j += 1
            return out

        def pipeline(stage1, stage2, n, prefetch=None):
            if prefetch is not None:
                prefetch(0)
                if n > 1:
                    prefetch(1)
            prev = capture(stage1, 0)
            em.ops.extend(prev)
            for t in range(n):
                if prefetch is not None and t + 2 < n:
                    prefetch(t + 2)
                s2 = capture(stage2, t)
                if t + 1 < n:
                    s1 = capture(stage1, t + 1)
                    k = split_at[0]
                    em.ops.extend(s1[:k])
                    em.ops.extend(merge(s1[k:], s2))
                else:
                    em.ops.extend(s2)

        def bc_mid(ap, n):
            return ap.unsqueeze(1).broadcast_to([ap.shape[0], n, ap.shape[1]])

        def layer_A(li, j, src, dst, last):
            with ExitStack() as st:
                load_gain(li, st, last)
                Wi = load_weights(st, a_w_in[j], A_IN, "Wi")
                Wo = load_weights(st, a_w_out[j], D, "Wo")
                with ExitStack() as s1:
                    w = Work()
                    alloc_common(s1, w)
                    kvfs = [sbuf(s1, "kvf%d" % i, [128, 512], F32) for i in range(2)]
                    cs = [sbuf(s1, "csA%d" % i, [128, 16], F32) for i in range(2)]
                    tmp = sbuf(s1, "ropetmp", [128, 4, 4, 8], F32)
                    ktz = [sbuf(s1, "ktz%d" % i, [128, 8, 128], BF16) for i in range(2)]
                    vau = [sbuf(s1, "vau%d" % i, [128, 8, 128], BF16) for i in range(2)]
                    kTs = [sbuf(s1, "kTs%d" % i, [128, 8, 128], BF16) for i in range(2)]
                    mmb = psum(s1, "mmA1", [128, 512], F32)
                    mmg = [(psum(s1, "mmA1g%d" % i, [128, 512], F32), ("mmg", i)) for i in range(2)]
                    sgs = [sbuf(s1, "sgs%d" % i, [128, 16, 512], BF16) for i in range(2)]
                    hT4 = [sbuf(s1, "hT4_%d" % i, [128, 8, 512], BF16) for i in range(2)]
                    for i in range(2):
                        em.op("pool", lambda i=i: G.memset(ktz[i][:], 0.0), writes=[("ktz", i)])
                        em.op("pool", lambda i=i: G.memset(vau[i][:], 1.0), writes=[("vau", i)])
                    def stage1(b):
                        sb, tl = b // 4, b % 4
                        ss = sb % 2
                        hTv, hk = hT4[ss][:, :, tl * 128:(tl + 1) * 128], ("hT4", ss, tl)
                        s = prologue(w, b, li, src, hTv, hk)
                        em.dma(cs[s][:], tb["ropeA"][b * 128:(b + 1) * 128, :], writes=[("cs", s)], semkey=("cs", s))
                        for kc in range(8):
                            em.op("pe", lambda kc=kc, s=s: P.matmul(mmb[:], lhsT=hTv[:, kc, :], rhs=Wi[:, kc, 2048:2560],
                                                                   start=(kc == 0), stop=(kc == 7)),
                                  reads=[hk], writes=[("mmb",)])
                        em.op("act", lambda: S.copy(out=kvfs[s][:], in_=mmb[:]), reads=[("mmb",)], writes=[("kvf", s)])

                    def stage2(b):
                        s = b % 2
                        kvf = kvfs[s]
                        kv4 = kvf[:, 0:256].rearrange("p (g d) -> p g d", g=4)
                        x1, x2 = kv4[:, :, 0:8], kv4[:, :, 8:16]
                        cosb, sinb = bc_mid(cs[s][:, 0:8], 4), bc_mid(cs[s][:, 8:16], 4)
                        for ti, (xa, cb_) in enumerate(((x1, cosb), (x2, sinb), (x2, cosb), (x1, sinb))):
                            em.op("pool", lambda ti=ti, xa=xa, cb_=cb_: G.tensor_tensor(out=tmp[:, ti], in0=xa, in1=cb_, op=ALU.mult),
                                  reads=[("kvf", s), ("cs", s)], writes=[("tmp", ti)])
                        kz = ktz[s][:].rearrange("p (g r) c -> p g r c", r=2)
                        va = vau[s][:].rearrange("p (g r) c -> p g r c", r=2)
                        for par in range(2):
                            o = par * 64
                            em.op("pool", lambda par=par, o=o, kz=kz: G.tensor_tensor(out=kz[:, :, par, o:o + 8], in0=tmp[:, 0], in1=tmp[:, 1],
                                                                          op=ALU.subtract),
                                  reads=[("tmp", 0), ("tmp", 1)], writes=[("ktz", s)])
                            em.op("pool", lambda par=par, o=o, kz=kz: G.tensor_tensor(out=kz[:, :, par, o + 8:o + 16], in0=tmp[:, 2], in1=tmp[:, 3],
                                                                          op=ALU.add),
                                  reads=[("tmp", 2), ("tmp", 3)], writes=[("ktz", s)])
                            em.op("dve", lambda par=par, o=o, kz=kz: V.tensor_copy(out=kz[:, :, par, o + 16:o + 64], in_=kv4[:, :, 16:64]),
                                  reads=[("kvf", s)], writes=[("ktz", s)])
                            em.op("dve", lambda par=par, o=o, va=va: V.tensor_copy(
                                out=va[:, :, par, o:o + 64], in_=kvf[:, 256:512].rearrange("p (g d) -> p g d", g=4)),
                                  reads=[("kvf", s)], writes=[("vau", s)])
                        for c in range(8):
                            em.op("pe", lambda c=c, s=s: P.transpose(out=w.tp[:, 8 + c, :], in_=ktz[s][:, c, :], identity=ident[:]),
                                  reads=[("ktz", s)], writes=[("tp2",)])
                        em.op("act", lambda s=s: S.copy(out=kTs[s][:], in_=w.tp[:, 8:16, :]), reads=[("tp2",)], writes=[("kTs", s)])
                        em.dma(KT[b].rearrange("p (c t) -> p c t", c=8), kTs[s][:], reads=[("kTs", s)], writes=[("D_KT", b)],
                               semkey=("kTst", s))
                        em.dma(VA[b].rearrange("p (c t) -> p c t", c=8), vau[s][:], reads=[("vau", s)], writes=[("D_VA", b)],
                               semkey=("vast", s))
                        if b % 4 == 3:
                            gate4(b // 4, (b // 4) % 2)

                    def gate4(sb, ss):
                        hks = [("hT4", ss, tl) for tl in range(4)]
                        for m in range(16):
                            bank, bk = mmg[m % 2]
                            for kc in range(8):
                                em.op("pe", lambda kc=kc, m=m, bank=bank: P.matmul(
                                    bank[:], lhsT=Wi[:, kc, 2560 + m * 128:2560 + (m + 1) * 128], rhs=hT4[ss][:, kc, :],
                                    start=(kc == 0), stop=(kc == 7)), reads=hks, writes=[bk])
                            em.op("act", lambda m=m, bank=bank: S.activation(out=sgs[ss][:, m, :], in_=bank[:], func=AF.Silu),
                                  reads=[bk], writes=[("sgs", ss)])
                        for tl in range(4):
                            em.dma(SG[4 * sb + tl].rearrange("p (c t) -> p c t", c=16), sgs[ss][:, :, tl * 128:(tl + 1) * 128],
                                   reads=[("sgs", ss)], writes=[("D_SG", sb, tl)], semkey=("sgst", ss))

                    pipeline(stage1, stage2, NT)
                    em.flush()
                with ExitStack() as s2:
                    w = Work()
                    alloc_common(s2, w, 3)
                    if last:
                        w.junk = sbuf(s2, "junkA", [128, D], BF16)
                    sk = sbuf(s2, "sinkbc", [128, 32], F32)
                    sinksm = sbuf(s2, "sinksm", [128, 8, 4], F32)
                    flags = sbuf(s2, "flagsA", [128, 2, NT], F32)
                    mask = sbuf(s2, "maskLR", [128, 2, 128], BF16)
                    maskb = [sbuf(s2, "maskb%d" % i, [128, 2, 128], BF16) for i in range(2)]
                    cs = [sbuf(s2, "csA%d" % i, [128, 16], F32) for i in range(2)]
                    kTl = [[sbuf(s2, "kTl%d_%d" % (i, jj), [128, 8, 128], BF16) for jj in range(3)] for i in range(2)]
                    val = [[sbuf(s2, "val%d_%d" % (i, jj), [128, 8, 128], BF16) for jj in range(3)] for i in range(2)]
                    qf = sbuf(s2, "qf", [128, 32, 16], F32)
                    tmp = sbuf(s2, "ropetmpq", [128, 4, 32, 8], F32)
                    qb = sbuf(s2, "qb", [128, E], BF16)
                    qT = [sbuf(s2, "qT%d" % i, [128, 16, 128], BF16) for i in range(2)]
                    sgT = [sbuf(s2, "sgT%d" % i, [128, 16, 128], BF16) for i in range(2)]
                    pex = [sbuf(s2, "pex%d" % i, [128, 4, 128], BF16) for i in range(12)]
                    rec = [sbuf(s2, "rec%d" % i, [128, 512], F32) for i in range(2)]
                    onrm = rec
                    gatedT = sbuf(s2, "gatedT", [128, 16, 128], BF16)
                    prow = sbuf(s2, "prow", [128, 8, 4], BF16)
                    sel = sbuf(s2, "sel", [128, 2, 128], BF16)
                    mm = [(psum(s2, "mmA%d" % i, [128, 512], F32), ("mm", i)) for i in range(6)]
                    em.dma(sk[:], a_sink[j:j + 1, :].partition_broadcast(128), writes=[("sk",)], semkey="sk")
                    em.dma(flags[:], tb["flagsA"], writes=[("flags",)], semkey="fl")
                    em.dma(mask[:], tb["maskLR"], writes=[("mask",)], semkey="mk")
                    em.op("act", lambda: S.activation(out=sk[:], in_=sk[:], func=AF.Exp), reads=[("sk",)], writes=[("sk",)])
                    em.op("pool", lambda: G.tensor_copy(out=sinksm[:].rearrange("p (g r) c -> p g r c", r=2),
                                                        in_=sk[:].rearrange("p (g c r) -> p g r c", g=4, c=4, r=2)),
                          reads=[("sk",)], writes=[("sinksm",)])
                    em.op("pool", lambda: G.memset(prow[:], 0.0), writes=[("prow",)])
                    em.op("pool", lambda: G.memset(sel[:], 0.0), writes=[("sel",)])
                    em.op("pool", lambda: G.memset(sel[:, 0, 64:128], 1.0), reads=[("sel",)], writes=[("sel",)])
                    em.op("pool", lambda: G.memset(sel[:, 1, 0:64], 1.0), reads=[("sel",)], writes=[("sel",)])
                    for gp in range(8):
                        em.op("pool", lambda gp=gp: G.tensor_copy(out=prow[0:1, gp, :], in_=sinksm[0:1, gp, :]),
                              reads=[("prow",), ("sinksm",)], writes=[("prow",)])
                    pexi = [0]

                    def stage1(b):
                        sx = prologue(w, b, li, src, preloaded=True)
                        s = b % 2
                        em.dma(cs[s][:], tb["ropeA"][b * 128:(b + 1) * 128, :], writes=[("cs", s)], semkey=("cs", s))
                        js = [jj for jj in range(3) if 0 <= b - 1 + jj < NT]
                        for side in range(2):
                            em.op("pool", lambda side=side: G.tensor_scalar(out=maskb[s][:, side, :], in0=mask[:, side, :],
                                                                            scalar1=flags[:, side, b:b + 1], scalar2=None, op0=ALU.mult),
                                  reads=[("mask",), ("flags",)], writes=[("maskb", s, side)])
                        for jj in js:
                            bb = b - 1 + jj
                            em.dma(kTl[s][jj][:], KT[bb].rearrange("p (c t) -> p c t", c=8), writes=[("kTl", s, jj)],
                                   semkey=("kTl", s, jj))
                            em.dma(val[s][jj][:], VA[bb].rearrange("p (c t) -> p c t", c=8), writes=[("val", s, jj)],
                                   semkey=("val", s, jj))
                        for cb in range(4):
                            bank, bk = mm[0]
                            for kc in range(8):
                                em.op("pe", lambda kc=kc, cb=cb, bank=bank: P.matmul(bank[:], lhsT=w.hT[sx % 2][:, kc, :],
                                                                                   rhs=Wi[:, kc, cb * 512:(cb + 1) * 512],
                                                                                   start=(kc == 0), stop=(kc == 7)),
                                      reads=[("hT", sx % 2)], writes=[bk])
                            em.op("dve", lambda cb=cb, bank=bank: V.tensor_scalar(out=qb[:, cb * 512:(cb + 1) * 512], in0=bank[:],
                                                                                 scalar1=0.125, scalar2=None, op0=ALU.mult),
                                  reads=[bk], writes=[("qb", cb)])
                            em.op("dve", lambda cb=cb, bank=bank: V.tensor_scalar(
                                out=qf[:, cb * 8:(cb + 1) * 8, :], in0=bank[:].rearrange("p (h d) -> p h d", h=8)[:, :, 0:16],
                                scalar1=0.125, scalar2=None, op0=ALU.mult), reads=[bk], writes=[("qf", cb)])
                        qb3 = qb[:].rearrange("p (h d) -> p h d", h=32)
                        x1, x2 = qf[:, :, 0:8], qf[:, :, 8:16]
                        cosb, sinb = bc_mid(cs[s][:, 0:8], 32), bc_mid(cs[s][:, 8:16], 32)
                        qfk = [("qf", cb) for cb in range(4)]
                        qbk = [("qb", cb) for cb in range(4)]
                        for ti, (xa, cb_) in enumerate(((x1, cosb), (x2, sinb), (x2, cosb), (x1, sinb))):
                            em.op("dve", lambda ti=ti, xa=xa, cb_=cb_: V.tensor_tensor(out=tmp[:, ti], in0=xa, in1=cb_, op=ALU.mult),
                                  reads=qfk + [("cs", s)], writes=[("tmpq", ti)])
                        em.op("dve", lambda: V.tensor_tensor(out=qb3[:, :, 0:8], in0=tmp[:, 0], in1=tmp[:, 1], op=ALU.subtract),
                              reads=[("tmpq", 0), ("tmpq", 1)], writes=qbk)
                        em.op("dve", lambda: V.tensor_tensor(out=qb3[:, :, 8:16], in0=tmp[:, 2], in1=tmp[:, 3], op=ALU.add),
                              reads=[("tmpq", 2), ("tmpq", 3)], writes=qbk)
                        for m in range(16):
                            em.op("pe", lambda m=m: P.transpose(out=w.tp[:, m, :], in_=qb[:, m * 128:(m + 1) * 128], identity=ident[:]),
                                  reads=[("qb", m // 4)], writes=[("tp" if m < 8 else "tp2",)])
                        em.op("dve", lambda: V.tensor_copy(out=qT[s][:, 0:8, :], in_=w.tp[:, 0:8, :]), reads=[("tp",)], writes=[("qT", s, 0)])
                        em.op("dve", lambda: V.tensor_copy(out=qT[s][:, 8:16, :], in_=w.tp[:, 8:16, :]), reads=[("tp2",)],
                              writes=[("qT", s, 1)])
                        em.dma(sgT[s][:], SG[b].rearrange("p (c t) -> p c t", c=16), writes=[("sgT", s)], semkey=("sgT", s))

                    def stage2(b):
                        s = b % 2
                        js = [jj for jj in range(3) if 0 <= b - 1 + jj < NT]
                        tiles = {}

                        def scores(gp):
                            g = gp // 2
                            pl = []
                            for jj in js:
                                bank, bk = mm[1 + (pexi[0] % 3)]
                                px = pex[pexi[0] % len(pex)]
                                pk = ("pex", pexi[0] % len(pex))
                                pexi[0] += 1
                                em.op("pe", lambda bank=bank, jj=jj: P.matmul(
                                    bank[:], lhsT=kTl[s][jj][:, gp, :], rhs=qT[s][:, 4 * g:4 * g + 4, :], start=True, stop=True),
                                      reads=[("kTl", s, jj), ("qT", s, g // 2)], writes=[bk])
                                em.op("act", lambda bank=bank, px=px: S.activation(out=px[:].rearrange("p c t -> p (c t)"), in_=bank[:],
                                                                                    func=AF.Exp),
                                      reads=[bk], writes=[pk])
                                if jj != 1:
                                    side = 0 if jj == 0 else 1
                                    if side == 0:
                                        em.op("dve", lambda px=px, side=side: V.tensor_tensor(
                                            out=px[:], in0=px[:], in1=bc_mid(maskb[s][:, side, :], 4), op=ALU.mult),
                                              reads=[pk, ("maskb", s, side)], writes=[pk])
                                    else:
                                        em.op("pool", lambda px=px, side=side: G.tensor_tensor(
                                            out=px[:], in0=px[:], in1=bc_mid(maskb[s][:, side, :], 4), op=ALU.mult),
                                              reads=[pk, ("maskb", s, side)], writes=[pk])
                                pl.append((jj, px, pk))
                            tiles[gp] = pl

                        def pv_norm(gp):
                            g, par = gp // 2, gp % 2
                            pl = tiles[gp]
                            bank, bk = mm[4 + (gp % 2)]
                            for n_, (jj, px, pk) in enumerate(pl):
                                em.op("pe", lambda bank=bank, jj=jj, px=px, n_=n_: P.matmul(
                                    bank[:], lhsT=val[s][jj][:, gp, :], rhs=px[:].rearrange("p c t -> p (c t)"),
                                    start=(n_ == 0), stop=False),
                                      reads=[("val", s, jj), pk], writes=[bk])
                            em.op("pe", lambda bank=bank: P.matmul(bank[:], lhsT=sel[:, par, :],
                                                                   rhs=prow[:, gp, :].unsqueeze(2).broadcast_to([128, 4, 128]),
                                                                   start=False, stop=True), writes=[bk])
                            nr = slice(par * 64, par * 64 + 64)
                            dr = slice((1 - par) * 64, (1 - par) * 64 + 64)
                            rc, on = rec[gp % 2], onrm[gp % 2]
                            rk, ok_ = ("rec", gp % 2), ("onrm", gp % 2)
                            em.op("act", lambda: S.activation(out=rc[dr, :], in_=bank[dr, :], func=AF.Ln), reads=[bk], writes=[rk])
                            em.op("act", lambda: S.activation(out=rc[dr, :], in_=rc[dr, :], func=AF.Exp, scale=-1.0),
                                  reads=[rk], writes=[rk])
                            em.op("dve", lambda: V.tensor_tensor(out=on[nr, :], in0=bank[nr, :], in1=rc[dr, :], op=ALU.mult),
                                  reads=[bk, rk], writes=[ok_])
                            em.op("pool", lambda: G.tensor_tensor(
                                out=gatedT[nr, 4 * g:4 * g + 4, :].rearrange("p c t -> p (c t)"), in0=on[nr, :],
                                in1=sgT[s][nr, 4 * g:4 * g + 4, :].rearrange("p c t -> p (c t)"), op=ALU.mult),
                                  reads=[ok_, ("sgT", s)], writes=[("gatedT", g)])

                        LAG = 3
                        for gp in range(8 + LAG):
                            if gp < 8:
                                scores(gp)
                            if gp >= LAG:
                                pv_norm(gp - LAG)
                        epilogue(w, b, b % 3, gatedT, "per_group", Wo, [mm[4], mm[5]], dst, last)

                    pipeline(stage1, stage2, NT, prefetch=lambda t: load_x(w, t, src))
                    em.flush()

        def layer_B(li, j, src, dst, last):
            with ExitStack() as st:
                load_gain(li, st, last)
                lg = sbuf(st, "lg", [128, 8], F32)
                kd16 = sbuf(st, "kd16", [128, 8], F32)
                cdt = sbuf(st, "cdt", [128, 8], F32)
                Dm = sbuf(st, "Dm", [128, 4, 128], F32)
                qdf = sbuf(st, "qdf", [128, 4, 128], F32)
                qdb = sbuf(st, "qdb", [128, 4, 128], F32)
                flg = sbuf(st, "flagsB", [128, 2, NT], F32)
                Sm = sbuf(st, "Sm", [128, 8, 512], F32)
                Sbf = sbuf(st, "Sbf", [128, 2, 2, 512], BF16)
                with ExitStack() as s0:
                    retT = sbuf(s0, "retT", [128, 4, 128], F32)
                    retP = sbuf(s0, "retP", [128, 2], F32)
                    t1 = sbuf(s0, "rt1", [128, 128], F32)
                    em.dma(lg[:], b_decay[j:j + 1, :].partition_broadcast(128), writes=[("lg",)], semkey="lg")
                    em.dma(retT[:], tb["retT"], writes=[("retT",)], semkey="retT")
                    em.dma(retP[:], tb["retP"], writes=[("retP",)], semkey="retP")
                    em.dma(flg[:], tb["flagsB"], writes=[("flg",)], semkey="flg")
                    em.op("act", lambda: S.activation(out=lg[:], in_=lg[:], func=AF.Exp, scale=-1.0), reads=[("lg",)], writes=[("lg",)])
                    em.op("dve", lambda: V.tensor_scalar(out=lg[:], in0=lg[:], scalar1=1.0, scalar2=None, op0=ALU.add),
                          reads=[("lg",)], writes=[("lg",)])
                    em.op("act", lambda: S.activation(out=lg[:], in_=lg[:], func=AF.Ln), reads=[("lg",)], writes=[("lg",)])
                    em.op("dve", lambda: V.tensor_scalar(out=lg[:], in0=lg[:], scalar1=-1.0, scalar2=None, op0=ALU.mult),
                          reads=[("lg",)], writes=[("lg",)])
                    em.op("act", lambda: S.activation(out=kd16[:, 0:4], in_=lg[:, 0:4], func=AF.Exp, scale=retP[:, 0:1]),
                          reads=[("lg",), ("retP",)], writes=[("kd16", 0)])
                    em.op("act", lambda: S.activation(out=kd16[:, 4:8], in_=lg[:, 4:8], func=AF.Exp, scale=retP[:, 1:2]),
                          reads=[("lg",), ("retP",)], writes=[("kd16", 1)])
                    em.op("dve", lambda: V.tensor_scalar(out=kd16[:], in0=kd16[:], scalar1=1.0 / 16, scalar2=None, op0=ALU.mult),
                          reads=[("kd16", 0), ("kd16", 1)], writes=[("kd16", 2)])
                    em.op("act", lambda: S.activation(out=cdt[:], in_=lg[:], func=AF.Exp, scale=128.0), reads=[("lg",)], writes=[("cdt",)])
                    for h in range(4):
                        em.op("act", lambda h=h: S.activation(out=qdf[:, h, :], in_=retT[:, 2, :], func=AF.Exp, scale=lg[:, h:h + 1]),
                              reads=[("lg",), ("retT",)], writes=[("qdf", h)])
                        em.op("act", lambda h=h: S.activation(out=qdb[:, h, :], in_=retT[:, 3, :], func=AF.Exp, scale=lg[:, 4 + h:5 + h]),
                              reads=[("lg",), ("retT",)], writes=[("qdb", h)])
                        em.op("dve", lambda h=h: V.tensor_scalar(out=t1[:], in0=retT[:, 0, :], scalar1=lg[:, h:h + 1], scalar2=None,
                                                                 op0=ALU.mult), reads=[("lg",), ("retT",)], writes=[("rt1",)])
                        em.op("dve", lambda h=h: V.scalar_tensor_tensor(out=t1[:], in0=retT[:, 1, :], scalar=lg[:, 4 + h:5 + h], in1=t1[:],
                                                                        op0=ALU.mult, op1=ALU.add),
                              reads=[("lg",), ("retT",), ("rt1",)], writes=[("rt1",)])
                        em.op("act", lambda h=h: S.activation(out=Dm[:, h, :], in_=t1[:], func=AF.Exp), reads=[("rt1",)], writes=[("Dm", h)])
                        em.op("dve", lambda h=h: V.tensor_scalar(out=Dm[:, h, :], in0=Dm[:, h, :], scalar1=1.0 / 16, scalar2=None,
                                                                 op0=ALU.mult), reads=[("Dm", h)], writes=[("Dm", h)])
                    em.op("pool", lambda: G.memset(Sm[:], 0.0), writes=[("Sm", i) for i in range(8)])
                    em.flush()

                def rope_half(x3, csb, nh, ta, tb_, tc, td):
                    x1, x2 = x3[:, :, 0:128], x3[:, :, 128:256]
                    cosb, sinb = bc_mid(csb[:, 0:128], nh), bc_mid(csb[:, 128:256], nh)
                    return x1, x2, cosb, sinb

                def state_update(kk, vb, direction, c, cdk, mmu):
                    for h in range(4):
                        for dc in range(2):
                            i = h * 2 + dc
                            bank, bk = mmu[i % len(mmu)]
                            em.op("pe", lambda h=h, dc=dc, bank=bank: P.matmul(bank[:], lhsT=kk[:, h, dc * 128:(dc + 1) * 128],
                                                                              rhs=vb[:, h * 512:(h + 1) * 512], start=True, stop=True),
                                  reads=[("ksc",), ("vbf",)], writes=[bk])
                            em.op("dve", lambda h=h, i=i, bank=bank: V.scalar_tensor_tensor(out=Sm[:, i, :], in0=Sm[:, i, :],
                                                                                        scalar=cdk[:, h:h + 1], in1=bank[:],
                                                                                        op0=ALU.mult, op1=ALU.add),
                                  reads=[bk, ("Sm", i), ("cdk",)], writes=[("Sm", i)])

                if bstop == 1:
                    return
                with ExitStack() as s1:
                    Wi = load_weights(s1, b_w_in[j][:, 1024:6144], 5120, "WiB1")
                    w = Work()
                    alloc_common(s1, w, 2)
                    cs = [sbuf(s1, "csB%d" % i, [128, 256], F32) for i in range(2)]
                    kf = [sbuf(s1, "kf%d" % i, [128, 4, 256], F32) for i in range(2)]
                    tt = [sbuf(s1, "rtmp%d" % i, [128, 4, 128], F32) for i in range(4)]
                    ksc = sbuf(s1, "ksc", [128, 4, 256], BF16)
                    vbf = [sbuf(s1, "vbf%d" % i, [128, E], BF16) for i in range(2)]
                    sgb = [sbuf(s1, "sgb%d" % i, [128, E], BF16) for i in range(2)]
                    kbs = sbuf(s1, "kbs", [128, 4, 256], BF16)
                    sck = sbuf(s1, "sck", [128, 8], F32)
                    mm = [(psum(s1, "mmB%d" % i, [128, 512], F32), ("mm", i)) for i in range(6)]

                    def stage1(t):
                        c = NT - 1 - t
                        s = t % 2
                        sx = prologue(w, c, li, src)
                        hTc, hk = w.hT[sx], ("hT", sx)
                        em.dma(cs[s][:], tb["ropeB"][c * 128:(c + 1) * 128, :], writes=[("cs", s)], semkey=("cs", s))
                        nb = [0]

                        def proj(col):
                            bank, bk = mm[nb[0] % 3]
                            nb[0] += 1
                            for kc in range(8):
                                em.op("pe", lambda kc=kc, bank=bank: P.matmul(bank[:], lhsT=hTc[:, kc, :], rhs=Wi[:, kc, col:col + 512],
                                                                              start=(kc == 0), stop=(kc == 7)),
                                      reads=[hk], writes=[bk])
                            return bank, bk
                        for cb in range(2):
                            bank, bk = proj(cb * 512)
                            em.op("act", lambda cb=cb, bank=bank: S.copy(out=kf[s][:, 2 * cb:2 * cb + 2, :].rearrange("p h d -> p (h d)"),
                                                                        in_=bank[:]), reads=[bk], writes=[("kf", s, cb)])
                        for cb in range(4):
                            bank, bk = proj(1024 + cb * 512)
                            if cb % 2 == 0:
                                em.op("act", lambda cb=cb, bank=bank: S.copy(out=vbf[s][:, cb * 512:(cb + 1) * 512], in_=bank[:]),
                                      reads=[bk], writes=[("vbf", s)])
                            else:
                                em.op("dve", lambda cb=cb, bank=bank: V.tensor_copy(out=vbf[s][:, cb * 512:(cb + 1) * 512], in_=bank[:]),
                                      reads=[bk], writes=[("vbf", s)])
                        for cb in range(4):
                            bank, bk = proj(3072 + cb * 512)
                            em.op("act", lambda cb=cb, bank=bank: S.activation(out=sgb[s][:, cb * 512:(cb + 1) * 512], in_=bank[:],
                                                                                func=AF.Silu), reads=[bk], writes=[("sgb", s)])

                    def stage2(t):
                        c = NT - 1 - t
                        s = t % 2
                        x1, x2, cosb, sinb = rope_half(kf[s][:], cs[s], 4, *[None] * 4)
                        kfk = [("kf", s, 0), ("kf", s, 1), ("cs", s)]
                        em.op("pool", lambda: G.tensor_tensor(out=tt[0][:], in0=x1, in1=cosb, op=ALU.mult), reads=kfk, writes=[("tt", 0)])
                        em.op("pool", lambda: G.tensor_tensor(out=tt[1][:], in0=x2, in1=sinb, op=ALU.mult), reads=kfk, writes=[("tt", 1)])
                        em.op("dve", lambda: V.tensor_tensor(out=tt[2][:], in0=x2, in1=cosb, op=ALU.mult), reads=kfk, writes=[("tt", 2)])
                        em.op("dve", lambda: V.tensor_tensor(out=tt[3][:], in0=x1, in1=sinb, op=ALU.mult), reads=kfk, writes=[("tt", 3)])
                        em.op("pool", lambda: G.tensor_tensor(out=tt[0][:], in0=tt[0][:], in1=tt[1][:], op=ALU.subtract),
                              reads=[("tt", 0), ("tt", 1)], writes=[("tt", 0)])
                        em.op("dve", lambda: V.tensor_tensor(out=tt[2][:], in0=tt[2][:], in1=tt[3][:], op=ALU.add),
                              reads=[("tt", 2), ("tt", 3)], writes=[("tt", 2)])
                        em.op("dve", lambda: V.tensor_scalar(out=sck[:, 0:4], in0=kd16[:, 4:8], scalar1=flg[:, 1, c:c + 1], scalar2=None,
                                                             op0=ALU.mult), writes=[("sck",)])
                        em.op("dve", lambda: V.tensor_scalar(out=sck[:, 4:8], in0=cdt[:, 4:8], scalar1=flg[:, 1, c:c + 1], scalar2=None,
                                                             op0=ALU.mult), writes=[("cdk",)])
                        em.op("act", lambda: S.copy(out=kbs[:, :, 0:128], in_=tt[0][:]), reads=[("tt", 0)], writes=[("kbs",)])
                        em.op("act", lambda: S.copy(out=kbs[:, :, 128:256], in_=tt[2][:]), reads=[("tt", 2)], writes=[("kbs",)])
                        em.dma(KB[c].rearrange("p (h d) -> p h d", h=4), kbs[:], reads=[("kbs",)], writes=[("D_KB", c)], semkey="kbst")
                        sb_ = sck[:, 0:4].unsqueeze(2).broadcast_to([128, 4, 128])
                        em.op("pool", lambda: G.tensor_tensor(out=ksc[:, :, 0:128], in0=tt[0][:], in1=sb_, op=ALU.mult),
                              reads=[("tt", 0), ("sck",)], writes=[("ksc",)])
                        em.op("dve", lambda: V.tensor_tensor(out=ksc[:, :, 128:256], in0=tt[2][:], in1=sb_, op=ALU.mult),
                              reads=[("tt", 2), ("sck",)], writes=[("ksc",)])
                        em.dma(SG[c], sgb[s][:], reads=[("sgb", s)], writes=[("D_SG", c)], semkey=("sgbst", s))
                        em.dma(VB[c], vbf[s][:], reads=[("vbf", s)], writes=[("D_VB", c)], semkey=("vbst", s))
                        for h in range(4):
                            em.op("act", lambda h=h: S.copy(out=Sbf[:, h % 2], in_=Sm[:, 2 * h:2 * h + 2, :]),
                                  reads=[("Sm", 2 * h), ("Sm", 2 * h + 1)], writes=[("Sbf", h % 2)])
                            em.dma(Sscr[c][:, h * 1024:(h + 1) * 1024].rearrange("p (i e) -> p i e", i=2), Sbf[:, h % 2],
                                   reads=[("Sbf", h % 2)], writes=[("D_S", c, h)], semkey=("Sst", h % 2))
                        for h in range(4):
                            for dc in range(2):
                                i = h * 2 + dc
                                bank, bk = mm[4 + i % 2]
                                em.op("pe", lambda h=h, dc=dc, bank=bank: P.matmul(bank[:], lhsT=ksc[:, h, dc * 128:(dc + 1) * 128],
                                                                                  rhs=vbf[s][:, h * 512:(h + 1) * 512], start=True, stop=True),
                                      reads=[("ksc",), ("vbf", s)], writes=[bk])
                                em.op("dve", lambda h=h, i=i, bank=bank: V.scalar_tensor_tensor(out=Sm[:, i, :], in0=Sm[:, i, :],
                                                                                            scalar=sck[:, 4 + h:5 + h], in1=bank[:],
                                                                                            op0=ALU.mult, op1=ALU.add),
                                      reads=[bk, ("Sm", i), ("cdk",)], writes=[("Sm", i)])

                    pipeline(stage1, stage2, NT)
                    em.flush()

                if bstop == 2:
                    return
                with ExitStack() as s2:
                    Wi = load_weights(s2, b_w_in[j][:, 0:1024], 1024, "WiB2")
                    Wo = load_weights(s2, b_w_out[j], D, "WoB")
                    w = Work()
                    alloc_common(s2, w, 3)
                    gjunk = sbuf(s2, "junkB", [128, 512], BF16)
                    cs = [sbuf(s2, "csB%d" % i, [128, 256], F32) for i in range(2)]
                    qkf = sbuf(s2, "qkf", [128, 4, 256], F32)
                    tt = [sbuf(s2, "rtmp%d" % i, [128, 4, 128], F32) for i in range(4)]
                    qbf = sbuf(s2, "qbf", [128, 4, 256], BF16)
                    kbf = [sbuf(s2, "kbf%d" % i, [128, 4, 256], BF16) for i in range(2)]
                    ksc = [sbuf(s2, "ksc%d" % i, [128, 4, 256], BF16) for i in range(2)]
                    qTs = [sbuf(s2, "qTs%d" % i, [128, 4, 8, 128], BF16) for i in range(2)]
                    vbf = [sbuf(s2, "vbf%d" % i, [128, E], BF16) for i in range(2)]
                    sgh = [sbuf(s2, "sgh%d" % i, [128, 512], BF16) for i in range(2)]
                    PT = [sbuf(s2, "PT%d" % i, [128, 128], BF16) for i in range(2)]
                    gtok = sbuf(s2, "gtok", [128, E], BF16)
                    Sbl = sbuf(s2, "Sbl", [128, 2, 2, 512], BF16)
                    sck = [sbuf(s2, "sck%d" % i, [128, 8], F32) for i in range(2)]
                    gst = sbuf(s2, "gst", [128, 4, 4], F32)
                    mm = [(psum(s2, "mmB%d" % i, [128, 512], F32), ("mm", i)) for i in range(5)]
                    tpB = psum(s2, "tpB", [128, 8, 128], BF16)
                    em.op("pool", lambda: G.memset(Sm[:], 0.0), writes=[("Sm", i) for i in range(8)])

                    def stage1(c):
                        sx = prologue(w, c, li, src)
                        s = c % 2
                        hTc, hk = w.hT[sx % 2], ("hT", sx % 2)
                        em.dma(cs[s][:], tb["ropeB"][c * 128:(c + 1) * 128, :], writes=[("cs", s)], semkey=("cs", s))
                        em.op("dve", lambda: V.tensor_scalar(out=sck[s][:, 0:4], in0=kd16[:, 0:4], scalar1=flg[:, 0, c:c + 1], scalar2=None,
                                                             op0=ALU.mult), writes=[("sck", s)])
                        em.op("dve", lambda: V.tensor_scalar(out=sck[s][:, 4:8], in0=cdt[:, 0:4], scalar1=flg[:, 0, c:c + 1], scalar2=None,
                                                             op0=ALU.mult), writes=[("cdk", s)])
                        for cb in range(2):
                            bank, bk = mm[0]
                            col = cb * 512
                            for kc in range(8):
                                em.op("pe", lambda kc=kc, col=col, bank=bank: P.matmul(bank[:], lhsT=hTc[:, kc, :],
                                                                                     rhs=Wi[:, kc, col:col + 512],
                                                                                     start=(kc == 0), stop=(kc == 7)),
                                      reads=[hk], writes=[bk])
                            em.op("act", lambda cb=cb, bank=bank: S.copy(out=qkf[:, 2 * cb:2 * cb + 2, :].rearrange("p h d -> p (h d)"),
                                                                        in_=bank[:]), reads=[bk], writes=[("qkf", cb)])
                        x1, x2, cosb, sinb = rope_half(qkf[:], cs[s], 4, *[None] * 4)
                        kfk = [("qkf", 0), ("qkf", 1), ("cs", s)]
                        em.op("pool", lambda: G.tensor_tensor(out=tt[0][:], in0=x1, in1=cosb, op=ALU.mult), reads=kfk, writes=[("tt", 0)])
                        em.op("pool", lambda: G.tensor_tensor(out=tt[1][:], in0=x2, in1=sinb, op=ALU.mult), reads=kfk, writes=[("tt", 1)])
                        em.op("dve", lambda: V.tensor_tensor(out=tt[2][:], in0=x2, in1=cosb, op=ALU.mult), reads=kfk, writes=[("tt", 2)])
                        em.op("dve", lambda: V.tensor_tensor(out=tt[3][:], in0=x1, in1=sinb, op=ALU.mult), reads=kfk, writes=[("tt", 3)])
                        em.op("pool", lambda: G.tensor_tensor(out=qbf[:, :, 0:128], in0=tt[0][:], in1=tt[1][:], op=ALU.subtract),
                              reads=[("tt", 0), ("tt", 1)], writes=[("qbf",)])
                        em.op("dve", lambda: V.tensor_tensor(out=qbf[:, :, 128:256], in0=tt[2][:], in1=tt[3][:], op=ALU.add),
                              reads=[("tt", 2), ("tt", 3)], writes=[("qbf",)])
                        em.dma(kbf[s][:], KB[c].rearrange("p (h d) -> p h d", h=4), writes=[("kbf", s)], semkey=("kbf", s))
                        em.dma(vbf[s][:], VB[c], writes=[("vbf", s)], semkey=("vbf", s))
                        em.op("pool", lambda: G.tensor_tensor(out=ksc[s][:], in0=kbf[s][:],
                                                              in1=sck[s][:, 0:4].unsqueeze(2).broadcast_to([128, 4, 256]), op=ALU.mult),
                              reads=[("kbf", s), ("sck", s)], writes=[("ksc", s)])
                        qb2 = qbf[:].rearrange("p h d -> p (h d)")
                        kb2 = kbf[s][:].rearrange("p h d -> p (h d)")
                        for m in range(8):
                            em.op("pe", lambda m=m: P.transpose(out=w.tp[:, m, :], in_=qb2[:, m * 128:(m + 1) * 128], identity=ident[:]),
                                  reads=[("qbf",)], writes=[("tp",)])
                        for m in range(8):
                            em.op("pe", lambda m=m: P.transpose(out=w.tp[:, 8 + m, :], in_=kb2[:, m * 128:(m + 1) * 128], identity=ident[:]),
                                  reads=[("kbf", s)], writes=[("tp2",)])
                        q4 = qTs[s]
                        em.op("act", lambda: S.copy(out=q4[:, 0], in_=w.tp[:, 0:8, :]), reads=[("tp",)], writes=[("qT", s)])
                        for h in range(4):
                            em.op("dve", lambda h=h: V.tensor_tensor(out=q4[:, 1, 2 * h:2 * h + 2, :], in0=q4[:, 0, 2 * h:2 * h + 2, :],
                                                                     in1=bc_mid(qdf[:, h, :], 2), op=ALU.mult),
                                  reads=[("qT", s)], writes=[("qTf", s)])
                            em.op("pool", lambda h=h: G.tensor_tensor(out=q4[:, 2, 2 * h:2 * h + 2, :], in0=q4[:, 0, 2 * h:2 * h + 2, :],
                                                                      in1=bc_mid(qdb[:, h, :], 2), op=ALU.mult),
                                  reads=[("qT", s)], writes=[("qTb", s)])
                        em.op("act", lambda: S.copy(out=q4[:, 3], in_=w.tp[:, 8:16, :]), reads=[("tp2",)], writes=[("kT", s)])

                    def stage2(c):
                        s = c % 2
                        q4 = qTs[s]
                        gatedT = q4[:, 0:2].rearrange("p a c t -> p (a c) t")
                        for h in range(4):
                            em.dma(sgh[h % 2][:], SG[c][:, h * 512:(h + 1) * 512], writes=[("sgh", h % 2)], semkey=("sgh", h % 2))
                            em.op("act", lambda h=h: S.copy(out=Sbf[:, h % 2], in_=Sm[:, 2 * h:2 * h + 2, :]),
                                  reads=[("Sm", 2 * h), ("Sm", 2 * h + 1)], writes=[("Sbf", h % 2)])
                            em.dma(Sbl[:, h % 2], Sscr[c][:, h * 1024:(h + 1) * 1024].rearrange("p (i e) -> p i e", i=2),
                                   writes=[("Sbl", h % 2)], semkey=("Sbl", h % 2))
                            sb, sk_ = mm[1]
                            for dc in range(2):
                                em.op("pe", lambda dc=dc, h=h: P.matmul(sb[:, 0:128], lhsT=q4[:, 3, h * 2 + dc, :],
                                                                        rhs=q4[:, 0, h * 2 + dc, :], start=(dc == 0), stop=(dc == 1)),
                                      reads=[("kT", s), ("qT", s)], writes=[sk_])
                            pt = PT[h % 2]
                            em.op("dve", lambda h=h, pt=pt: V.tensor_tensor(out=pt[:], in0=sb[:, 0:128], in1=Dm[:, h, :], op=ALU.mult),
                                  reads=[sk_], writes=[("PT", h % 2)])
                            ob, ok_ = mm[2 + h % 2]
                            em.op("pe", lambda h=h, ob=ob, pt=pt: P.matmul(ob[:], lhsT=pt[:], rhs=vbf[s][:, h * 512:(h + 1) * 512],
                                                                           start=True, stop=False),
                                  reads=[("PT", h % 2), ("vbf", s)], writes=[ok_])
                            for dc in range(2):
                                i = h * 2 + dc
                                em.op("pe", lambda i=i, ob=ob, h=h, dc=dc: P.matmul(ob[:], lhsT=q4[:, 1, i, :], rhs=Sbf[:, h % 2, dc, :],
                                                                               start=False, stop=False),
                                      reads=[("qTf", s), ("Sbf", h % 2)], writes=[ok_])
                            for dc in range(2):
                                i = h * 2 + dc
                                em.op("pe", lambda i=i, ob=ob, dc=dc, h=h: P.matmul(ob[:], lhsT=q4[:, 2, i, :], rhs=Sbl[:, h % 2, dc, :],
                                                                               start=False, stop=(dc == 1)),
                                      reads=[("qTb", s), ("Sbl", h % 2)], writes=[ok_])
                            gs = gst[:, h, :]
                            em.op("act", lambda ob=ob, gs=gs: S.activation(out=gjunk[:], in_=ob[:], func=AF.Square, accum_out=gs[:, 0:1]),
                                  reads=[ok_], writes=[("gjunk",), ("gst", h)])
                            em.op("dve", lambda gs=gs: V.tensor_scalar(out=gs[:, 1:2], in0=gs[:, 0:1], scalar1=1.0 / 512, scalar2=EPS,
                                                                       op0=ALU.mult, op1=ALU.add), reads=[("gst", h)], writes=[("gst", h)])
                            em.op("act", lambda gs=gs: S.activation(out=gs[:, 2:3], in_=gs[:, 1:2], func=AF.Ln),
                                  reads=[("gst", h)], writes=[("gst", h)])
                            em.op("act", lambda gs=gs: S.activation(out=gs[:, 3:4], in_=gs[:, 2:3], func=AF.Exp, scale=-0.5),
                                  reads=[("gst", h)], writes=[("gst", h)])
                            em.op("dve", lambda h=h, ob=ob, gs=gs: V.scalar_tensor_tensor(out=gtok[:, h * 512:(h + 1) * 512], in0=ob[:],
                                                                                        scalar=gs[:, 3:4], in1=sgh[h % 2][:], op0=ALU.mult,
                                                                                        op1=ALU.mult),
                                  reads=[ok_, ("gst", h), ("sgh", h % 2)], writes=[("gtok", h)])
                        for h in range(4):
                            for dc in range(2):
                                i = h * 2 + dc
                                bank, bk = mm[4] if i % 2 == 0 else mm[1]
                                em.op("pe", lambda h=h, dc=dc, bank=bank: P.matmul(bank[:], lhsT=ksc[s][:, h, dc * 128:(dc + 1) * 128],
                                                                                  rhs=vbf[s][:, h * 512:(h + 1) * 512], start=True, stop=True),
                                      reads=[("ksc", s), ("vbf", s)], writes=[bk])
                                em.op("dve", lambda h=h, i=i, bank=bank: V.scalar_tensor_tensor(out=Sm[:, i, :], in0=Sm[:, i, :],
                                                                                            scalar=sck[s][:, 4 + h:5 + h], in1=bank[:],
                                                                                            op0=ALU.mult, op1=ALU.add),
                                      reads=[bk, ("Sm", i), ("cdk", s)], writes=[("Sm", i)])
                        for half in range(2):
                            for m in range(8):
                                mm_ = half * 8 + m
                                em.op("pe", lambda m=m, mm_=mm_: P.transpose(out=tpB[:, m, :], in_=gtok[:, mm_ * 128:(mm_ + 1) * 128],
                                                                             identity=ident[:]),
                                      reads=[("gtok", mm_ // 4)], writes=[("tpB",)])
                            if half == 0:
                                em.op("act", lambda: S.copy(out=gatedT[:, 0:8, :], in_=tpB[:]), reads=[("tpB",)], writes=[("qT", s)])
                            else:
                                em.op("dve", lambda: V.tensor_copy(out=gatedT[:, 8:16, :], in_=tpB[:]), reads=[("tpB",)], writes=[("qTf", s)])
                        epilogue(w, c, c % 3, gatedT, [("qT", s), ("qTf", s)], Wo, [mm[2], mm[3]], dst, last)

                    pipeline(stage1, stage2, NT)
                    em.flush()

        def layer_C(li, j, src, dst, last):
            with ExitStack() as st:
                load_gain(li, st, False)
                Wi = load_weights(st, c_w_in[j][:, 0:2048], 2048, "WiCu")
                w = Work()
                alloc_common(st, w, 2)
                FC = sbuf(st, "FC", [128, 4, 2, 512], BF16)
                zb = [sbuf(st, "zb%d" % i, [128, 2, 2048], BF16) for i in range(2)]
                mm = [(psum(st, "mmC%d" % i, [128, 512], F32), ("mm", i)) for i in range(6)]
                em.dma(FC[:], tb["FC"], writes=[("FC",)], semkey="FC")

                hT4 = [sbuf(st, "hT4C_%d" % i, [128, 8, 512], BF16) for i in range(2)]
                uT4s = [sbuf(st, "uT4_%d" % i, [128, 16, 512], BF16) for i in range(2)]

                def uproj4(ss):
                    uT4 = uT4s[ss]
                    hks = [("hT4", ss, tl) for tl in range(4)]
                    for m in range(16):
                        bank, bk = mm[m % 2]
                        for kc in range(8):
                            em.op("pe", lambda kc=kc, m=m, bank=bank: P.matmul(bank[:], lhsT=Wi[:, kc, m * 128:(m + 1) * 128],
                                                                              rhs=hT4[ss][:, kc, :], start=(kc == 0), stop=(kc == 7)),
                                  reads=hks, writes=[bk])
                        if m % 2 == 0:
                            em.op("act", lambda m=m, bank=bank: S.copy(out=uT4[:, m, :], in_=bank[:]), reads=[bk], writes=[("uT", ss, m // 4)])
                        else:
                            em.op("dve", lambda m=m, bank=bank: V.tensor_copy(out=uT4[:, m, :], in_=bank[:]), reads=[bk],
                                  writes=[("uT", ss, m // 4)])

                def body1(t):
                    tl = t % 4
                    uT4 = uT4s[(t // 4) % 2]
                    ss = (t // 4) % 2
                    zs = t % 2
                    for grp in range(4):
                        for ri in range(2):
                            n_ = grp * 2 + ri
                            bank, bk = mm[2 + n_ % 4]
                            for cc in range(4):
                                em.op("pe", lambda cc=cc, grp=grp, ri=ri, bank=bank: P.matmul(
                                    bank[:], lhsT=uT4[:, grp * 4 + cc, tl * 128:(tl + 1) * 128], rhs=FC[:, cc, ri, :],
                                    start=(cc == 0), stop=(cc == 3)),
                                      reads=[("uT", ss, grp), ("FC",)], writes=[bk])
                            if n_ % 2 == 0:
                                em.op("act", lambda grp=grp, ri=ri, bank=bank: S.copy(out=zb[zs][:, ri, grp * 512:(grp + 1) * 512], in_=bank[:]),
                                      reads=[bk], writes=[("zb", zs)])
                            else:
                                em.op("dve", lambda grp=grp, ri=ri, bank=bank: V.tensor_copy(out=zb[zs][:, ri, grp * 512:(grp + 1) * 512],
                                                                                            in_=bank[:]), reads=[bk], writes=[("zb", zs)])
                    em.dma(Zscr[t * 128:(t + 1) * 128, :].rearrange("p (r c) -> p r c", r=2), zb[zs][:], reads=[("zb", zs)],
                           writes=[("D_Z", t)], semkey=("zst", zs))

                def c1s1(sb):
                    ss = sb % 2
                    for tl in range(4):
                        prologue(w, 4 * sb + tl, li, src, hT4[ss][:, :, tl * 128:(tl + 1) * 128], ("hT4", ss, tl))
                    uproj4(ss)

                def c1s2(sb):
                    for tl in range(4):
                        body1(4 * sb + tl)

                pipeline(c1s1, c1s2, NT // 4)
                em.flush()
            with ExitStack() as st:
                Gt = sbuf(st, "Gt", [NT, 128, 3, NT], BF16)
                zr = [sbuf(st, "zr%d" % i, [NT, 2, 2048], BF16) for i in range(2)]
                Bs = [sbuf(st, "Bs%d" % i, [NT, 2, 2048], BF16) for i in range(2)]
                mm = [(psum(st, "mmC%d" % i, [128, 512], F32), ("mm", i)) for i in range(8)]
                em.dma(Gt[:], tb["G"], writes=[("Gt",)], semkey="Gt")

                def body2a(n2):
                    s = n2 % 2
                    em.dma(zr[s][:], Zscr[n2::128, :].rearrange("p (r c) -> p r c", r=2), writes=[("zr", s)], semkey=("zr", s))
                    for cb in range(4):
                        (bre, kre), (bim, kim) = mm[(2 * cb) % 8], mm[(2 * cb + 1) % 8]
                        cs_ = slice(cb * 512, (cb + 1) * 512)
                        for (bank, bk, parts) in ((bre, kre, ((0, 0), (2, 1))), (bim, kim, ((1, 0), (0, 1)))):
                            for n_, (gi, ri) in enumerate(parts):
                                em.op("pe", lambda bank=bank, gi=gi, ri=ri, n_=n_, cs_=cs_: P.matmul(
                                    bank[0:NT, :], lhsT=Gt[:, n2, gi, :], rhs=zr[s][:, ri, cs_], start=(n_ == 0), stop=(n_ == 1)),
                                      reads=[("zr", s), ("Gt",)], writes=[bk])
                        em.op("act", lambda bre=bre, cs_=cs_: S.copy(out=Bs[s][:, 0, cs_], in_=bre[0:NT, :]), reads=[kre], writes=[("Bs", s)])
                        em.op("dve", lambda bim=bim, cs_=cs_: V.tensor_copy(out=Bs[s][:, 1, cs_], in_=bim[0:NT, :]), reads=[kim],
                              writes=[("Bs", s)])
                    em.dma(Bscr[:, n2, :].rearrange("p (r c) -> p r c", r=2), Bs[s][:], reads=[("Bs", s)], writes=[("D_B", n2)],
                           semkey=("bst", s))

                for n2 in range(128):
                    body2a(n2)
                em.flush()
            with ExitStack() as st:
                Ht = sbuf(st, "Ht", [128, 2, 128], BF16)
                Br = [sbuf(st, "Br%d" % i, [128, 2, 2048], BF16) for i in range(2)]
                Ms = [sbuf(st, "Ms%d" % i, [128, 2048], BF16) for i in range(2)]
                mm = [(psum(st, "mmC%d" % i, [128, 512], F32), ("mm", i)) for i in range(8)]
                em.dma(Ht[:], tb["H"], writes=[("Ht",)], semkey="Ht")

                def body2b(q):
                    s = q % 2
                    em.dma(Br[s][:], Bscr[q].rearrange("p (r c) -> p r c", r=2), writes=[("Br", s)], semkey=("Br", s))
                    for cb in range(4):
                        bank, bk = mm[(q * 4 + cb) % 8]
                        cs_ = slice(cb * 512, (cb + 1) * 512)
                        for ri in range(2):
                            em.op("pe", lambda bank=bank, ri=ri, cs_=cs_: P.matmul(bank[:], lhsT=Ht[:, ri, :], rhs=Br[s][:, ri, cs_],
                                                                                  start=(ri == 0), stop=(ri == 1)),
                                  reads=[("Br", s), ("Ht",)], writes=[bk])
                        if cb % 2 == 0:
                            em.op("act", lambda bank=bank, cs_=cs_: S.copy(out=Ms[s][:, cs_], in_=bank[:]), reads=[bk], writes=[("Ms", s)])
                        else:
                            em.op("dve", lambda bank=bank, cs_=cs_: V.tensor_copy(out=Ms[s][:, cs_], in_=bank[:]), reads=[bk],
                                  writes=[("Ms", s)])
                    em.dma(Mscr[q * 128:(q + 1) * 128, :], Ms[s][:], reads=[("Ms", s)], writes=[("D_M", q)], semkey=("mst", s))

                for q in range(NT):
                    body2b(q)
                em.flush()
            with ExitStack() as st:
                load_gain(li, st, last)
                Wi = load_weights(st, c_w_in[j][:, 2048:4096], 2048, "WiCg")
                Wo = load_weights(st, c_w_out[j], D, "WoC")
                w = Work()
                alloc_common(st, w, 8)
                hT4 = [sbuf(st, "hT4C3_%d" % i, [128, 8, 512], BF16) for i in range(2)]
                PAB = sbuf(st, "PAB", [128, 2, 128], BF16)
                MA = [sbuf(st, "MA%d" % i, [128, E], BF16) for i in range(2)]
                MB = [sbuf(st, "MB%d" % i, [128, E], BF16) for i in range(2)]
                sgT = [sbuf(st, "sgTC%d" % i, [128, 16, 512], BF16) for i in range(2)]
                gatedT = sbuf(st, "gatedTC", [128, 16, 128], BF16)
                mm = [(psum(st, "mmC%d" % i, [128, 512], F32), ("mm", i)) for i in range(6)]
                em.dma(PAB[:], tb["PAB"], writes=[("PAB",)], semkey="PAB")

                def gate4(ss):
                    hks = [("hT4", ss, tl) for tl in range(4)]
                    for m in range(16):
                        bank, bk = mm[m % 2]
                        for kc in range(8):
                            em.op("pe", lambda kc=kc, m=m, bank=bank: P.matmul(bank[:], lhsT=Wi[:, kc, m * 128:(m + 1) * 128],
                                                                              rhs=hT4[ss][:, kc, :], start=(kc == 0), stop=(kc == 7)),
                                  reads=hks, writes=[bk])
                        em.op("act", lambda m=m, bank=bank: S.activation(out=sgT[ss][:, m, :], in_=bank[:], func=AF.Silu),
                              reads=[bk], writes=[("sgT", ss, m // 4)])

                def body3(t, ss):
                    s = t % 8
                    tl = t % 4
                    em.dma(MA[t % 2][:], Mscr[t:t + TSA * 127 + 1:TSA, :], writes=[("MA", t % 2)], semkey=("MA", t % 2))
                    bB = 128 * TSB * (t // TSB) + t % TSB
                    em.dma(MB[t % 2][:], Mscr[bB:bB + TSB * 127 + 1:TSB, :], writes=[("MB", t % 2)], semkey=("MB", t % 2))
                    for m4 in range(4):
                        bank, bk = mm[2 + m4 % 2]
                        for mi in range(4):
                            m = m4 * 4 + mi
                            em.op("pe", lambda m=m, mi=mi, bank=bank: P.matmul(bank[:, mi * 128:(mi + 1) * 128],
                                                                              lhsT=MA[t % 2][:, m * 128:(m + 1) * 128], rhs=PAB[:, 0, :],
                                                                              start=True, stop=False),
                                  reads=[("MA", t % 2), ("PAB",)], writes=[bk])
                            em.op("pe", lambda m=m, mi=mi, bank=bank: P.matmul(bank[:, mi * 128:(mi + 1) * 128],
                                                                              lhsT=MB[t % 2][:, m * 128:(m + 1) * 128], rhs=PAB[:, 1, :],
                                                                              start=False, stop=True),
                                  reads=[("MB", t % 2), ("PAB",)], writes=[bk])
                        em.op("dve", lambda m4=m4, bank=bank: V.tensor_tensor(
                            out=gatedT[:, m4 * 4:(m4 + 1) * 4, :], in0=bank[:].rearrange("p (c t) -> p c t", c=4),
                            in1=sgT[ss][:, m4 * 4:(m4 + 1) * 4, tl * 128:(tl + 1) * 128], op=ALU.mult),
                              reads=[bk, ("sgT", ss, m4)], writes=[("gatedT",)])
                    epilogue(w, t, s, gatedT, ("gatedT",), Wo, [mm[4], mm[5]], dst, last)

                def c3s1(sb):
                    ss = sb % 2
                    for tl in range(4):
                        prologue(w, 4 * sb + tl, li, src, hT4[ss][:, :, tl * 128:(tl + 1) * 128], ("hT4", ss, tl))
                    gate4(ss)

                def c3s2(sb):
                    for tl in range(4):
                        body3(4 * sb + tl, sb % 2)

                pipeline(c3s1, c3s2, NT // 4)
                em.flush()

        cnt = {"A": 0, "B": 0, "C": 0}
        src = x_in
        for li, kind in enumerate(layers):
            last = final_norm and (li == len(layers) - 1)
            dst = y_out if li == len(layers) - 1 else xscr
            j = cnt[kind]
            cnt[kind] += 1
            if kind == "A":
                layer_A(li, j, src, dst, last)
            elif kind == "B":
                layer_B(li, j, src, dst, last)
            else:
                layer_C(li, j, src, dst, last)
            src = dst
        nc._em_n_instr = em.n_instr
    return nc


def kernel(x_prompt, x_sample, norm_g, final_norm_g, a_w_in, a_sink, a_w_out,
           b_w_in, b_decay, b_w_out, c_w_in, c_w_out):
    NT, TSA, TSB = 128, 128, 16
    nc = build(NT, TSA, TSB)
    f = lambda a: np.ascontiguousarray(np.asarray(a, dtype=np.float32))
    common = {"norm_g": f(norm_g), "final_norm_g": f(final_norm_g).reshape(1, D), "a_w_in": f(a_w_in), "a_sink": f(a_sink),
              "a_w_out": f(a_w_out), "b_w_in": f(b_w_in), "b_decay": f(b_decay).reshape(1, 8), "b_w_out": f(b_w_out),
              "c_w_in": f(c_w_in), "c_w_out": f(c_w_out)}
    tabA = const_tables(NT, TSA, 1.0, 0.0, TSA, TSB)
    tabB = const_tables(NT, TSB, 0.0, 1.0, TSA, TSB)
    xp = f(x_prompt)
    xs = f(x_sample)
    in_maps = []
    for c in range(8):
        if c < 2:
            m = dict(common, **tabA)
            m["x"] = xs[c]
        else:
            m = dict(common, **tabB)
            xx = np.zeros((8, 2048, D), np.float32)
            seqs = list(range((c - 2) * 6, min((c - 2) * 6 + 6, 32)))
            xx[:len(seqs)] = xp[seqs]
            m["x"] = xx.reshape(NT * 128, D)
        in_maps.append(m)
    res = run_bass_kernel_spmd(nc, in_maps, core_ids=list(range(8)))
    y_s = np.stack([res.results[c]["y"] for c in range(2)], 0).astype(np.float32)
    y_p = np.zeros((32, 2048, D), np.float32)
    for c in range(2, 8):
        seqs = list(range((c - 2) * 6, min((c - 2) * 6 + 6, 32)))
        yy = res.results[c]["y"].reshape(8, 2048, D)
        y_p[seqs] = yy[:len(seqs)]
    return (y_p, y_s)
```

```python
import math
from contextlib import ExitStack

import numpy as np
import ml_dtypes

import concourse.bass as bass
import concourse.mybir as mybir
from concourse.bass_utils import run_bass_kernel_spmd

F32 = mybir.dt.float32
BF16 = mybir.dt.bfloat16
AF = mybir.ActivationFunctionType
ALU = mybir.AluOpType
NPBF = ml_dtypes.bfloat16

D = 1024
E = 2048
EPS = 1e-6
A_IN = 4608
B_IN = 6144
C_IN = 4096
LAYERS = ("A", "B", "C", "A")


class Em:
    ENGS = ("pe", "act", "dve", "pool", "sp")

    def __init__(self, nc, stack):
        self.nc = nc
        self.eng = {"pe": nc.tensor, "act": nc.scalar, "dve": nc.vector, "pool": nc.gpsimd, "sp": nc.sync}
        self.stack = stack
        self.sem = {e: stack.enter_context(nc.semaphore("s_" + e)) for e in self.ENGS}
        self.semval = {e: 0 for e in self.ENGS}
        self.dsem = {}
        self.dval = {}
        self.ops = []
        self.n_instr = 0

    def op(self, e, fn, reads=(), writes=()):
        self.ops.append(("c", e, fn, tuple(reads), tuple(writes), None))

    def dma(self, out, in_, reads=(), writes=(), semkey=None, q="sp", **kw):
        assert semkey is not None
        self.ops.append(("d", q, (out, in_, kw), tuple(reads), tuple(writes), semkey))

    @staticmethod
    def _hoist(ops):
        pos = [0.0] * len(ops)
        last_touch = {}
        last_sem = {}
        nh = 0
        for i, o in enumerate(ops):
            kind, e, fn, reads, writes, semkey = o
            p = float(i)
            if kind == "d" and not reads and writes and not str(writes[0][0]).startswith("D_"):
                q = max([last_touch.get(k, -1.0) for k in writes] + [last_sem.get(semkey, -1.0)])
                nh += 1
                p = min(p, q + 1e-7 * nh)
            pos[i] = p
            for k in reads:
                last_touch[k] = max(last_touch.get(k, -1.0), p)
            for k in writes:
                last_touch[k] = max(last_touch.get(k, -1.0), p)
            if kind == "d":
                last_sem[semkey] = max(last_sem.get(semkey, -1.0), p)
        order = sorted(range(len(ops)), key=lambda i: (pos[i], i))
        return [ops[i] for i in order]

    def flush(self):
        ops = self._hoist(self.ops)
        self.ops = []
        n = len(ops)
        last_w = {}
        readers = {}
        eidx = {e: 0 for e in self.ENGS}
        op_eidx = [0] * n
        known = {e: {} for e in self.ENGS}
        waits = [None] * n
        needed = [False] * n
        for i, o in enumerate(ops):
            kind, e, fn, reads, writes, semkey = o
            eidx[e] += 1
            op_eidx[i] = eidx[e]
            deps = set()
            for k in reads:
                j = last_w.get(k)
                if j is not None:
                    deps.add(j)
            for k in writes:
                j = last_w.get(k)
                if j is not None:
                    deps.add(j)
                for r in readers.get(k, ()):
                    deps.add(r)
            deps.discard(i)
            w = []
            kn = known[e]
            for j in sorted(deps):
                oj = ops[j]
                if oj[0] == "c":
                    ej = oj[1]
                    if ej == e and e == "pe":
                        continue
                    kk = ej
                    if kn.get(kk, -1) >= op_eidx[j]:
                        continue
                    kn[kk] = op_eidx[j]
                else:
                    kk = ("d", oj[5])
                    if kn.get(kk, -1) >= j:
                        continue
                    kn[kk] = j
                w.append(j)
                needed[j] = True
            waits[i] = w
            for k in writes:
                last_w[k] = i
                readers[k] = []
            for k in reads:
                readers.setdefault(k, []).append(i)
        last_of = {}
        for i, o in enumerate(ops):
            if o[0] == "c":
                last_of[o[1]] = i
            else:
                needed[i] = True
        for i in last_of.values():
            needed[i] = True
        val = [None] * n
        for i, o in enumerate(ops):
            if not needed[i]:
                continue
            if o[0] == "c":
                self.semval[o[1]] += 1
                val[i] = (self.sem[o[1]], self.semval[o[1]])
            else:
                sk = o[5]
                if sk not in self.dsem:
                    self.dsem[sk] = self.stack.enter_context(self.nc.semaphore("d%d" % len(self.dsem)))
                    self.dval[sk] = 0
                self.dval[sk] += 16
                val[i] = (self.dsem[sk], self.dval[sk])
        for i, o in enumerate(ops):
            kind, e, fn, reads, writes, semkey = o
            eng = self.eng[e]
            for j in waits[i]:
                s, v = val[j]
                eng.wait_ge(s, v)
            if kind == "c":
                ins = fn()
                if needed[i]:
                    ins.then_inc(val[i][0], 1)
            else:
                out, in_, kw = fn
                eng.dma_start(out=out, in_=in_, **kw).then_inc(val[i][0], 16)
            self.n_instr += 1 + len(waits[i])
        self.barrier()

    def barrier(self):
        sp = self.eng["sp"]
        for sk, s in self.dsem.items():
            if self.dval[sk] > 0:
                sp.wait_ge(s, self.dval[sk])
        for e in ("pe", "act", "dve", "pool"):
            if self.semval[e] > 0:
                sp.wait_ge(self.sem[e], self.semval[e])
        self.semval["sp"] += 1
        sp.nop().then_inc(self.sem["sp"], 1)
        for e in ("pe", "act", "dve", "pool"):
            self.eng[e].wait_ge(self.sem["sp"], self.semval["sp"])


def const_tables(NT, TS, fa, fb, TSA, TSB):
    T = NT * 128
    pos = (np.arange(T) % (TS * 128)).astype(np.float32)
    t = {}
    t["ident"] = np.eye(128, dtype=np.float32).astype(NPBF)
    invA = np.exp(-(np.arange(8, dtype=np.float32) * (2.0 / 16)) * np.float32(math.log(500000.0))).astype(np.float32)
    angA = (pos[:, None] * invA[None, :]).astype(np.float32)
    t["ropeA"] = np.concatenate([np.cos(angA), np.sin(angA)], 1).astype(np.float32)
    invB = np.exp(-(np.arange(128, dtype=np.float32) * (2.0 / 256)) * np.float32(math.log(10000.0))).astype(np.float32)
    angB = (pos[:, None] * invB[None, :]).astype(np.float32)
    t["ropeB"] = np.concatenate([np.cos(angB), np.sin(angB)], 1).astype(np.float32)
    tiles = np.arange(NT)
    fl = np.zeros((2, NT), np.float32)
    fl[0] = (tiles % TS != 0)
    fl[1] = (tiles % TS != TS - 1)
    t["flagsA"] = np.broadcast_to(fl[None], (128, 2, NT)).astype(np.float32).copy()
    kk = np.arange(128)
    mk = np.zeros((128, 2, 128), np.float32)
    mk[:, 0, :] = (kk[:, None] >= kk[None, :])
    mk[:, 1, :] = (kk[:, None] <= kk[None, :])
    t["maskLR"] = mk.astype(NPBF)
    jj = kk[:, None].astype(np.float32)
    ii = kk[None, :].astype(np.float32)
    rt = np.zeros((128, 4, 128), np.float32)
    rt[:, 0, :] = np.where(jj <= ii, ii - jj, 0.0)
    rt[:, 1, :] = np.where(jj > ii, jj - ii, 0.0)
    rt[:, 2, :] = ii + 1.0
    rt[:, 3, :] = 128.0 - ii
    t["retT"] = rt
    rp = np.zeros((128, 2), np.float32)
    rp[:, 0] = 127.0 - kk
    rp[:, 1] = kk
    t["retP"] = rp
    fb_ = np.zeros((2, NT), np.float32)
    fb_[0] = ((tiles + 1) % TS != 0)
    fb_[1] = (tiles % TS != 0)
    t["flagsB"] = np.broadcast_to(fb_[None], (128, 2, NT)).astype(np.float32).copy()
    c = np.arange(512)
    ang = 2 * np.pi * np.outer(c, c) / 512.0
    fc = np.stack([np.cos(ang), -np.sin(ang)], 1) / math.sqrt(512.0)
    t["FC"] = fc.reshape(4, 128, 2, 512).transpose(1, 0, 2, 3).astype(NPBF).copy()
    NS = NT // TS
    n1 = np.arange(NT)
    s1, m1 = n1 // TS, n1 % TS
    q = np.arange(NT)
    sq, k1 = q // TS, q % TS
    n2 = np.arange(128)
    ph = (m1[:, None, None] * k1[None, None, :] / TS) + (n2[None, :, None] * k1[None, None, :] / (128.0 * TS))
    Gc = np.exp(-2j * np.pi * ph) * (s1[:, None, None] == sq[None, None, :])
    G = np.stack([Gc.real, Gc.imag, -Gc.imag], 2)
    t["G"] = G.astype(NPBF)
    R = 128 // TS
    k2 = np.arange(128)
    sig = TS * (k2 % R) + k2 // R
    Hc = np.exp(-2j * np.pi * np.outer(n2, k2) / 128.0) / math.sqrt(128.0 * TS)
    H = np.zeros((128, 2, 128), np.float64)
    H[:, 0, sig] = Hc.real
    H[:, 1, sig] = -Hc.imag
    t["H"] = H.astype(NPBF)
    pi = np.arange(128)
    P = np.zeros((128, 2, 128), np.float32)
    for v, (f, ts) in enumerate(((fa, TSA), (fb, TSB))):
        r = 128 // ts
        w = ts * (pi % r) + pi // r
        P[pi, v, w] = f
    t["PAB"] = P.astype(NPBF)
    return t


TABLE_SPECS = lambda NT: {
    "ident": ([128, 128], BF16), "ropeA": ([NT * 128, 16], F32), "ropeB": ([NT * 128, 256], F32),
    "flagsA": ([128, 2, NT], F32), "maskLR": ([128, 2, 128], BF16), "retT": ([128, 4, 128], F32),
    "retP": ([128, 2], F32), "flagsB": ([128, 2, NT], F32), "FC": ([128, 4, 2, 512], BF16),
    "G": ([NT, 128, 3, NT], BF16), "H": ([128, 2, 128], BF16), "PAB": ([128, 2, 128], BF16),
}


def build(NT, TSA, TSB, layers=LAYERS, final_norm=True, debug=False, bstop=0):
    T = NT * 128
    nc = bass.Bass("TRN2", target_bir_lowering=False)
    din = lambda name, shape, dt=F32: nc.dram_tensor(name, shape, dt, kind="ExternalInput").ap()
    x_in = din("x", [T, D])
    norm_g = din("norm_g", [4, D])
    fin_g = din("final_norm_g", [1, D])
    a_w_in = din("a_w_in", [2, D, A_IN])
    a_sink = din("a_sink", [2, 32])
    a_w_out = din("a_w_out", [2, E, D])
    b_w_in = din("b_w_in", [1, D, B_IN])
    b_decay = din("b_decay", [1, 8])
    b_w_out = din("b_w_out", [1, E, D])
    c_w_in = din("c_w_in", [1, D, C_IN])
    c_w_out = din("c_w_out", [1, E, D])
    tb = {k: din(k, sh, dt) for k, (sh, dt) in TABLE_SPECS(NT).items()}
    y_out = nc.dram_tensor("y", [T, D], F32, kind="ExternalOutput").ap()
    dscr = lambda name, shape, dt: nc.dram_tensor(name, shape, dt, kind="Internal").ap()
    xscr = dscr("xscr", [T, D], F32)
    KT = dscr("KT", [NT, 128, 1024], BF16)
    VA = dscr("VA", [NT, 128, 1024], BF16)
    SG = dscr("SG", [NT, 128, 2048], BF16)
    KB = dscr("KBs", [NT, 128, 1024], BF16)
    VB = dscr("VBs", [NT, 128, 2048], BF16)
    Sscr = dscr("Sscr", [NT, 128, 4096], BF16)
    Zscr = dscr("Zscr", [T, 4096], BF16)
    Bscr = dscr("Bscr", [NT, 128, 4096], BF16)
    Mscr = dscr("Mscr", [T, E], BF16)

    with ExitStack() as top:
        em = Em(nc, top)
        V, S, G, P = nc.vector, nc.scalar, nc.gpsimd, nc.tensor

        uid = [0]

        def sbuf(st, name, shape, dt):
            uid[0] += 1
            return st.enter_context(nc.sbuf_tensor("sb%d_%s" % (uid[0], name), shape, dt))

        def psum(st, name, shape, dt):
            uid[0] += 1
            return st.enter_context(nc.psum_tensor("ps%d_%s" % (uid[0], name), shape, dt))

        ident = sbuf(top, "ident", [128, 128], BF16)
        gl = sbuf(top, "gl", [128, D], F32)
        gfin_box = [None]
        em.dma(ident[:], tb["ident"], writes=[("ident",)], semkey="c0")
        em.flush()

        def load_weights(st, Wd, ncols, name):
            K = Wd.shape[0]
            Wb = sbuf(st, name, [128, K // 128, ncols], BF16)
            with ExitStack() as st2:
                stg = [sbuf(st2, "stg%d" % i, [128, ncols], F32) for i in range(2)]
                h = ncols // 2
                for kc in range(K // 128):
                    s = kc % 2
                    em.dma(stg[s][:], Wd[kc * 128:(kc + 1) * 128, :], writes=[("stg", s)], semkey=("stg", s))
                    em.op("act", lambda kc=kc, s=s: S.copy(out=Wb[:, kc, 0:h], in_=stg[s][:, 0:h]),
                          reads=[("stg", s)], writes=[(name, kc, 0)])
                    em.op("dve", lambda kc=kc, s=s: V.tensor_copy(out=Wb[:, kc, h:ncols], in_=stg[s][:, h:ncols]),
                          reads=[("stg", s)], writes=[(name, kc, 1)])
                em.flush()
            return Wb

        def load_gain(li, st=None, last=False):
            em.dma(gl[:], norm_g[li:li + 1, :].partition_broadcast(128), writes=[("gl",)], semkey="gl")
            if last:
                gfin_box[0] = sbuf(st, "gfin", [128, D], F32)
                em.dma(gfin_box[0][:], fin_g.partition_broadcast(128), writes=[("gfin",)], semkey="c2")

        class Work:
            pass

        def alloc_common(st, w, nslot=2):
            w.nslot = nslot
            w.xt = [sbuf(st, "xt%d" % i, [128, D], F32) for i in range(nslot)]
            w.hb = sbuf(st, "hb", [128, D], BF16)
            w.junk = w.hb
            w.st = [sbuf(st, "stat%d" % i, [128, 4], F32) for i in range(nslot)]
            w.hT = [sbuf(st, "hT%d" % i, [128, 8, 128], BF16) for i in range(min(nslot, 2))]
            w.tp = psum(st, "tp", [128, 16, 128], BF16)

        def load_x(w, t, src):
            s = t % w.nslot
            em.dma(w.xt[s][:], src[t * 128:(t + 1) * 128, :], writes=[("xt", s)], semkey=("xt", s))

        def prologue(w, t, li, src, hdst=None, hkey=None, preloaded=False):
            s = t % w.nslot
            xt, stt = w.xt[s], w.st[s]
            sh = s % len(w.hT)
            hT = w.hT[sh][:] if hdst is None else hdst
            hkey = ("hT", sh) if hkey is None else hkey
            if not preloaded:
                em.dma(xt[:], src[t * 128:(t + 1) * 128, :], writes=[("xt", s)], semkey=("xt", s))
            em.op("act", lambda: S.activation(out=w.junk[:], in_=xt[:], func=AF.Square, accum_out=stt[:, 0:1]),
                  reads=[("xt", s)], writes=[("hb",), ("st", s)])
            em.op("dve", lambda: V.tensor_scalar(out=stt[:, 1:2], in0=stt[:, 0:1], scalar1=1.0 / D, scalar2=EPS,
                                                 op0=ALU.mult, op1=ALU.add), reads=[("st", s)], writes=[("st", s)])
            em.op("act", lambda: S.activation(out=stt[:, 2:3], in_=stt[:, 1:2], func=AF.Ln),
                  reads=[("st", s)], writes=[("st", s)])
            em.op("act", lambda: S.activation(out=stt[:, 3:4], in_=stt[:, 2:3], func=AF.Exp, scale=-0.5),
                  reads=[("st", s)], writes=[("st", s)])
            em.op("dve", lambda: V.scalar_tensor_tensor(out=w.hb[:], in0=xt[:], scalar=stt[:, 3:4], in1=gl[:],
                                                        op0=ALU.mult, op1=ALU.mult),
                  reads=[("xt", s), ("st", s)], writes=[("hb",)])
            split_at[0] = len(em.ops)
            for c in range(8):
                em.op("pe", lambda c=c: P.transpose(out=w.tp[:, c, :], in_=w.hb[:, c * 128:(c + 1) * 128], identity=ident[:]),
                      reads=[("hb",)], writes=[("tp",)])
            em.op("dve", lambda: V.tensor_copy(out=hT[:, 0:4, :], in_=w.tp[:, 0:4, :]), reads=[("tp",)], writes=[hkey])
            em.op("dve", lambda: V.tensor_copy(out=hT[:, 4:8, :], in_=w.tp[:, 4:8, :]), reads=[("tp",)], writes=[hkey])
            return s

        def epilogue(w, t, s, gatedT, gkey, Wo, mm, dst, last):
            xt, stt = w.xt[s], w.st[s]
            def gk(m):
                if gkey == "per_group":
                    return [("gatedT", m // 4)]
                return list(gkey) if isinstance(gkey, list) else [gkey]
            for (m0, m1) in ((0, 12), (12, 16)):
                for dh in range(2):
                    bank, bk = mm[dh]
                    for m in range(m0, m1):
                        em.op("pe", lambda m=m, dh=dh, bank=bank: P.matmul(bank[:], lhsT=gatedT[:, m, :],
                                                                          rhs=Wo[:, m, dh * 512:(dh + 1) * 512],
                                                                          start=(m == 0), stop=(m == 15)),
                              reads=gk(m), writes=[bk])
            for dh in range(2):
                bank, bk = mm[dh]
                em.op("dve", lambda dh=dh, bank=bank: V.tensor_tensor(out=xt[:, dh * 512:(dh + 1) * 512], in0=bank[:],
                                                                     in1=xt[:, dh * 512:(dh + 1) * 512], op=ALU.add),
                      reads=[bk, ("xt", s)], writes=[("xt", s)])
            if last:
                em.op("act", lambda: S.activation(out=w.junk[:], in_=xt[:], func=AF.Square, accum_out=stt[:, 0:1]),
                      reads=[("xt", s)], writes=[("hb",) if w.junk is w.hb else ("junk",), ("st", s)])
                em.op("dve", lambda: V.tensor_scalar(out=stt[:, 1:2], in0=stt[:, 0:1], scalar1=1.0 / D, scalar2=EPS,
                                                     op0=ALU.mult, op1=ALU.add), reads=[("st", s)], writes=[("st", s)])
                em.op("act", lambda: S.activation(out=stt[:, 2:3], in_=stt[:, 1:2], func=AF.Ln),
                      reads=[("st", s)], writes=[("st", s)])
                em.op("act", lambda: S.activation(out=stt[:, 3:4], in_=stt[:, 2:3], func=AF.Exp, scale=-0.5),
                      reads=[("st", s)], writes=[("st", s)])
                em.op("dve", lambda: V.scalar_tensor_tensor(out=xt[:], in0=xt[:], scalar=stt[:, 3:4], in1=gfin_box[0][:],
                                                            op0=ALU.mult, op1=ALU.mult),
                      reads=[("xt", s), ("st", s)], writes=[("xt", s)])
            em.dma(dst[t * 128:(t + 1) * 128, :], xt[:], reads=[("xt", s)], writes=[("D_x", t)], semkey=("xst", s))

        def dump(name, ap, key):
            if not debug:
                return
            d = nc.dram_tensor("dbg_" + name, list(ap.shape), ap.dtype, kind="ExternalOutput").ap()
            em.dma(d, ap, reads=[key], writes=[("D_dbg", name)], semkey=("dbg", name))

        split_at = [0]

        def capture(fn, *args):
            saved = em.ops
            em.ops = []
            fn(*args)
            out = em.ops
            em.ops = saved
            return out

        def merge(a, b):
            out, i, j = [], 0, 0
            na, nb = len(a), len(b)
            while i < na or j < nb:
                if j >= nb or (i < na and i * nb <= j * na):
                    out.append(a[i]); i += 1
                else:
                    out.append(b[j]); j += 1
            return out

        def pipeline(stage1, stage2, n, prefetch=None):
            if prefetch is not None:
                prefetch(0)
                if n > 1:
                    prefetch(1)
            prev = capture(stage1, 0)
            em.ops.extend(prev)
            for t in range(n):
                if prefetch is not None and t + 2 < n:
                    prefetch(t + 2)
                s2 = capture(stage2, t)
                if t + 1 < n:
                    s1 = capture(stage1, t + 1)
                    k = split_at[0]
                    em.ops.extend(s1[:k])
                    em.ops.extend(merge(s1[k:], s2))
                else:
                    em.ops.extend(s2)

        def bc_mid(ap, n):
            return ap.unsqueeze(1).broadcast_to([ap.shape[0], n, ap.shape[1]])

        def layer_A(li, j, src, dst, last):
            with ExitStack() as st:
                load_gain(li, st, last)
                Wi = load_weights(st, a_w_in[j], A_IN, "Wi")
                Wo = load_weights(st, a_w_out[j], D, "Wo")
                with ExitStack() as s1:
                    w = Work()
                    alloc_common(s1, w)
                    kvfs = [sbuf(s1, "kvf%d" % i, [128, 512], F32) for i in range(2)]
                    cs = [sbuf(s1, "csA%d" % i, [128, 16], F32) for i in range(2)]
                    tmp = sbuf(s1, "ropetmp", [128, 4, 4, 8], F32)
                    ktz = [sbuf(s1, "ktz%d" % i, [128, 8, 128], BF16) for i in range(2)]
                    vau = [sbuf(s1, "vau%d" % i, [128, 8, 128], BF16) for i in range(2)]
                    kTs = [sbuf(s1, "kTs%d" % i, [128, 8, 128], BF16) for i in range(2)]
                    mmb = psum(s1, "mmA1", [128, 512], F32)
                    mmg = [(psum(s1, "mmA1g%d" % i, [128, 512], F32), ("mmg", i)) for i in range(2)]
                    sgs = [sbuf(s1, "sgs%d" % i, [128, 16, 512], BF16) for i in range(2)]
                    hT4 = [sbuf(s1, "hT4_%d" % i, [128, 8, 512], BF16) for i in range(2)]
                    for i in range(2):
                        em.op("pool", lambda i=i: G.memset(ktz[i][:], 0.0), writes=[("ktz", i)])
                        em.op("pool", lambda i=i: G.memset(vau[i][:], 1.0), writes=[("vau", i)])
                    def stage1(b):
                        sb, tl = b // 4, b % 4
                        ss = sb % 2
                        hTv, hk = hT4[ss][:, :, tl * 128:(tl + 1) * 128], ("hT4", ss, tl)
                        s = prologue(w, b, li, src, hTv, hk)
                        em.dma(cs[s][:], tb["ropeA"][b * 128:(b + 1) * 128, :], writes=[("cs", s)], semkey=("cs", s))
                        for kc in range(8):
                            em.op("pe", lambda kc=kc, s=s: P.matmul(mmb[:], lhsT=hTv[:, kc, :], rhs=Wi[:, kc, 2048:2560],
                                                                   start=(kc == 0), stop=(kc == 7)),
                                  reads=[hk], writes=[("mmb",)])
                        em.op("act", lambda: S.copy(out=kvfs[s][:], in_=mmb[:]), reads=[("mmb",)], writes=[("kvf", s)])

                    def stage2(b):
                        s = b % 2
                        kvf = kvfs[s]
                        kv4 = kvf[:, 0:256].rearrange("p (g d) -> p g d", g=4)
                        x1, x2 = kv4[:, :, 0:8], kv4[:, :, 8:16]
                        cosb, sinb = bc_mid(cs[s][:, 0:8], 4), bc_mid(cs[s][:, 8:16], 4)
                        for ti, (xa, cb_) in enumerate(((x1, cosb), (x2, sinb), (x2, cosb), (x1, sinb))):
                            em.op("pool", lambda ti=ti, xa=xa, cb_=cb_: G.tensor_tensor(out=tmp[:, ti], in0=xa, in1=cb_, op=ALU.mult),
                                  reads=[("kvf", s), ("cs", s)], writes=[("tmp", ti)])
                        kz = ktz[s][:].rearrange("p (g r) c -> p g r c", r=2)
                        va = vau[s][:].rearrange("p (g r) c -> p g r c", r=2)
                        for par in range(2):
                            o = par * 64
                            em.op("pool", lambda par=par, o=o, kz=kz: G.tensor_tensor(out=kz[:, :, par, o:o + 8], in0=tmp[:, 0], in1=tmp[:, 1],
                                                                          op=ALU.subtract),
                                  reads=[("tmp", 0), ("tmp", 1)], writes=[("ktz", s)])
                            em.op("pool", lambda par=par, o=o, kz=kz: G.tensor_tensor(out=kz[:, :, par, o + 8:o + 16], in0=tmp[:, 2], in1=tmp[:, 3],
                                                                          op=ALU.add),
                                  reads=[("tmp", 2), ("tmp", 3)], writes=[("ktz", s)])
                            em.op("dve", lambda par=par, o=o, kz=kz: V.tensor_copy(out=kz[:, :, par, o + 16:o + 64], in_=kv4[:, :, 16:64]),
                                  reads=[("kvf", s)], writes=[("ktz", s)])
                            em.op("dve", lambda par=par, o=o, va=va: V.tensor_copy(
                                out=va[:, :, par, o:o + 64], in_=kvf[:, 256:512].rearrange("p (g d) -> p g d", g=4)),
                                  reads=[("kvf", s)], writes=[("vau", s)])
                        for c in range(8):
                            em.op("pe", lambda c=c, s=s: P.transpose(out=w.tp[:, 8 + c, :], in_=ktz[s][:, c, :], identity=ident[:]),
                                  reads=[("ktz", s)], writes=[("tp2",)])
                        em.op("act", lambda s=s: S.copy(out=kTs[s][:], in_=w.tp[:, 8:16, :]), reads=[("tp2",)], writes=[("kTs", s)])
                        em.dma(KT[b].rearrange("p (c t) -> p c t", c=8), kTs[s][:], reads=[("kTs", s)], writes=[("D_KT", b)],
                               semkey=("kTst", s))
                        em.dma(VA[b].rearrange("p (c t) -> p c t", c=8), vau[s][:], reads=[("vau", s)], writes=[("D_VA", b)],
                               semkey=("vast", s))
                        if b % 4 == 3:
                            gate4(b // 4, (b // 4) % 2)

                    def gate4(sb, ss):
                        hks = [("hT4", ss, tl) for tl in range(4)]
                        for m in range(16):
                            bank, bk = mmg[m % 2]
                            for kc in range(8):
                                em.op("pe", lambda kc=kc, m=m, bank=bank: P.matmul(
                                    bank[:], lhsT=Wi[:, kc, 2560 + m * 128:2560 + (m + 1) * 128], rhs=hT4[ss][:, kc, :],
                                    start=(kc == 0), stop=(kc == 7)), reads=hks, writes=[bk])
                            em.op("act", lambda m=m, bank=bank: S.activation(out=sgs[ss][:, m, :], in_=bank[:], func=AF.Silu),
                                  reads=[bk], writes=[("sgs", ss)])
                        for tl in range(4):
                            em.dma(SG[4 * sb + tl].rearrange("p (c t) -> p c t", c=16), sgs[ss][:, :, tl * 128:(tl + 1) * 128],
                                   reads=[("sgs", ss)], writes=[("D_SG", sb, tl)], semkey=("sgst", ss))

                    pipeline(stage1, stage2, NT)
                    em.flush()
                with ExitStack() as s2:
                    w = Work()
                    alloc_common(s2, w, 3)
                    if last:
                        w.junk = sbuf(s2, "junkA", [128, D], BF16)
                    sk = sbuf(s2, "sinkbc", [128, 32], F32)
                    sinksm = sbuf(s2, "sinksm", [128, 8, 4], F32)
                    flags = sbuf(s2, "flagsA", [128, 2, NT], F32)
                    mask = sbuf(s2, "maskLR", [128, 2, 128], BF16)
                    maskb = [sbuf(s2, "maskb%d" % i, [128, 2, 128], BF16) for i in range(2)]
                    cs = [sbuf(s2, "csA%d" % i, [128, 16], F32) for i in range(2)]
                    kTl = [[sbuf(s2, "kTl%d_%d" % (i, jj), [128, 8, 128], BF16) for jj in range(3)] for i in range(2)]
                    val = [[sbuf(s2, "val%d_%d" % (i, jj), [128, 8, 128], BF16) for jj in range(3)] for i in range(2)]
                    qf = sbuf(s2, "qf", [128, 32, 16], F32)
                    tmp = sbuf(s2, "ropetmpq", [128, 4, 32, 8], F32)
                    qb = sbuf(s2, "qb", [128, E], BF16)
                    qT = [sbuf(s2, "qT%d" % i, [128, 16, 128], BF16) for i in range(2)]
                    sgT = [sbuf(s2, "sgT%d" % i, [128, 16, 128], BF16) for i in range(2)]
                    pex = [sbuf(s2, "pex%d" % i, [128, 4, 128], BF16) for i in range(10)]
                    rec = [sbuf(s2, "rec%d" % i, [128, 512], F32) for i in range(2)]
                    onrm = rec
                    gatedT = sbuf(s2, "gatedT", [128, 16, 128], BF16)
                    prow = sbuf(s2, "prow", [128, 8, 4], BF16)
                    sel = sbuf(s2, "sel", [128, 2, 128], BF16)
                    mm = [(psum(s2, "mmA%d" % i, [128, 512], F32), ("mm", i)) for i in range(6)]
                    em.dma(sk[:], a_sink[j:j + 1, :].partition_broadcast(128), writes=[("sk",)], semkey="sk")
                    em.dma(flags[:], tb["flagsA"], writes=[("flags",)], semkey="fl")
                    em.dma(mask[:], tb["maskLR"], writes=[("mask",)], semkey="mk")
                    em.op("act", lambda: S.activation(out=sk[:], in_=sk[:], func=AF.Exp), reads=[("sk",)], writes=[("sk",)])
                    em.op("pool", lambda: G.tensor_copy(out=sinksm[:].rearrange("p (g r) c -> p g r c", r=2),
                                                        in_=sk[:].rearrange("p (g c r) -> p g r c", g=4, c=4, r=2)),
                          reads=[("sk",)], writes=[("sinksm",)])
                    em.op("pool", lambda: G.memset(prow[:], 0.0), writes=[("prow",)])
                    em.op("pool", lambda: G.memset(sel[:], 0.0), writes=[("sel",)])
                    em.op("pool", lambda: G.memset(sel[:, 0, 64:128], 1.0), reads=[("sel",)], writes=[("sel",)])
                    em.op("pool", lambda: G.memset(sel[:, 1, 0:64], 1.0), reads=[("sel",)], writes=[("sel",)])
                    for gp in range(8):
                        em.op("pool", lambda gp=gp: G.tensor_copy(out=prow[0:1, gp, :], in_=sinksm[0:1, gp, :]),
                              reads=[("prow",), ("sinksm",)], writes=[("prow",)])
                    pexi = [0]

                    def stage1(b):
                        sx = prologue(w, b, li, src, preloaded=True)
                        s = b % 2
                        em.dma(cs[s][:], tb["ropeA"][b * 128:(b + 1) * 128, :], writes=[("cs", s)], semkey=("cs", s))
                        js = [jj for jj in range(3) if 0 <= b - 1 + jj < NT]
                        for side in range(2):
                            em.op("pool", lambda side=side: G.tensor_scalar(out=maskb[s][:, side, :], in0=mask[:, side, :],
                                                                            scalar1=flags[:, side, b:b + 1], scalar2=None, op0=ALU.mult),
                                  reads=[("mask",), ("flags",)], writes=[("maskb", s, side)])
                        for jj in js:
                            bb = b - 1 + jj
                            em.dma(kTl[s][jj][:], KT[bb].rearrange("p (c t) -> p c t", c=8), writes=[("kTl", s, jj)],
                                   semkey=("kTl", s, jj))
                            em.dma(val[s][jj][:], VA[bb].rearrange("p (c t) -> p c t", c=8), writes=[("val", s, jj)],
                                   semkey=("val", s, jj))
                        for cb in range(4):
                            bank, bk = mm[0]
                            for kc in range(8):
                                em.op("pe", lambda kc=kc, cb=cb, bank=bank: P.matmul(bank[:], lhsT=w.hT[sx % 2][:, kc, :],
                                                                                   rhs=Wi[:, kc, cb * 512:(cb + 1) * 512],
                                                                                   start=(kc == 0), stop=(kc == 7)),
                                      reads=[("hT", sx % 2)], writes=[bk])
                            em.op("dve", lambda cb=cb, bank=bank: V.tensor_scalar(out=qb[:, cb * 512:(cb + 1) * 512], in0=bank[:],
                                                                                 scalar1=0.125, scalar2=None, op0=ALU.mult),
                                  reads=[bk], writes=[("qb", cb)])
                            em.op("dve", lambda cb=cb, bank=bank: V.tensor_scalar(
                                out=qf[:, cb * 8:(cb + 1) * 8, :], in0=bank[:].rearrange("p (h d) -> p h d", h=8)[:, :, 0:16],
                                scalar1=0.125, scalar2=None, op0=ALU.mult), reads=[bk], writes=[("qf", cb)])
                        qb3 = qb[:].rearrange("p (h d) -> p h d", h=32)
                        x1, x2 = qf[:, :, 0:8], qf[:, :, 8:16]
                        cosb, sinb = bc_mid(cs[s][:, 0:8], 32), bc_mid(cs[s][:, 8:16], 32)
                        qfk = [("qf", cb) for cb in range(4)]
                        qbk = [("qb", cb) for cb in range(4)]
                        for ti, (xa, cb_) in enumerate(((x1, cosb), (x2, sinb), (x2, cosb), (x1, sinb))):
                            em.op("dve", lambda ti=ti, xa=xa, cb_=cb_: V.tensor_tensor(out=tmp[:, ti], in0=xa, in1=cb_, op=ALU.mult),
                                  reads=qfk + [("cs", s)], writes=[("tmpq", ti)])
                        em.op("dve", lambda: V.tensor_tensor(out=qb3[:, :, 0:8], in0=tmp[:, 0], in1=tmp[:, 1], op=ALU.subtract),
                              reads=[("tmpq", 0), ("tmpq", 1)], writes=qbk)
                        em.op("dve", lambda: V.tensor_tensor(out=qb3[:, :, 8:16], in0=tmp[:, 2], in1=tmp[:, 3], op=ALU.add),
                              reads=[("tmpq", 2), ("tmpq", 3)], writes=qbk)
                        for m in range(16):
                            em.op("pe", lambda m=m: P.transpose(out=w.tp[:, m, :], in_=qb[:, m * 128:(m + 1) * 128], identity=ident[:]),
                                  reads=[("qb", m // 4)], writes=[("tp" if m < 8 else "tp2",)])
                        em.op("dve", lambda: V.tensor_copy(out=qT[s][:, 0:8, :], in_=w.tp[:, 0:8, :]), reads=[("tp",)], writes=[("qT", s, 0)])
                        em.op("dve", lambda: V.tensor_copy(out=qT[s][:, 8:16, :], in_=w.tp[:, 8:16, :]), reads=[("tp2",)],
                              writes=[("qT", s, 1)])
                        em.dma(sgT[s][:], SG[b].rearrange("p (c t) -> p c t", c=16), writes=[("sgT", s)], semkey=("sgT", s))

                    def stage2(b):
                        s = b % 2
                        js = [jj for jj in range(3) if 0 <= b - 1 + jj < NT]
                        tiles = {}

                        def scores(gp):
                            g = gp // 2
                            pl = []
                            for jj in js:
                                bank, bk = mm[1 + (pexi[0] % 3)]
                                px = pex[pexi[0] % len(pex)]
                                pk = ("pex", pexi[0] % len(pex))
                                pexi[0] += 1
                                em.op("pe", lambda bank=bank, jj=jj: P.matmul(
                                    bank[:], lhsT=kTl[s][jj][:, gp, :], rhs=qT[s][:, 4 * g:4 * g + 4, :], start=True, stop=True),
                                      reads=[("kTl", s, jj), ("qT", s, g // 2)], writes=[bk])
                                em.op("act", lambda bank=bank, px=px: S.activation(out=px[:].rearrange("p c t -> p (c t)"), in_=bank[:],
                                                                                    func=AF.Exp),
                                      reads=[bk], writes=[pk])
                                if jj != 1:
                                    side = 0 if jj == 0 else 1
                                    if side == 0:
                                        em.op("dve", lambda px=px, side=side: V.tensor_tensor(
                                            out=px[:], in0=px[:], in1=bc_mid(maskb[s][:, side, :], 4), op=ALU.mult),
                                              reads=[pk, ("maskb", s, side)], writes=[pk])
                                    else:
                                        em.op("pool", lambda px=px, side=side: G.tensor_tensor(
                                            out=px[:], in0=px[:], in1=bc_mid(maskb[s][:, side, :], 4), op=ALU.mult),
                                              reads=[pk, ("maskb", s, side)], writes=[pk])
                                pl.append((jj, px, pk))
                            tiles[gp] = pl

                        def pv_norm(gp):
                            g, par = gp // 2, gp % 2
                            pl = tiles[gp]
                            bank, bk = mm[4 + (gp % 2)]
                            for n_, (jj, px, pk) in enumerate(pl):
                                em.op("pe", lambda bank=bank, jj=jj, px=px, n_=n_: P.matmul(
                                    bank[:], lhsT=val[s][jj][:, gp, :], rhs=px[:].rearrange("p c t -> p (c t)"),
                                    start=(n_ == 0), stop=False),
                                      reads=[("val", s, jj), pk], writes=[bk])
                            em.op("pe", lambda bank=bank: P.matmul(bank[:], lhsT=sel[:, par, :],
                                                                   rhs=prow[:, gp, :].unsqueeze(2).broadcast_to([128, 4, 128]),
                                                                   start=False, stop=True), writes=[bk])
                            nr = slice(par * 64, par * 64 + 64)
                            dr = slice((1 - par) * 64, (1 - par) * 64 + 64)
                            rc, on = rec[gp % 2], onrm[gp % 2]
                            rk, ok_ = ("rec", gp % 2), ("onrm", gp % 2)
                            em.op("act", lambda: S.activation(out=rc[dr, :], in_=bank[dr, :], func=AF.Ln), reads=[bk], writes=[rk])
                            em.op("act", lambda: S.activation(out=rc[dr, :], in_=rc[dr, :], func=AF.Exp, scale=-1.0),
                                  reads=[rk], writes=[rk])
                            em.op("dve", lambda: V.tensor_tensor(out=on[nr, :], in0=bank[nr, :], in1=rc[dr, :], op=ALU.mult),
                                  reads=[bk, rk], writes=[ok_])
                            em.op("pool", lambda: G.tensor_tensor(
                                out=gatedT[nr, 4 * g:4 * g + 4, :].rearrange("p c t -> p (c t)"), in0=on[nr, :],
                                in1=sgT[s][nr, 4 * g:4 * g + 4, :].rearrange("p c t -> p (c t)"), op=ALU.mult),
                                  reads=[ok_, ("sgT", s)], writes=[("gatedT", g)])

                        LAG = 2
                        for gp in range(8 + LAG):
                            if gp < 8:
                                scores(gp)
                            if gp >= LAG:
                                pv_norm(gp - LAG)
                        epilogue(w, b, b % 3, gatedT, "per_group", Wo, [mm[4], mm[5]], dst, last)

                    pipeline(stage1, stage2, NT, prefetch=lambda t: load_x(w, t, src))
                    em.flush()

        def layer_B(li, j, src, dst, last):
            with ExitStack() as st:
                load_gain(li, st, last)
                lg = sbuf(st, "lg", [128, 8], F32)
                kd16 = sbuf(st, "kd16", [128, 8], F32)
                cdt = sbuf(st, "cdt", [128, 8], F32)
                Dm = sbuf(st, "Dm", [128, 4, 128], F32)
                qdf = sbuf(st, "qdf", [128, 4, 128], F32)
                qdb = sbuf(st, "qdb", [128, 4, 128], F32)
                flg = sbuf(st, "flagsB", [128, 2, NT], F32)
                Sm = sbuf(st, "Sm", [128, 8, 512], F32)
                Sbf = sbuf(st, "Sbf", [128, 2, 2, 512], BF16)
                with ExitStack() as s0:
                    retT = sbuf(s0, "retT", [128, 4, 128], F32)
                    retP = sbuf(s0, "retP", [128, 2], F32)
                    t1 = sbuf(s0, "rt1", [128, 128], F32)
                    em.dma(lg[:], b_decay[j:j + 1, :].partition_broadcast(128), writes=[("lg",)], semkey="lg")
                    em.dma(retT[:], tb["retT"], writes=[("retT",)], semkey="retT")
                    em.dma(retP[:], tb["retP"], writes=[("retP",)], semkey="retP")
                    em.dma(flg[:], tb["flagsB"], writes=[("flg",)], semkey="flg")
                    em.op("act", lambda: S.activation(out=lg[:], in_=lg[:], func=AF.Exp, scale=-1.0), reads=[("lg",)], writes=[("lg",)])
                    em.op("dve", lambda: V.tensor_scalar(out=lg[:], in0=lg[:], scalar1=1.0, scalar2=None, op0=ALU.add),
                          reads=[("lg",)], writes=[("lg",)])
                    em.op("act", lambda: S.activation(out=lg[:], in_=lg[:], func=AF.Ln), reads=[("lg",)], writes=[("lg",)])
                    em.op("dve", lambda: V.tensor_scalar(out=lg[:], in0=lg[:], scalar1=-1.0, scalar2=None, op0=ALU.mult),
                          reads=[("lg",)], writes=[("lg",)])
                    em.op("act", lambda: S.activation(out=kd16[:, 0:4], in_=lg[:, 0:4], func=AF.Exp, scale=retP[:, 0:1]),
                          reads=[("lg",), ("retP",)], writes=[("kd16", 0)])
                    em.op("act", lambda: S.activation(out=kd16[:, 4:8], in_=lg[:, 4:8], func=AF.Exp, scale=retP[:, 1:2]),
                          reads=[("lg",), ("retP",)], writes=[("kd16", 1)])
                    em.op("dve", lambda: V.tensor_scalar(out=kd16[:], in0=kd16[:], scalar1=1.0 / 16, scalar2=None, op0=ALU.mult),
                          reads=[("kd16", 0), ("kd16", 1)], writes=[("kd16", 2)])
                    em.op("act", lambda: S.activation(out=cdt[:], in_=lg[:], func=AF.Exp, scale=128.0), reads=[("lg",)], writes=[("cdt",)])
                    for h in range(4):
                        em.op("act", lambda h=h: S.activation(out=qdf[:, h, :], in_=retT[:, 2, :], func=AF.Exp, scale=lg[:, h:h + 1]),
                              reads=[("lg",), ("retT",)], writes=[("qdf", h)])
                        em.op("act", lambda h=h: S.activation(out=qdb[:, h, :], in_=retT[:, 3, :], func=AF.Exp, scale=lg[:, 4 + h:5 + h]),
                              reads=[("lg",), ("retT",)], writes=[("qdb", h)])
                        em.op("dve", lambda h=h: V.tensor_scalar(out=t1[:], in0=retT[:, 0, :], scalar1=lg[:, h:h + 1], scalar2=None,
                                                                 op0=ALU.mult), reads=[("lg",), ("retT",)], writes=[("rt1",)])
                        em.op("dve", lambda h=h: V.scalar_tensor_tensor(out=t1[:], in0=retT[:, 1, :], scalar=lg[:, 4 + h:5 + h], in1=t1[:],
                                                                        op0=ALU.mult, op1=ALU.add),
                              reads=[("lg",), ("retT",), ("rt1",)], writes=[("rt1",)])
                        em.op("act", lambda h=h: S.activation(out=Dm[:, h, :], in_=t1[:], func=AF.Exp), reads=[("rt1",)], writes=[("Dm", h)])
                        em.op("dve", lambda h=h: V.tensor_scalar(out=Dm[:, h, :], in0=Dm[:, h, :], scalar1=1.0 / 16, scalar2=None,
                                                                 op0=ALU.mult), reads=[("Dm", h)], writes=[("Dm", h)])
                    em.op("pool", lambda: G.memset(Sm[:], 0.0), writes=[("Sm", i) for i in range(8)])
                    em.flush()

                def rope_half(x3, csb, nh, ta, tb_, tc, td):
                    x1, x2 = x3[:, :, 0:128], x3[:, :, 128:256]
                    cosb, sinb = bc_mid(csb[:, 0:128], nh), bc_mid(csb[:, 128:256], nh)
                    return x1, x2, cosb, sinb

                def state_update(kk, vb, direction, c, cdk, mmu):
                    for h in range(4):
                        for dc in range(2):
                            i = h * 2 + dc
                            bank, bk = mmu[i % len(mmu)]
                            em.op("pe", lambda h=h, dc=dc, bank=bank: P.matmul(bank[:], lhsT=kk[:, h, dc * 128:(dc + 1) * 128],
                                                                              rhs=vb[:, h * 512:(h + 1) * 512], start=True, stop=True),
                                  reads=[("ksc",), ("vbf",)], writes=[bk])
                            em.op("dve", lambda h=h, i=i, bank=bank: V.scalar_tensor_tensor(out=Sm[:, i, :], in0=Sm[:, i, :],
                                                                                        scalar=cdk[:, h:h + 1], in1=bank[:],
                                                                                        op0=ALU.mult, op1=ALU.add),
                                  reads=[bk, ("Sm", i), ("cdk",)], writes=[("Sm", i)])

                if bstop == 1:
                    return
                with ExitStack() as s1:
                    Wi = load_weights(s1, b_w_in[j][:, 1024:6144], 5120, "WiB1")
                    w = Work()
                    alloc_common(s1, w, 2)
                    cs = [sbuf(s1, "csB%d" % i, [128, 256], F32) for i in range(2)]
                    kf = [sbuf(s1, "kf%d" % i, [128, 4, 256], F32) for i in range(2)]
                    tt = [sbuf(s1, "rtmp%d" % i, [128, 4, 128], F32) for i in range(4)]
                    ksc = sbuf(s1, "ksc", [128, 4, 256], BF16)
                    vbf = [sbuf(s1, "vbf%d" % i, [128, E], BF16) for i in range(2)]
                    sgb = [sbuf(s1, "sgb%d" % i, [128, E], BF16) for i in range(2)]
                    kbs = sbuf(s1, "kbs", [128, 4, 256], BF16)
                    sck = sbuf(s1, "sck", [128, 8], F32)
                    mm = [(psum(s1, "mmB%d" % i, [128, 512], F32), ("mm", i)) for i in range(6)]

                    def stage1(t):
                        c = NT - 1 - t
                        s = t % 2
                        sx = prologue(w, c, li, src)
                        hTc, hk = w.hT[sx], ("hT", sx)
                        em.dma(cs[s][:], tb["ropeB"][c * 128:(c + 1) * 128, :], writes=[("cs", s)], semkey=("cs", s))
                        nb = [0]

                        def proj(col):
                            bank, bk = mm[nb[0] % 3]
                            nb[0] += 1
                            for kc in range(8):
                                em.op("pe", lambda kc=kc, bank=bank: P.matmul(bank[:], lhsT=hTc[:, kc, :], rhs=Wi[:, kc, col:col + 512],
                                                                              start=(kc == 0), stop=(kc == 7)),
                                      reads=[hk], writes=[bk])
                            return bank, bk
                        for cb in range(2):
                            bank, bk = proj(cb * 512)
                            em.op("act", lambda cb=cb, bank=bank: S.copy(out=kf[s][:, 2 * cb:2 * cb + 2, :].rearrange("p h d -> p (h d)"),
                                                                        in_=bank[:]), reads=[bk], writes=[("kf", s, cb)])
                        for cb in range(4):
                            bank, bk = proj(1024 + cb * 512)
                            if cb % 2 == 0:
                                em.op("act", lambda cb=cb, bank=bank: S.copy(out=vbf[s][:, cb * 512:(cb + 1) * 512], in_=bank[:]),
                                      reads=[bk], writes=[("vbf", s)])
                            else:
                                em.op("dve", lambda cb=cb, bank=bank: V.tensor_copy(out=vbf[s][:, cb * 512:(cb + 1) * 512], in_=bank[:]),
                                      reads=[bk], writes=[("vbf", s)])
                        for cb in range(4):
                            bank, bk = proj(3072 + cb * 512)
                            em.op("act", lambda cb=cb, bank=bank: S.activation(out=sgb[s][:, cb * 512:(cb + 1) * 512], in_=bank[:],
                                                                                func=AF.Silu), reads=[bk], writes=[("sgb", s)])

                    def stage2(t):
                        c = NT - 1 - t
                        s = t % 2
                        x1, x2, cosb, sinb = rope_half(kf[s][:], cs[s], 4, *[None] * 4)
                        kfk = [("kf", s, 0), ("kf", s, 1), ("cs", s)]
                        em.op("pool", lambda: G.tensor_tensor(out=tt[0][:], in0=x1, in1=cosb, op=ALU.mult), reads=kfk, writes=[("tt", 0)])
                        em.op("pool", lambda: G.tensor_tensor(out=tt[1][:], in0=x2, in1=sinb, op=ALU.mult), reads=kfk, writes=[("tt", 1)])
                        em.op("dve", lambda: V.tensor_tensor(out=tt[2][:], in0=x2, in1=cosb, op=ALU.mult), reads=kfk, writes=[("tt", 2)])
                        em.op("dve", lambda: V.tensor_tensor(out=tt[3][:], in0=x1, in1=sinb, op=ALU.mult), reads=kfk, writes=[("tt", 3)])
                        em.op("pool", lambda: G.tensor_tensor(out=tt[0][:], in0=tt[0][:], in1=tt[1][:], op=ALU.subtract),
                              reads=[("tt", 0), ("tt", 1)], writes=[("tt", 0)])
                        em.op("dve", lambda: V.tensor_tensor(out=tt[2][:], in0=tt[2][:], in1=tt[3][:], op=ALU.add),
                              reads=[("tt", 2), ("tt", 3)], writes=[("tt", 2)])
                        em.op("dve", lambda: V.tensor_scalar(out=sck[:, 0:4], in0=kd16[:, 4:8], scalar1=flg[:, 1, c:c + 1], scalar2=None,
                                                             op0=ALU.mult), writes=[("sck",)])
                        em.op("dve", lambda: V.tensor_scalar(out=sck[:, 4:8], in0=cdt[:, 4:8], scalar1=flg[:, 1, c:c + 1], scalar2=None,
                                                             op0=ALU.mult), writes=[("cdk",)])
                        em.op("act", lambda: S.copy(out=kbs[:, :, 0:128], in_=tt[0][:]), reads=[("tt", 0)], writes=[("kbs",)])
                        em.op("act", lambda: S.copy(out=kbs[:, :, 128:256], in_=tt[2][:]), reads=[("tt", 2)], writes=[("kbs",)])
                        em.dma(KB[c].rearrange("p (h d) -> p h d", h=4), kbs[:], reads=[("kbs",)], writes=[("D_KB", c)], semkey="kbst")
                        sb_ = sck[:, 0:4].unsqueeze(2).broadcast_to([128, 4, 128])
                        em.op("pool", lambda: G.tensor_tensor(out=ksc[:, :, 0:128], in0=tt[0][:], in1=sb_, op=ALU.mult),
                              reads=[("tt", 0), ("sck",)], writes=[("ksc",)])
                        em.op("dve", lambda: V.tensor_tensor(out=ksc[:, :, 128:256], in0=tt[2][:], in1=sb_, op=ALU.mult),
                              reads=[("tt", 2), ("sck",)], writes=[("ksc",)])
                        em.dma(SG[c], sgb[s][:], reads=[("sgb", s)], writes=[("D_SG", c)], semkey=("sgbst", s))
                        em.dma(VB[c], vbf[s][:], reads=[("vbf", s)], writes=[("D_VB", c)], semkey=("vbst", s))
                        for h in range(4):
                            em.op("act", lambda h=h: S.copy(out=Sbf[:, h % 2], in_=Sm[:, 2 * h:2 * h + 2, :]),
                                  reads=[("Sm", 2 * h), ("Sm", 2 * h + 1)], writes=[("Sbf", h % 2)])
                            em.dma(Sscr[c][:, h * 1024:(h + 1) * 1024].rearrange("p (i e) -> p i e", i=2), Sbf[:, h % 2],
                                   reads=[("Sbf", h % 2)], writes=[("D_S", c, h)], semkey=("Sst", h % 2))
                        for h in range(4):
                            for dc in range(2):
                                i = h * 2 + dc
                                bank, bk = mm[4 + i % 2]
                                em.op("pe", lambda h=h, dc=dc, bank=bank: P.matmul(bank[:], lhsT=ksc[:, h, dc * 128:(dc + 1) * 128],
                                                                                  rhs=vbf[s][:, h * 512:(h + 1) * 512], start=True, stop=True),
                                      reads=[("ksc",), ("vbf", s)], writes=[bk])
                                em.op("dve", lambda h=h, i=i, bank=bank: V.scalar_tensor_tensor(out=Sm[:, i, :], in0=Sm[:, i, :],
                                                                                            scalar=sck[:, 4 + h:5 + h], in1=bank[:],
                                                                                            op0=ALU.mult, op1=ALU.add),
                                      reads=[bk, ("Sm", i), ("cdk",)], writes=[("Sm", i)])

                    pipeline(stage1, stage2, NT)
                    em.flush()

                if bstop == 2:
                    return
                with ExitStack() as s2:
                    Wi = load_weights(s2, b_w_in[j][:, 0:1024], 1024, "WiB2")
                    Wo = load_weights(s2, b_w_out[j], D, "WoB")
                    w = Work()
                    alloc_common(s2, w, 3)
                    gjunk = sbuf(s2, "junkB", [128, 512], BF16)
                    cs = [sbuf(s2, "csB%d" % i, [128, 256], F32) for i in range(2)]
                    qkf = sbuf(s2, "qkf", [128, 4, 256], F32)
                    tt = [sbuf(s2, "rtmp%d" % i, [128, 4, 128], F32) for i in range(4)]
                    qbf = sbuf(s2, "qbf", [128, 4, 256], BF16)
                    kbf = [sbuf(s2, "kbf%d" % i, [128, 4, 256], BF16) for i in range(2)]
                    ksc = [sbuf(s2, "ksc%d" % i, [128, 4, 256], BF16) for i in range(2)]
                    qTs = [sbuf(s2, "qTs%d" % i, [128, 4, 8, 128], BF16) for i in range(2)]
                    vbf = [sbuf(s2, "vbf%d" % i, [128, E], BF16) for i in range(2)]
                    sgh = [sbuf(s2, "sgh%d" % i, [128, 512], BF16) for i in range(2)]
                    PT = [sbuf(s2, "PT%d" % i, [128, 128], BF16) for i in range(4)]
                    gtok = sbuf(s2, "gtok", [128, E], BF16)
                    Sbl = sbuf(s2, "Sbl", [128, 2, 2, 512], BF16)
                    sck = [sbuf(s2, "sck%d" % i, [128, 8], F32) for i in range(2)]
                    gst = sbuf(s2, "gst", [128, 4, 4], F32)
                    mm = [(psum(s2, "mmB%d" % i, [128, 512], F32), ("mm", i)) for i in range(5)]
                    tpB = psum(s2, "tpB", [128, 8, 128], BF16)
                    em.op("pool", lambda: G.memset(Sm[:], 0.0), writes=[("Sm", i) for i in range(8)])

                    def stage1(c):
                        sx = prologue(w, c, li, src)
                        s = c % 2
                        hTc, hk = w.hT[sx % 2], ("hT", sx % 2)
                        em.dma(cs[s][:], tb["ropeB"][c * 128:(c + 1) * 128, :], writes=[("cs", s)], semkey=("cs", s))
                        em.op("dve", lambda: V.tensor_scalar(out=sck[s][:, 0:4], in0=kd16[:, 0:4], scalar1=flg[:, 0, c:c + 1], scalar2=None,
                                                             op0=ALU.mult), writes=[("sck", s)])
                        em.op("dve", lambda: V.tensor_scalar(out=sck[s][:, 4:8], in0=cdt[:, 0:4], scalar1=flg[:, 0, c:c + 1], scalar2=None,
                                                             op0=ALU.mult), writes=[("cdk", s)])
                        for cb in range(2):
                            bank, bk = mm[0]
                            col = cb * 512
                            for kc in range(8):
                                em.op("pe", lambda kc=kc, col=col, bank=bank: P.matmul(bank[:], lhsT=hTc[:, kc, :],
                                                                                     rhs=Wi[:, kc, col:col + 512],
                                                                                     start=(kc == 0), stop=(kc == 7)),
                                      reads=[hk], writes=[bk])
                            em.op("act", lambda cb=cb, bank=bank: S.copy(out=qkf[:, 2 * cb:2 * cb + 2, :].rearrange("p h d -> p (h d)"),
                                                                        in_=bank[:]), reads=[bk], writes=[("qkf", cb)])
                        x1, x2, cosb, sinb = rope_half(qkf[:], cs[s], 4, *[None] * 4)
                        kfk = [("qkf", 0), ("qkf", 1), ("cs", s)]
                        em.op("pool", lambda: G.tensor_tensor(out=tt[0][:], in0=x1, in1=cosb, op=ALU.mult), reads=kfk, writes=[("tt", 0)])
                        em.op("pool", lambda: G.tensor_tensor(out=tt[1][:], in0=x2, in1=sinb, op=ALU.mult), reads=kfk, writes=[("tt", 1)])
                        em.op("dve", lambda: V.tensor_tensor(out=tt[2][:], in0=x2, in1=cosb, op=ALU.mult), reads=kfk, writes=[("tt", 2)])
                        em.op("dve", lambda: V.tensor_tensor(out=tt[3][:], in0=x1, in1=sinb, op=ALU.mult), reads=kfk, writes=[("tt", 3)])
                        em.op("pool", lambda: G.tensor_tensor(out=qbf[:, :, 0:128], in0=tt[0][:], in1=tt[1][:], op=ALU.subtract),
                              reads=[("tt", 0), ("tt", 1)], writes=[("qbf",)])
                        em.op("dve", lambda: V.tensor_tensor(out=qbf[:, :, 128:256], in0=tt[2][:], in1=tt[3][:], op=ALU.add),
                              reads=[("tt", 2), ("tt", 3)], writes=[("qbf",)])
                        em.dma(kbf[s][:], KB[c].rearrange("p (h d) -> p h d", h=4), writes=[("kbf", s)], semkey=("kbf", s))
                        em.dma(vbf[s][:], VB[c], writes=[("vbf", s)], semkey=("vbf", s))
                        em.op("pool", lambda: G.tensor_tensor(out=ksc[s][:], in0=kbf[s][:],
                                                              in1=sck[s][:, 0:4].unsqueeze(2).broadcast_to([128, 4, 256]), op=ALU.mult),
                              reads=[("kbf", s), ("sck", s)], writes=[("ksc", s)])
                        qb2 = qbf[:].rearrange("p h d -> p (h d)")
                        kb2 = kbf[s][:].rearrange("p h d -> p (h d)")
                        for m in range(8):
                            em.op("pe", lambda m=m: P.transpose(out=w.tp[:, m, :], in_=qb2[:, m * 128:(m + 1) * 128], identity=ident[:]),
                                  reads=[("qbf",)], writes=[("tp",)])
                        for m in range(8):
                            em.op("pe", lambda m=m: P.transpose(out=w.tp[:, 8 + m, :], in_=kb2[:, m * 128:(m + 1) * 128], identity=ident[:]),
                                  reads=[("kbf", s)], writes=[("tp2",)])
                        q4 = qTs[s]
                        em.op("act", lambda: S.copy(out=q4[:, 0], in_=w.tp[:, 0:8, :]), reads=[("tp",)], writes=[("qT", s)])
                        for h in range(4):
                            em.op("dve", lambda h=h: V.tensor_tensor(out=q4[:, 1, 2 * h:2 * h + 2, :], in0=q4[:, 0, 2 * h:2 * h + 2, :],
                                                                     in1=bc_mid(qdf[:, h, :], 2), op=ALU.mult),
                                  reads=[("qT", s)], writes=[("qTf", s)])
                            em.op("pool", lambda h=h: G.tensor_tensor(out=q4[:, 2, 2 * h:2 * h + 2, :], in0=q4[:, 0, 2 * h:2 * h + 2, :],
                                                                      in1=bc_mid(qdb[:, h, :], 2), op=ALU.mult),
                                  reads=[("qT", s)], writes=[("qTb", s)])
                        em.op("act", lambda: S.copy(out=q4[:, 3], in_=w.tp[:, 8:16, :]), reads=[("tp2",)], writes=[("kT", s)])

                    def stage2(c):
                        s = c % 2
                        q4 = qTs[s]
                        gatedT = q4[:, 0:2].rearrange("p a c t -> p (a c) t")
                        sb, sk_ = mm[1]
                        for h in range(4):
                            for dc in range(2):
                                em.op("pe", lambda dc=dc, h=h: P.matmul(sb[:, h * 128:(h + 1) * 128], lhsT=q4[:, 3, h * 2 + dc, :],
                                                                        rhs=q4[:, 0, h * 2 + dc, :], start=(dc == 0), stop=(dc == 1)),
                                      reads=[("kT", s), ("qT", s)], writes=[sk_])
                        for h in range(4):
                            em.op("dve", lambda h=h: V.tensor_tensor(out=PT[h][:], in0=sb[:, h * 128:(h + 1) * 128], in1=Dm[:, h, :],
                                                                     op=ALU.mult), reads=[sk_], writes=[("PT", h)])
                        for h in range(4):
                            em.dma(sgh[h % 2][:], SG[c][:, h * 512:(h + 1) * 512], writes=[("sgh", h % 2)], semkey=("sgh", h % 2))
                            em.op("act", lambda h=h: S.copy(out=Sbf[:, h % 2], in_=Sm[:, 2 * h:2 * h + 2, :]),
                                  reads=[("Sm", 2 * h), ("Sm", 2 * h + 1)], writes=[("Sbf", h % 2)])
                            em.dma(Sbl[:, h % 2], Sscr[c][:, h * 1024:(h + 1) * 1024].rearrange("p (i e) -> p i e", i=2),
                                   writes=[("Sbl", h % 2)], semkey=("Sbl", h % 2))
                            pt = PT[h]
                            ob, ok_ = mm[2 + h % 2]
                            em.op("pe", lambda h=h, ob=ob, pt=pt: P.matmul(ob[:], lhsT=pt[:], rhs=vbf[s][:, h * 512:(h + 1) * 512],
                                                                           start=True, stop=False),
                                  reads=[("PT", h), ("vbf", s)], writes=[ok_])
                            for dc in range(2):
                                i = h * 2 + dc
                                em.op("pe", lambda i=i, ob=ob, h=h, dc=dc: P.matmul(ob[:], lhsT=q4[:, 1, i, :], rhs=Sbf[:, h % 2, dc, :],
                                                                               start=False, stop=False),
                                      reads=[("qTf", s), ("Sbf", h % 2)], writes=[ok_])
                            for dc in range(2):
                                i = h * 2 + dc
                                em.op("pe", lambda i=i, ob=ob, dc=dc, h=h: P.matmul(ob[:], lhsT=q4[:, 2, i, :], rhs=Sbl[:, h % 2, dc, :],
                                                                               start=False, stop=(dc == 1)),
                                      reads=[("qTb", s), ("Sbl", h % 2)], writes=[ok_])
                            gs = gst[:, h, :]
                            em.op("act", lambda ob=ob, gs=gs: S.activation(out=gjunk[:], in_=ob[:], func=AF.Square, accum_out=gs[:, 0:1]),
                                  reads=[ok_], writes=[("gjunk",), ("gst", h)])
                            em.op("dve", lambda gs=gs: V.tensor_scalar(out=gs[:, 1:2], in0=gs[:, 0:1], scalar1=1.0 / 512, scalar2=EPS,
                                                                       op0=ALU.mult, op1=ALU.add), reads=[("gst", h)], writes=[("gst", h)])
                            em.op("act", lambda gs=gs: S.activation(out=gs[:, 2:3], in_=gs[:, 1:2], func=AF.Ln),
                                  reads=[("gst", h)], writes=[("gst", h)])
                            em.op("act", lambda gs=gs: S.activation(out=gs[:, 3:4], in_=gs[:, 2:3], func=AF.Exp, scale=-0.5),
                                  reads=[("gst", h)], writes=[("gst", h)])
                            em.op("dve", lambda h=h, ob=ob, gs=gs: V.scalar_tensor_tensor(out=gtok[:, h * 512:(h + 1) * 512], in0=ob[:],
                                                                                        scalar=gs[:, 3:4], in1=sgh[h % 2][:], op0=ALU.mult,
                                                                                        op1=ALU.mult),
                                  reads=[ok_, ("gst", h), ("sgh", h % 2)], writes=[("gtok", h)])
                        for h in range(4):
                            for dc in range(2):
                                i = h * 2 + dc
                                bank, bk = mm[4] if i % 2 == 0 else mm[1]
                                em.op("pe", lambda h=h, dc=dc, bank=bank: P.matmul(bank[:], lhsT=ksc[s][:, h, dc * 128:(dc + 1) * 128],
                                                                                  rhs=vbf[s][:, h * 512:(h + 1) * 512], start=True, stop=True),
                                      reads=[("ksc", s), ("vbf", s)], writes=[bk])
                                em.op("dve", lambda h=h, i=i, bank=bank: V.scalar_tensor_tensor(out=Sm[:, i, :], in0=Sm[:, i, :],
                                                                                            scalar=sck[s][:, 4 + h:5 + h], in1=bank[:],
                                                                                            op0=ALU.mult, op1=ALU.add),
                                      reads=[bk, ("Sm", i), ("cdk", s)], writes=[("Sm", i)])
                        for half in range(2):
                            for m in range(8):
                                mm_ = half * 8 + m
                                em.op("pe", lambda m=m, mm_=mm_: P.transpose(out=tpB[:, m, :], in_=gtok[:, mm_ * 128:(mm_ + 1) * 128],
                                                                             identity=ident[:]),
                                      reads=[("gtok", mm_ // 4)], writes=[("tpB",)])
                            if half == 0:
                                em.op("act", lambda: S.copy(out=gatedT[:, 0:8, :], in_=tpB[:]), reads=[("tpB",)], writes=[("qT", s)])
                            else:
                                em.op("dve", lambda: V.tensor_copy(out=gatedT[:, 8:16, :], in_=tpB[:]), reads=[("tpB",)], writes=[("qTf", s)])
                        epilogue(w, c, c % 3, gatedT, [("qT", s), ("qTf", s)], Wo, [mm[2], mm[3]], dst, last)

                    pipeline(stage1, stage2, NT)
                    em.flush()

        def layer_C(li, j, src, dst, last):
            with ExitStack() as st:
                load_gain(li, st, False)
                Wi = load_weights(st, c_w_in[j][:, 0:2048], 2048, "WiCu")
                w = Work()
                alloc_common(st, w, 2)
                FC = sbuf(st, "FC", [128, 4, 2, 512], BF16)
                zb = [sbuf(st, "zb%d" % i, [128, 2, 2048], BF16) for i in range(2)]
                mm = [(psum(st, "mmC%d" % i, [128, 512], F32), ("mm", i)) for i in range(6)]
                em.dma(FC[:], tb["FC"], writes=[("FC",)], semkey="FC")

                hT4 = [sbuf(st, "hT4C_%d" % i, [128, 8, 512], BF16) for i in range(2)]
                uT4s = [sbuf(st, "uT4_%d" % i, [128, 16, 512], BF16) for i in range(2)]

                def uproj4(ss):
                    uT4 = uT4s[ss]
                    hks = [("hT4", ss, tl) for tl in range(4)]
                    for m in range(16):
                        bank, bk = mm[m % 2]
                        for kc in range(8):
                            em.op("pe", lambda kc=kc, m=m, bank=bank: P.matmul(bank[:], lhsT=Wi[:, kc, m * 128:(m + 1) * 128],
                                                                              rhs=hT4[ss][:, kc, :], start=(kc == 0), stop=(kc == 7)),
                                  reads=hks, writes=[bk])
                        if m % 2 == 0:
                            em.op("act", lambda m=m, bank=bank: S.copy(out=uT4[:, m, :], in_=bank[:]), reads=[bk], writes=[("uT", ss, m // 4)])
                        else:
                            em.op("dve", lambda m=m, bank=bank: V.tensor_copy(out=uT4[:, m, :], in_=bank[:]), reads=[bk],
                                  writes=[("uT", ss, m // 4)])

                def body1(t):
                    tl = t % 4
                    uT4 = uT4s[(t // 4) % 2]
                    ss = (t // 4) % 2
                    zs = t % 2
                    for grp in range(4):
                        for ri in range(2):
                            n_ = grp * 2 + ri
                            bank, bk = mm[2 + n_ % 4]
                            for cc in range(4):
                                em.op("pe", lambda cc=cc, grp=grp, ri=ri, bank=bank: P.matmul(
                                    bank[:], lhsT=uT4[:, grp * 4 + cc, tl * 128:(tl + 1) * 128], rhs=FC[:, cc, ri, :],
                                    start=(cc == 0), stop=(cc == 3)),
                                      reads=[("uT", ss, grp), ("FC",)], writes=[bk])
                            if n_ % 2 == 0:
                                em.op("act", lambda grp=grp, ri=ri, bank=bank: S.copy(out=zb[zs][:, ri, grp * 512:(grp + 1) * 512], in_=bank[:]),
                                      reads=[bk], writes=[("zb", zs)])
                            else:
                                em.op("dve", lambda grp=grp, ri=ri, bank=bank: V.tensor_copy(out=zb[zs][:, ri, grp * 512:(grp + 1) * 512],
                                                                                            in_=bank[:]), reads=[bk], writes=[("zb", zs)])
                    em.dma(Zscr[t * 128:(t + 1) * 128, :].rearrange("p (r c) -> p r c", r=2), zb[zs][:], reads=[("zb", zs)],
                           writes=[("D_Z", t)], semkey=("zst", zs))

                def c1s1(sb):
                    ss = sb % 2
                    for tl in range(4):
                        prologue(w, 4 * sb + tl, li, src, hT4[ss][:, :, tl * 128:(tl + 1) * 128], ("hT4", ss, tl))
                    uproj4(ss)

                def c1s2(sb):
                    for tl in range(4):
                        body1(4 * sb + tl)

                pipeline(c1s1, c1s2, NT // 4)
                em.flush()
            with ExitStack() as st:
                Gt = sbuf(st, "Gt", [NT, 128, 3, NT], BF16)
                zr = [sbuf(st, "zr%d" % i, [NT, 2, 2048], BF16) for i in range(2)]
                Bs = [sbuf(st, "Bs%d" % i, [NT, 2, 2048], BF16) for i in range(2)]
                mm = [(psum(st, "mmC%d" % i, [128, 512], F32), ("mm", i)) for i in range(8)]
                em.dma(Gt[:], tb["G"], writes=[("Gt",)], semkey="Gt")

                def body2a(n2):
                    s = n2 % 2
                    em.dma(zr[s][:], Zscr[n2::128, :].rearrange("p (r c) -> p r c", r=2), writes=[("zr", s)], semkey=("zr", s))
                    for cb in range(4):
                        (bre, kre), (bim, kim) = mm[(2 * cb) % 8], mm[(2 * cb + 1) % 8]
                        cs_ = slice(cb * 512, (cb + 1) * 512)
                        for (bank, bk, parts) in ((bre, kre, ((0, 0), (2, 1))), (bim, kim, ((1, 0), (0, 1)))):
                            for n_, (gi, ri) in enumerate(parts):
                                em.op("pe", lambda bank=bank, gi=gi, ri=ri, n_=n_, cs_=cs_: P.matmul(
                                    bank[0:NT, :], lhsT=Gt[:, n2, gi, :], rhs=zr[s][:, ri, cs_], start=(n_ == 0), stop=(n_ == 1)),
                                      reads=[("zr", s), ("Gt",)], writes=[bk])
                        em.op("act", lambda bre=bre, cs_=cs_: S.copy(out=Bs[s][:, 0, cs_], in_=bre[0:NT, :]), reads=[kre], writes=[("Bs", s)])
                        em.op("dve", lambda bim=bim, cs_=cs_: V.tensor_copy(out=Bs[s][:, 1, cs_], in_=bim[0:NT, :]), reads=[kim],
                              writes=[("Bs", s)])
                    em.dma(Bscr[:, n2, :].rearrange("p (r c) -> p r c", r=2), Bs[s][:], reads=[("Bs", s)], writes=[("D_B", n2)],
                           semkey=("bst", s))

                for n2 in range(128):
                    body2a(n2)
                em.flush()
            with ExitStack() as st:
                Ht = sbuf(st, "Ht", [128, 2, 128], BF16)
                Br = [sbuf(st, "Br%d" % i, [128, 2, 2048], BF16) for i in range(2)]
                Ms = [sbuf(st, "Ms%d" % i, [128, 2048], BF16) for i in range(2)]
                mm = [(psum(st, "mmC%d" % i, [128, 512], F32), ("mm", i)) for i in range(8)]
                em.dma(Ht[:], tb["H"], writes=[("Ht",)], semkey="Ht")

                def body2b(q):
                    s = q % 2
                    em.dma(Br[s][:], Bscr[q].rearrange("p (r c) -> p r c", r=2), writes=[("Br", s)], semkey=("Br", s))
                    for cb in range(4):
                        bank, bk = mm[(q * 4 + cb) % 8]
                        cs_ = slice(cb * 512, (cb + 1) * 512)
                        for ri in range(2):
                            em.op("pe", lambda bank=bank, ri=ri, cs_=cs_: P.matmul(bank[:], lhsT=Ht[:, ri, :], rhs=Br[s][:, ri, cs_],
                                                                                  start=(ri == 0), stop=(ri == 1)),
                                  reads=[("Br", s), ("Ht",)], writes=[bk])
                        if cb % 2 == 0:
                            em.op("act", lambda bank=bank, cs_=cs_: S.copy(out=Ms[s][:, cs_], in_=bank[:]), reads=[bk], writes=[("Ms", s)])
                        else:
                            em.op("dve", lambda bank=bank, cs_=cs_: V.tensor_copy(out=Ms[s][:, cs_], in_=bank[:]), reads=[bk],
                                  writes=[("Ms", s)])
                    em.dma(Mscr[q * 128:(q + 1) * 128, :], Ms[s][:], reads=[("Ms", s)], writes=[("D_M", q)], semkey=("mst", s))

                for q in range(NT):
                    body2b(q)
                em.flush()
            with ExitStack() as st:
                load_gain(li, st, last)
                Wi = load_weights(st, c_w_in[j][:, 2048:4096], 2048, "WiCg")
                Wo = load_weights(st, c_w_out[j], D, "WoC")
                w = Work()
                alloc_common(st, w, 8)
                hT4 = [sbuf(st, "hT4C3_%d" % i, [128, 8, 512], BF16) for i in range(2)]
                PAB = sbuf(st, "PAB", [128, 2, 128], BF16)
                MA = [sbuf(st, "MA%d" % i, [128, E], BF16) for i in range(2)]
                MB = [sbuf(st, "MB%d" % i, [128, E], BF16) for i in range(2)]
                sgT = [sbuf(st, "sgTC%d" % i, [128, 16, 512], BF16) for i in range(2)]
                gatedT = sbuf(st, "gatedTC", [128, 16, 128], BF16)
                mm = [(psum(st, "mmC%d" % i, [128, 512], F32), ("mm", i)) for i in range(6)]
                em.dma(PAB[:], tb["PAB"], writes=[("PAB",)], semkey="PAB")

                def gate4(ss):
                    hks = [("hT4", ss, tl) for tl in range(4)]
                    for m in range(16):
                        bank, bk = mm[m % 2]
                        for kc in range(8):
                            em.op("pe", lambda kc=kc, m=m, bank=bank: P.matmul(bank[:], lhsT=Wi[:, kc, m * 128:(m + 1) * 128],
                                                                              rhs=hT4[ss][:, kc, :], start=(kc == 0), stop=(kc == 7)),
                                  reads=hks, writes=[bk])
                        em.op("act", lambda m=m, bank=bank: S.activation(out=sgT[ss][:, m, :], in_=bank[:], func=AF.Silu),
                              reads=[bk], writes=[("sgT", ss, m // 4)])

                def body3(t, ss):
                    s = t % 8
                    tl = t % 4
                    em.dma(MA[t % 2][:], Mscr[t:t + TSA * 127 + 1:TSA, :], writes=[("MA", t % 2)], semkey=("MA", t % 2))
                    bB = 128 * TSB * (t // TSB) + t % TSB
                    em.dma(MB[t % 2][:], Mscr[bB:bB + TSB * 127 + 1:TSB, :], writes=[("MB", t % 2)], semkey=("MB", t % 2))
                    for m4 in range(4):
                        bank, bk = mm[2 + m4 % 2]
                        for mi in range(4):
                            m = m4 * 4 + mi
                            em.op("pe", lambda m=m, mi=mi, bank=bank: P.matmul(bank[:, mi * 128:(mi + 1) * 128],
                                                                              lhsT=MA[t % 2][:, m * 128:(m + 1) * 128], rhs=PAB[:, 0, :],
                                                                              start=True, stop=False),
                                  reads=[("MA", t % 2), ("PAB",)], writes=[bk])
                            em.op("pe", lambda m=m, mi=mi, bank=bank: P.matmul(bank[:, mi * 128:(mi + 1) * 128],
                                                                              lhsT=MB[t % 2][:, m * 128:(m + 1) * 128], rhs=PAB[:, 1, :],
                                                                              start=False, stop=True),
                                  reads=[("MB", t % 2), ("PAB",)], writes=[bk])
                        em.op("dve", lambda m4=m4, bank=bank: V.tensor_tensor(
                            out=gatedT[:, m4 * 4:(m4 + 1) * 4, :], in0=bank[:].rearrange("p (c t) -> p c t", c=4),
                            in1=sgT[ss][:, m4 * 4:(m4 + 1) * 4, tl * 128:(tl + 1) * 128], op=ALU.mult),
                              reads=[bk, ("sgT", ss, m4)], writes=[("gatedT",)])
                    epilogue(w, t, s, gatedT, ("gatedT",), Wo, [mm[4], mm[5]], dst, last)

                def c3s1(sb):
                    ss = sb % 2
                    for tl in range(4):
                        prologue(w, 4 * sb + tl, li, src, hT4[ss][:, :, tl * 128:(tl + 1) * 128], ("hT4", ss, tl))
                    gate4(ss)

                def c3s2(sb):
                    for tl in range(4):
                        body3(4 * sb + tl, sb % 2)

                pipeline(c3s1, c3s2, NT // 4)
                em.flush()

        cnt = {"A": 0, "B": 0, "C": 0}
        src = x_in
        for li, kind in enumerate(layers):
            last = final_norm and (li == len(layers) - 1)
            dst = y_out if li == len(layers) - 1 else xscr
            j = cnt[kind]
            cnt[kind] += 1
            if kind == "A":
                layer_A(li, j, src, dst, last)
            elif kind == "B":
                layer_B(li, j, src, dst, last)
            else:
                layer_C(li, j, src, dst, last)
            src = dst
        nc._em_n_instr = em.n_instr
    return nc


def kernel(x_prompt, x_sample, norm_g, final_norm_g, a_w_in, a_sink, a_w_out,
           b_w_in, b_decay, b_w_out, c_w_in, c_w_out):
    NT, TSA, TSB = 128, 128, 16
    nc = build(NT, TSA, TSB)
    f = lambda a: np.ascontiguousarray(np.asarray(a, dtype=np.float32))
    common = {"norm_g": f(norm_g), "final_norm_g": f(final_norm_g).reshape(1, D), "a_w_in": f(a_w_in), "a_sink": f(a_sink),
              "a_w_out": f(a_w_out), "b_w_in": f(b_w_in), "b_decay": f(b_decay).reshape(1, 8), "b_w_out": f(b_w_out),
              "c_w_in": f(c_w_in), "c_w_out": f(c_w_out)}
    tabA = const_tables(NT, TSA, 1.0, 0.0, TSA, TSB)
    tabB = const_tables(NT, TSB, 0.0, 1.0, TSA, TSB)
    xp = f(x_prompt)
    xs = f(x_sample)
    in_maps = []
    for c in range(8):
        if c < 2:
            m = dict(common, **tabA)
            m["x"] = xs[c]
        else:
            m = dict(common, **tabB)
            xx = np.zeros((8, 2048, D), np.float32)
            seqs = list(range((c - 2) * 6, min((c - 2) * 6 + 6, 32)))
            xx[:len(seqs)] = xp[seqs]
            m["x"] = xx.reshape(NT * 128, D)
        in_maps.append(m)
    res = run_bass_kernel_spmd(nc, in_maps, core_ids=list(range(8)))
    y_s = np.stack([res.results[c]["y"] for c in range(2)], 0).astype(np.float32)
    y_p = np.zeros((32, 2048, D), np.float32)
    for c in range(2, 8):
        seqs = list(range((c - 2) * 6, min((c - 2) * 6 + 6, 32)))
        yy = res.results[c]["y"].reshape(8, 2048, D)
        y_p[seqs] = yy[:len(seqs)]
    return (y_p, y_s)
```

```python
import math
from contextlib import ExitStack

import numpy as np
import ml_dtypes

import concourse.bass as bass
import concourse.mybir as mybir
from concourse.bass_utils import run_bass_kernel_spmd

F32 = mybir.dt.float32
BF16 = mybir.dt.bfloat16
AF = mybir.ActivationFunctionType
ALU = mybir.AluOpType
NPBF = ml_dtypes.bfloat16

D = 1024
E = 2048
EPS = 1e-6
A_IN = 4608
B_IN = 6144
C_IN = 4096
LAYERS = ("A", "B", "C", "A")


class Em:
    ENGS = ("pe", "act", "dve", "pool", "sp")

    def __init__(self, nc, stack):
        self.nc = nc
        self.eng = {"pe": nc.tensor, "act": nc.scalar, "dve": nc.vector, "pool": nc.gpsimd, "sp": nc.sync}
        self.stack = stack
        self.sem = {e: stack.enter_context(nc.semaphore("s_" + e)) for e in self.ENGS}
        self.semval = {e: 0 for e in self.ENGS}
        self.dsem = {}
        self.dval = {}
        self.ops = []
        self.n_instr = 0

    def op(self, e, fn, reads=(), writes=()):
        self.ops.append(("c", e, fn, tuple(reads), tuple(writes), None))

    def dma(self, out, in_, reads=(), writes=(), semkey=None, q="sp", **kw):
        assert semkey is not None
        self.ops.append(("d", q, (out, in_, kw), tuple(reads), tuple(writes), semkey))

    @staticmethod
    def _hoist(ops):
        pos = [0.0] * len(ops)
        last_touch = {}
        last_sem = {}
        nh = 0
        for i, o in enumerate(ops):
            kind, e, fn, reads, writes, semkey = o
            p = float(i)
            if kind == "d" and not reads and writes and not str(writes[0][0]).startswith("D_"):
                q = max([last_touch.get(k, -1.0) for k in writes] + [last_sem.get(semkey, -1.0)])
                nh += 1
                p = min(p, q + 1e-7 * nh)
            pos[i] = p
            for k in reads:
                last_touch[k] = max(last_touch.get(k, -1.0), p)
            for k in writes:
                last_touch[k] = max(last_touch.get(k, -1.0), p)
            if kind == "d":
                last_sem[semkey] = max(last_sem.get(semkey, -1.0), p)
        order = sorted(range(len(ops)), key=lambda i: (pos[i], i))
        return [ops[i] for i in order]

    def flush(self):
        ops = self._hoist(self.ops)
        self.ops = []
        n = len(ops)
        last_w = {}
        readers = {}
        eidx = {e: 0 for e in self.ENGS}
        op_eidx = [0] * n
        known = {e: {} for e in self.ENGS}
        waits = [None] * n
        needed = [False] * n
        for i, o in enumerate(ops):
            kind, e, fn, reads, writes, semkey = o
            eidx[e] += 1
            op_eidx[i] = eidx[e]
            deps = set()
            for k in reads:
                j = last_w.get(k)
                if j is not None:
                    deps.add(j)
            for k in writes:
                j = last_w.get(k)
                if j is not None:
                    deps.add(j)
                for r in readers.get(k, ()):
                    deps.add(r)
            deps.discard(i)
            w = []
            kn = known[e]
            for j in sorted(deps):
                oj = ops[j]
                if oj[0] == "c":
                    ej = oj[1]
                    if ej == e and e == "pe":
                        continue
                    kk = ej
                    if kn.get(kk, -1) >= op_eidx[j]:
                        continue
                    kn[kk] = op_eidx[j]
                else:
                    kk = ("d", oj[5])
                    if kn.get(kk, -1) >= j:
                        continue
                    kn[kk] = j
                w.append(j)
                needed[j] = True
            waits[i] = w
            for k in writes:
                last_w[k] = i
                readers[k] = []
            for k in reads:
                readers.setdefault(k, []).append(i)
        last_of = {}
        for i, o in enumerate(ops):
            if o[0] == "c":
                last_of[o[1]] = i
            else:
                needed[i] = True
        for i in last_of.values():
            needed[i] = True
        val = [None] * n
        for i, o in enumerate(ops):
            if not needed[i]:
                continue
            if o[0] == "c":
                self.semval[o[1]] += 1
                val[i] = (self.sem[o[1]], self.semval[o[1]])
            else:
                sk = o[5]
                if sk not in self.dsem:
                    self.dsem[sk] = self.stack.enter_context(self.nc.semaphore("d%d" % len(self.dsem)))
                    self.dval[sk] = 0
                self.dval[sk] += 16
                val[i] = (self.dsem[sk], self.dval[sk])
        for i, o in enumerate(ops):
            kind, e, fn, reads, writes, semkey = o
            eng = self.eng[e]
            for j in waits[i]:
                s, v = val[j]
                eng.wait_ge(s, v)
            if kind == "c":
                ins = fn()
                if needed[i]:
                    ins.then_inc(val[i][0], 1)
            else:
                out, in_, kw = fn
                eng.dma_start(out=out, in_=in_, **kw).then_inc(val[i][0], 16)
            self.n_instr += 1 + len(waits[i])
        self.barrier()

    def barrier(self):
        sp = self.eng["sp"]
        for sk, s in self.dsem.items():
            if self.dval[sk] > 0:
                sp.wait_ge(s, self.dval[sk])
        for e in ("pe", "act", "dve", "pool"):
            if self.semval[e] > 0:
                sp.wait_ge(self.sem[e], self.semval[e])
        self.semval["sp"] += 1
        sp.nop().then_inc(self.sem["sp"], 1)
        for e in ("pe", "act", "dve", "pool"):
            self.eng[e].wait_ge(self.sem["sp"], self.semval["sp"])


def const_tables(NT, TS, fa, fb, TSA, TSB):
    T = NT * 128
    pos = (np.arange(T) % (TS * 128)).astype(np.float32)
    t = {}
    t["ident"] = np.eye(128, dtype=np.float32).astype(NPBF)
    invA = np.exp(-(np.arange(8, dtype=np.float32) * (2.0 / 16)) * np.float32(math.log(500000.0))).astype(np.float32)
    angA = (pos[:, None] * invA[None, :]).astype(np.float32)
    t["ropeA"] = np.concatenate([np.cos(angA), np.sin(angA)], 1).astype(np.float32)
    invB = np.exp(-(np.arange(128, dtype=np.float32) * (2.0 / 256)) * np.float32(math.log(10000.0))).astype(np.float32)
    angB = (pos[:, None] * invB[None, :]).astype(np.float32)
    t["ropeB"] = np.concatenate([np.cos(angB), np.sin(angB)], 1).astype(np.float32)
    tiles = np.arange(NT)
    fl = np.zeros((2, NT), np.float32)
    fl[0] = (tiles % TS != 0)
    fl[1] = (tiles % TS != TS - 1)
    t["flagsA"] = np.broadcast_to(fl[None], (128, 2, NT)).astype(np.float32).copy()
    kk = np.arange(128)
    mk = np.zeros((128, 2, 128), np.float32)
    mk[:, 0, :] = (kk[:, None] >= kk[None, :])
    mk[:, 1, :] = (kk[:, None] <= kk[None, :])
    t["maskLR"] = mk.astype(NPBF)
    jj = kk[:, None].astype(np.float32)
    ii = kk[None, :].astype(np.float32)
    rt = np.zeros((128, 4, 128), np.float32)
    rt[:, 0, :] = np.where(jj <= ii, ii - jj, 0.0)
    rt[:, 1, :] = np.where(jj > ii, jj - ii, 0.0)
    rt[:, 2, :] = ii + 1.0
    rt[:, 3, :] = 128.0 - ii
    t["retT"] = rt
    rp = np.zeros((128, 2), np.float32)
    rp[:, 0] = 127.0 - kk
    rp[:, 1] = kk
    t["retP"] = rp
    fb_ = np.zeros((2, NT), np.float32)
    fb_[0] = ((tiles + 1) % TS != 0)
    fb_[1] = (tiles % TS != 0)
    t["flagsB"] = np.broadcast_to(fb_[None], (128, 2, NT)).astype(np.float32).copy()
    c = np.arange(512)
    ang = 2 * np.pi * np.outer(c, c) / 512.0
    fc = np.stack([np.cos(ang), -np.sin(ang)], 1) / math.sqrt(512.0)
    t["FC"] = fc.reshape(4, 128, 2, 512).transpose(1, 0, 2, 3).astype(NPBF).copy()
    NS = NT // TS
    n1 = np.arange(NT)
    s1, m1 = n1 // TS, n1 % TS
    q = np.arange(NT)
    sq, k1 = q // TS, q % TS
    n2 = np.arange(128)
    ph = (m1[:, None, None] * k1[None, None, :] / TS) + (n2[None, :, None] * k1[None, None, :] / (128.0 * TS))
    Gc = np.exp(-2j * np.pi * ph) * (s1[:, None, None] == sq[None, None, :])
    G = np.stack([Gc.real, Gc.imag, -Gc.imag], 2)
    t["G"] = G.astype(NPBF)
    R = 128 // TS
    k2 = np.arange(128)
    sig = TS * (k2 % R) + k2 // R
    Hc = np.exp(-2j * np.pi * np.outer(n2, k2) / 128.0) / math.sqrt(128.0 * TS)
    H = np.zeros((128, 2, 128), np.float64)
    H[:, 0, sig] = Hc.real
    H[:, 1, sig] = -Hc.imag
    t["H"] = H.astype(NPBF)
    pi = np.arange(128)
    P = np.zeros((128, 2, 128), np.float32)
    for v, (f, ts) in enumerate(((fa, TSA), (fb, TSB))):
        r = 128 // ts
        w = ts * (pi % r) + pi // r
        P[pi, v, w] = f
    t["PAB"] = P.astype(NPBF)
    return t


TABLE_SPECS = lambda NT: {
    "ident": ([128, 128], BF16), "ropeA": ([NT * 128, 16], F32), "ropeB": ([NT * 128, 256], F32),
    "flagsA": ([128, 2, NT], F32), "maskLR": ([128, 2, 128], BF16), "retT": ([128, 4, 128], F32),
    "retP": ([128, 2], F32), "flagsB": ([128, 2, NT], F32), "FC": ([128, 4, 2, 512], BF16),
    "G": ([NT, 128, 3, NT], BF16), "H": ([128, 2, 128], BF16), "PAB": ([128, 2, 128], BF16),
}


def build(NT, TSA, TSB, layers=LAYERS, final_norm=True, debug=False, bstop=0):
    T = NT * 128
    nc = bass.Bass("TRN2", target_bir_lowering=False)
    din = lambda name, shape, dt=F32: nc.dram_tensor(name, shape, dt, kind="ExternalInput").ap()
    x_in = din("x", [T, D])
    norm_g = din("norm_g", [4, D])
    fin_g = din("final_norm_g", [1, D])
    a_w_in = din("a_w_in", [2, D, A_IN])
    a_sink = din("a_sink", [2, 32])
    a_w_out = din("a_w_out", [2, E, D])
    b_w_in = din("b_w_in", [1, D, B_IN])
    b_decay = din("b_decay", [1, 8])
    b_w_out = din("b_w_out", [1, E, D])
    c_w_in = din("c_w_in", [1, D, C_IN])
    c_w_out = din("c_w_out", [1, E, D])
    tb = {k: din(k, sh, dt) for k, (sh, dt) in TABLE_SPECS(NT).items()}
    y_out = nc.dram_tensor("y", [T, D], F32, kind="ExternalOutput").ap()
    dscr = lambda name, shape, dt: nc.dram_tensor(name, shape, dt, kind="Internal").ap()
    xscr = dscr("xscr", [T, D], F32)
    KT = dscr("KT", [NT, 128, 1024], BF16)
    VA = dscr("VA", [NT, 128, 1024], BF16)
    SG = dscr("SG", [NT, 128, 2048], BF16)
    KB = dscr("KBs", [NT, 128, 1024], BF16)
    VB = dscr("VBs", [NT, 128, 2048], BF16)
    Sscr = dscr("Sscr", [NT, 128, 4096], BF16)
    Zscr = dscr("Zscr", [T, 4096], BF16)
    Bscr = dscr("Bscr", [NT, 128, 4096], BF16)
    Mscr = dscr("Mscr", [T, E], BF16)

    with ExitStack() as top:
        em = Em(nc, top)
        V, S, G, P = nc.vector, nc.scalar, nc.gpsimd, nc.tensor

        uid = [0]

        def sbuf(st, name, shape, dt):
            uid[0] += 1
            return st.enter_context(nc.sbuf_tensor("sb%d_%s" % (uid[0], name), shape, dt))

        def psum(st, name, shape, dt):
            uid[0] += 1
            return st.enter_context(nc.psum_tensor("ps%d_%s" % (uid[0], name), shape, dt))

        ident = sbuf(top, "ident", [128, 128], BF16)
        gl = sbuf(top, "gl", [128, D], F32)
        gfin_box = [None]
        epsb = sbuf(top, "epsb", [128, 1], F32)
        em.op("pool", lambda: G.memset(epsb[:], EPS), writes=[("epsb",)])
        em.dma(ident[:], tb["ident"], writes=[("ident",)], semkey="c0")
        em.flush()

        def load_weights(st, Wd, ncols, name):
            K = Wd.shape[0]
            Wb = sbuf(st, name, [128, K // 128, ncols], BF16)
            with ExitStack() as st2:
                stg = [sbuf(st2, "stg%d" % i, [128, ncols], F32) for i in range(2)]
                h = ncols // 2
                for kc in range(K // 128):
                    s = kc % 2
                    em.dma(stg[s][:], Wd[kc * 128:(kc + 1) * 128, :], writes=[("stg", s)], semkey=("stg", s))
                    em.op("act", lambda kc=kc, s=s: S.copy(out=Wb[:, kc, 0:h], in_=stg[s][:, 0:h]),
                          reads=[("stg", s)], writes=[(name, kc, 0)])
                    em.op("dve", lambda kc=kc, s=s: V.tensor_copy(out=Wb[:, kc, h:ncols], in_=stg[s][:, h:ncols]),
                          reads=[("stg", s)], writes=[(name, kc, 1)])
                em.flush()
            return Wb

        def load_gain(li, st=None, last=False):
            em.dma(gl[:], norm_g[li:li + 1, :].partition_broadcast(128), writes=[("gl",)], semkey="gl")
            if last:
                gfin_box[0] = sbuf(st, "gfin", [128, D], F32)
                em.dma(gfin_box[0][:], fin_g.partition_broadcast(128), writes=[("gfin",)], semkey="c2")

        class Work:
            pass

        def alloc_common(st, w, nslot=2):
            w.nslot = nslot
            w.xt = [sbuf(st, "xt%d" % i, [128, D], F32) for i in range(nslot)]
            w.hb = sbuf(st, "hb", [128, D], BF16)
            w.junk = w.hb
            w.st = [sbuf(st, "stat%d" % i, [128, 4], F32) for i in range(nslot)]
            w.hT = [sbuf(st, "hT%d" % i, [128, 8, 128], BF16) for i in range(min(nslot, 2))]
            w.tp = psum(st, "tp", [128, 16, 128], BF16)

        def load_x(w, t, src):
            s = t % w.nslot
            em.dma(w.xt[s][:], src[t * 128:(t + 1) * 128, :], writes=[("xt", s)], semkey=("xt", s))

        def prologue(w, t, li, src, hdst=None, hkey=None, preloaded=False):
            s = t % w.nslot
            xt, stt = w.xt[s], w.st[s]
            sh = s % len(w.hT)
            hT = w.hT[sh][:] if hdst is None else hdst
            hkey = ("hT", sh) if hkey is None else hkey
            if not preloaded:
                em.dma(xt[:], src[t * 128:(t + 1) * 128, :], writes=[("xt", s)], semkey=("xt", s))
            em.op("act", lambda: S.activation(out=w.junk[:], in_=xt[:], func=AF.Square, accum_out=stt[:, 0:1]),
                  reads=[("xt", s)], writes=[("hb",), ("st", s)])
            em.op("act", lambda: S.activation(out=stt[:, 2:3], in_=stt[:, 0:1], func=AF.Ln, scale=1.0 / D, bias=epsb[:, 0:1]),
                  reads=[("st", s)], writes=[("st", s)])
            em.op("act", lambda: S.activation(out=stt[:, 3:4], in_=stt[:, 2:3], func=AF.Exp, scale=-0.5),
                  reads=[("st", s)], writes=[("st", s)])
            em.op("dve", lambda: V.scalar_tensor_tensor(out=w.hb[:], in0=xt[:], scalar=stt[:, 3:4], in1=gl[:],
                                                        op0=ALU.mult, op1=ALU.mult),
                  reads=[("xt", s), ("st", s)], writes=[("hb",)])
            split_at[0] = len(em.ops)
            for c in range(8):
                em.op("pe", lambda c=c: P.transpose(out=w.tp[:, c, :], in_=w.hb[:, c * 128:(c + 1) * 128], identity=ident[:]),
                      reads=[("hb",)], writes=[("tp",)])
            em.op("dve", lambda: V.tensor_copy(out=hT[:, 0:4, :], in_=w.tp[:, 0:4, :]), reads=[("tp",)], writes=[hkey])
            em.op("dve", lambda: V.tensor_copy(out=hT[:, 4:8, :], in_=w.tp[:, 4:8, :]), reads=[("tp",)], writes=[hkey])
            return s

        def epilogue(w, t, s, gatedT, gkey, Wo, mm, dst, last):
            xt, stt = w.xt[s], w.st[s]
            def gk(m):
                if gkey == "per_group":
                    return [("gatedT", m // 4)]
                return list(gkey) if isinstance(gkey, list) else [gkey]
            for (m0, m1) in ((0, 12), (12, 16)):
                for dh in range(2):
                    bank, bk = mm[dh]
                    for m in range(m0, m1):
                        em.op("pe", lambda m=m, dh=dh, bank=bank: P.matmul(bank[:], lhsT=gatedT[:, m, :],
                                                                          rhs=Wo[:, m, dh * 512:(dh + 1) * 512],
                                                                          start=(m == 0), stop=(m == 15)),
                              reads=gk(m), writes=[bk])
            for dh in range(2):
                bank, bk = mm[dh]
                em.op("dve", lambda dh=dh, bank=bank: V.tensor_tensor(out=xt[:, dh * 512:(dh + 1) * 512], in0=bank[:],
                                                                     in1=xt[:, dh * 512:(dh + 1) * 512], op=ALU.add),
                      reads=[bk, ("xt", s)], writes=[("xt", s)])
            if last:
                em.op("act", lambda: S.activation(out=w.junk[:], in_=xt[:], func=AF.Square, accum_out=stt[:, 0:1]),
                      reads=[("xt", s)], writes=[("hb",) if w.junk is w.hb else ("junk",), ("st", s)])
                em.op("act", lambda: S.activation(out=stt[:, 2:3], in_=stt[:, 0:1], func=AF.Ln, scale=1.0 / D, bias=epsb[:, 0:1]),
                      reads=[("st", s)], writes=[("st", s)])
                em.op("act", lambda: S.activation(out=stt[:, 3:4], in_=stt[:, 2:3], func=AF.Exp, scale=-0.5),
                      reads=[("st", s)], writes=[("st", s)])
                em.op("dve", lambda: V.scalar_tensor_tensor(out=xt[:], in0=xt[:], scalar=stt[:, 3:4], in1=gfin_box[0][:],
                                                            op0=ALU.mult, op1=ALU.mult),
                      reads=[("xt", s), ("st", s)], writes=[("xt", s)])
            em.dma(dst[t * 128:(t + 1) * 128, :], xt[:], reads=[("xt", s)], writes=[("D_x", t)], semkey=("xst", s))

        def dump(name, ap, key):
            if not debug:
                return
            d = nc.dram_tensor("dbg_" + name, list(ap.shape), ap.dtype, kind="ExternalOutput").ap()
            em.dma(d, ap, reads=[key], writes=[("D_dbg", name)], semkey=("dbg", name))

        split_at = [0]

        def capture(fn, *args):
            saved = em.ops
            em.ops = []
            fn(*args)
            out = em.ops
            em.ops = saved
            return out

        def merge(a, b):
            out, i, j = [], 0, 0
            na, nb = len(a), len(b)
            while i < na or j < nb:
                if j >= nb or (i < na and i * nb <= j * na):
                    out.append(a[i]); i += 1
                else:
                    out.append(b[j]); j += 1
            return out

        def pipeline(stage1, stage2, n, prefetch=None):
            if prefetch is not None:
                prefetch(0)
                if n > 1:
                    prefetch(1)
            prev = capture(stage1, 0)
            em.ops.extend(prev)
            for t in range(n):
                if prefetch is not None and t + 2 < n:
                    prefetch(t + 2)
                s2 = capture(stage2, t)
                if t + 1 < n:
                    s1 = capture(stage1, t + 1)
                    k = split_at[0]
                    em.ops.extend(s1[:k])
                    em.ops.extend(merge(s1[k:], s2))
                else:
                    em.ops.extend(s2)

        def bc_mid(ap, n):
            return ap.unsqueeze(1).broadcast_to([ap.shape[0], n, ap.shape[1]])

        def layer_A(li, j, src, dst, last):
            with ExitStack() as st:
                load_gain(li, st, last)
                Wi = load_weights(st, a_w_in[j], A_IN, "Wi")
                Wo = load_weights(st, a_w_out[j], D, "Wo")
                with ExitStack() as s1:
                    w = Work()
                    alloc_common(s1, w)
                    kvfs = [sbuf(s1, "kvf%d" % i, [128, 512], F32) for i in range(2)]
                    cs = [sbuf(s1, "csA%d" % i, [128, 16], F32) for i in range(2)]
                    tmp = sbuf(s1, "ropetmp", [128, 4, 4, 8], F32)
                    ktz = [sbuf(s1, "ktz%d" % i, [128, 8, 128], BF16) for i in range(2)]
                    vau = [sbuf(s1, "vau%d" % i, [128, 8, 128], BF16) for i in range(2)]
                    kTs = [sbuf(s1, "kTs%d" % i, [128, 8, 128], BF16) for i in range(2)]
                    mmb = psum(s1, "mmA1", [128, 512], F32)
                    mmg = [(psum(s1, "mmA1g%d" % i, [128, 512], F32), ("mmg", i)) for i in range(2)]
                    sgs = [sbuf(s1, "sgs%d" % i, [128, 16, 512], BF16) for i in range(2)]
                    hT4 = [sbuf(s1, "hT4_%d" % i, [128, 8, 512], BF16) for i in range(2)]
                    for i in range(2):
                        em.op("pool", lambda i=i: G.memset(ktz[i][:], 0.0), writes=[("ktz", i)])
                        em.op("pool", lambda i=i: G.memset(vau[i][:], 1.0), writes=[("vau", i)])
                    def stage1(b):
                        sb, tl = b // 4, b % 4
                        ss = sb % 2
                        hTv, hk = hT4[ss][:, :, tl * 128:(tl + 1) * 128], ("hT4", ss, tl)
                        s = prologue(w, b, li, src, hTv, hk)
                        em.dma(cs[s][:], tb["ropeA"][b * 128:(b + 1) * 128, :], writes=[("cs", s)], semkey=("cs", s))
                        for kc in range(8):
                            em.op("pe", lambda kc=kc, s=s: P.matmul(mmb[:], lhsT=hTv[:, kc, :], rhs=Wi[:, kc, 2048:2560],
                                                                   start=(kc == 0), stop=(kc == 7)),
                                  reads=[hk], writes=[("mmb",)])
                        em.op("act", lambda: S.copy(out=kvfs[s][:], in_=mmb[:]), reads=[("mmb",)], writes=[("kvf", s)])

                    def stage2(b):
                        s = b % 2
                        kvf = kvfs[s]
                        kv4 = kvf[:, 0:256].rearrange("p (g d) -> p g d", g=4)
                        x1, x2 = kv4[:, :, 0:8], kv4[:, :, 8:16]
                        cosb, sinb = bc_mid(cs[s][:, 0:8], 4), bc_mid(cs[s][:, 8:16], 4)
                        for ti, (xa, cb_) in enumerate(((x1, cosb), (x2, sinb), (x2, cosb), (x1, sinb))):
                            em.op("pool", lambda ti=ti, xa=xa, cb_=cb_: G.tensor_tensor(out=tmp[:, ti], in0=xa, in1=cb_, op=ALU.mult),
                                  reads=[("kvf", s), ("cs", s)], writes=[("tmp", ti)])
                        kz = ktz[s][:].rearrange("p (g r) c -> p g r c", r=2)
                        va = vau[s][:].rearrange("p (g r) c -> p g r c", r=2)
                        for par in range(2):
                            o = par * 64
                            em.op("pool", lambda par=par, o=o, kz=kz: G.tensor_tensor(out=kz[:, :, par, o:o + 8], in0=tmp[:, 0], in1=tmp[:, 1],
                                                                          op=ALU.subtract),
                                  reads=[("tmp", 0), ("tmp", 1)], writes=[("ktz", s)])
                            em.op("pool", lambda par=par, o=o, kz=kz: G.tensor_tensor(out=kz[:, :, par, o + 8:o + 16], in0=tmp[:, 2], in1=tmp[:, 3],
                                                                          op=ALU.add),
                                  reads=[("tmp", 2), ("tmp", 3)], writes=[("ktz", s)])
                            em.op("dve", lambda par=par, o=o, kz=kz: V.tensor_copy(out=kz[:, :, par, o + 16:o + 64], in_=kv4[:, :, 16:64]),
                                  reads=[("kvf", s)], writes=[("ktz", s)])
                            em.op("dve", lambda par=par, o=o, va=va: V.tensor_copy(
                                out=va[:, :, par, o:o + 64], in_=kvf[:, 256:512].rearrange("p (g d) -> p g d", g=4)),
                                  reads=[("kvf", s)], writes=[("vau", s)])
                        for c in range(8):
                            em.op("pe", lambda c=c, s=s: P.transpose(out=w.tp[:, 8 + c, :], in_=ktz[s][:, c, :], identity=ident[:]),
                                  reads=[("ktz", s)], writes=[("tp2",)])
                        em.op("act", lambda s=s: S.copy(out=kTs[s][:], in_=w.tp[:, 8:16, :]), reads=[("tp2",)], writes=[("kTs", s)])
                        em.dma(KT[b].rearrange("p (c t) -> p c t", c=8), kTs[s][:], reads=[("kTs", s)], writes=[("D_KT", b)],
                               semkey=("kTst", s))
                        em.dma(VA[b].rearrange("p (c t) -> p c t", c=8), vau[s][:], reads=[("vau", s)], writes=[("D_VA", b)],
                               semkey=("vast", s))
                        if b % 4 == 3:
                            gate4(b // 4, (b // 4) % 2)

                    def gate4(sb, ss):
                        hks = [("hT4", ss, tl) for tl in range(4)]
                        for m in range(16):
                            bank, bk = mmg[m % 2]
                            for kc in range(8):
                                em.op("pe", lambda kc=kc, m=m, bank=bank: P.matmul(
                                    bank[:], lhsT=Wi[:, kc, 2560 + m * 128:2560 + (m + 1) * 128], rhs=hT4[ss][:, kc, :],
                                    start=(kc == 0), stop=(kc == 7)), reads=hks, writes=[bk])
                            em.op("act", lambda m=m, bank=bank: S.activation(out=sgs[ss][:, m, :], in_=bank[:], func=AF.Silu),
                                  reads=[bk], writes=[("sgs", ss)])
                        for tl in range(4):
                            em.dma(SG[4 * sb + tl].rearrange("p (c t) -> p c t", c=16), sgs[ss][:, :, tl * 128:(tl + 1) * 128],
                                   reads=[("sgs", ss)], writes=[("D_SG", sb, tl)], semkey=("sgst", ss))

                    pipeline(stage1, stage2, NT)
                    em.flush()
                with ExitStack() as s2:
                    w = Work()
                    alloc_common(s2, w, 3)
                    if last:
                        w.junk = sbuf(s2, "junkA", [128, D], BF16)
                    sk = sbuf(s2, "sinkbc", [128, 32], F32)
                    sinksm = sbuf(s2, "sinksm", [128, 8, 4], F32)
                    flags = sbuf(s2, "flagsA", [128, 2, NT], F32)
                    mask = sbuf(s2, "maskLR", [128, 2, 128], BF16)
                    maskb = [sbuf(s2, "maskb%d" % i, [128, 2, 128], BF16) for i in range(2)]
                    cs = [sbuf(s2, "csA%d" % i, [128, 16], F32) for i in range(2)]
                    kTl = [[sbuf(s2, "kTl%d_%d" % (i, jj), [128, 8, 128], BF16) for jj in range(3)] for i in range(2)]
                    val = [[sbuf(s2, "val%d_%d" % (i, jj), [128, 8, 128], BF16) for jj in range(3)] for i in range(2)]
                    qf = sbuf(s2, "qf", [128, 32, 16], F32)
                    tmp = sbuf(s2, "ropetmpq", [128, 4, 32, 8], F32)
                    qb = sbuf(s2, "qb", [128, E], BF16)
                    qT = [sbuf(s2, "qT%d" % i, [128, 16, 128], BF16) for i in range(2)]
                    sgT = [sbuf(s2, "sgT%d" % i, [128, 16, 128], BF16) for i in range(2)]
                    pex = [sbuf(s2, "pex%d" % i, [128, 4, 128], BF16) for i in range(10)]
                    rec = [sbuf(s2, "rec%d" % i, [128, 512], F32) for i in range(2)]
                    onrm = rec
                    gatedT = sbuf(s2, "gatedT", [128, 16, 128], BF16)
                    prow = sbuf(s2, "prow", [128, 8, 4], BF16)
                    sel = sbuf(s2, "sel", [128, 2, 128], BF16)
                    mm = [(psum(s2, "mmA%d" % i, [128, 512], F32), ("mm", i)) for i in range(6)]
                    em.dma(sk[:], a_sink[j:j + 1, :].partition_broadcast(128), writes=[("sk",)], semkey="sk")
                    em.dma(flags[:], tb["flagsA"], writes=[("flags",)], semkey="fl")
                    em.dma(mask[:], tb["maskLR"], writes=[("mask",)], semkey="mk")
                    em.op("act", lambda: S.activation(out=sk[:], in_=sk[:], func=AF.Exp), reads=[("sk",)], writes=[("sk",)])
                    em.op("pool", lambda: G.tensor_copy(out=sinksm[:].rearrange("p (g r) c -> p g r c", r=2),
                                                        in_=sk[:].rearrange("p (g c r) -> p g r c", g=4, c=4, r=2)),
                          reads=[("sk",)], writes=[("sinksm",)])
                    em.op("pool", lambda: G.memset(prow[:], 0.0), writes=[("prow",)])
                    em.op("pool", lambda: G.memset(sel[:], 0.0), writes=[("sel",)])
                    em.op("pool", lambda: G.memset(sel[:, 0, 64:128], 1.0), reads=[("sel",)], writes=[("sel",)])
                    em.op("pool", lambda: G.memset(sel[:, 1, 0:64], 1.0), reads=[("sel",)], writes=[("sel",)])
                    for gp in range(8):
                        em.op("pool", lambda gp=gp: G.tensor_copy(out=prow[0:1, gp, :], in_=sinksm[0:1, gp, :]),
                              reads=[("prow",), ("sinksm",)], writes=[("prow",)])
                    pexi = [0]

                    def stage1(b):
                        sx = prologue(w, b, li, src, preloaded=True)
                        s = b % 2
                        em.dma(cs[s][:], tb["ropeA"][b * 128:(b + 1) * 128, :], writes=[("cs", s)], semkey=("cs", s))
                        js = [jj for jj in range(3) if 0 <= b - 1 + jj < NT]
                        for side in range(2):
                            em.op("pool", lambda side=side: G.tensor_scalar(out=maskb[s][:, side, :], in0=mask[:, side, :],
                                                                            scalar1=flags[:, side, b:b + 1], scalar2=None, op0=ALU.mult),
                                  reads=[("mask",), ("flags",)], writes=[("maskb", s, side)])
                        for jj in js:
                            bb = b - 1 + jj
                            em.dma(kTl[s][jj][:], KT[bb].rearrange("p (c t) -> p c t", c=8), writes=[("kTl", s, jj)],
                                   semkey=("kTl", s, jj))
                            em.dma(val[s][jj][:], VA[bb].rearrange("p (c t) -> p c t", c=8), writes=[("val", s, jj)],
                                   semkey=("val", s, jj))
                        for cb in range(4):
                            bank, bk = mm[0]
                            for kc in range(8):
                                em.op("pe", lambda kc=kc, cb=cb, bank=bank: P.matmul(bank[:], lhsT=w.hT[sx % 2][:, kc, :],
                                                                                   rhs=Wi[:, kc, cb * 512:(cb + 1) * 512],
                                                                                   start=(kc == 0), stop=(kc == 7)),
                                      reads=[("hT", sx % 2)], writes=[bk])
                            em.op("dve", lambda cb=cb, bank=bank: V.tensor_scalar(out=qb[:, cb * 512:(cb + 1) * 512], in0=bank[:],
                                                                                 scalar1=0.125, scalar2=None, op0=ALU.mult),
                                  reads=[bk], writes=[("qb", cb)])
                            em.op("dve", lambda cb=cb, bank=bank: V.tensor_scalar(
                                out=qf[:, cb * 8:(cb + 1) * 8, :], in0=bank[:].rearrange("p (h d) -> p h d", h=8)[:, :, 0:16],
                                scalar1=0.125, scalar2=None, op0=ALU.mult), reads=[bk], writes=[("qf", cb)])
                        qb3 = qb[:].rearrange("p (h d) -> p h d", h=32)
                        x1, x2 = qf[:, :, 0:8], qf[:, :, 8:16]
                        cosb, sinb = bc_mid(cs[s][:, 0:8], 32), bc_mid(cs[s][:, 8:16], 32)
                        qfk = [("qf", cb) for cb in range(4)]
                        qbk = [("qb", cb) for cb in range(4)]
                        for ti, (xa, cb_) in enumerate(((x1, cosb), (x2, sinb), (x2, cosb), (x1, sinb))):
                            em.op("dve", lambda ti=ti, xa=xa, cb_=cb_: V.tensor_tensor(out=tmp[:, ti], in0=xa, in1=cb_, op=ALU.mult),
                                  reads=qfk + [("cs", s)], writes=[("tmpq", ti)])
                        em.op("dve", lambda: V.tensor_tensor(out=qb3[:, :, 0:8], in0=tmp[:, 0], in1=tmp[:, 1], op=ALU.subtract),
                              reads=[("tmpq", 0), ("tmpq", 1)], writes=qbk)
                        em.op("dve", lambda: V.tensor_tensor(out=qb3[:, :, 8:16], in0=tmp[:, 2], in1=tmp[:, 3], op=ALU.add),
                              reads=[("tmpq", 2), ("tmpq", 3)], writes=qbk)
                        for m in range(16):
                            em.op("pe", lambda m=m: P.transpose(out=w.tp[:, m, :], in_=qb[:, m * 128:(m + 1) * 128], identity=ident[:]),
                                  reads=[("qb", m // 4)], writes=[("tp" if m < 8 else "tp2",)])
                        em.op("dve", lambda: V.tensor_copy(out=qT[s][:, 0:8, :], in_=w.tp[:, 0:8, :]), reads=[("tp",)], writes=[("qT", s, 0)])
                        em.op("dve", lambda: V.tensor_copy(out=qT[s][:, 8:16, :], in_=w.tp[:, 8:16, :]), reads=[("tp2",)],
                              writes=[("qT", s, 1)])
                        em.dma(sgT[s][:], SG[b].rearrange("p (c t) -> p c t", c=16), writes=[("sgT", s)], semkey=("sgT", s))

                    def stage2(b):
                        s = b % 2
                        js = [jj for jj in range(3) if 0 <= b - 1 + jj < NT]
                        tiles = {}

                        def scores(gp):
                            g = gp // 2
                            pl = []
                            for jj in js:
                                bank, bk = mm[1 + (pexi[0] % 3)]
                                px = pex[pexi[0] % len(pex)]
                                pk = ("pex", pexi[0] % len(pex))
                                pexi[0] += 1
                                em.op("pe", lambda bank=bank, jj=jj: P.matmul(
                                    bank[:], lhsT=kTl[s][jj][:, gp, :], rhs=qT[s][:, 4 * g:4 * g + 4, :], start=True, stop=True),
                                      reads=[("kTl", s, jj), ("qT", s, g // 2)], writes=[bk])
                                em.op("act", lambda bank=bank, px=px: S.activation(out=px[:].rearrange("p c t -> p (c t)"), in_=bank[:],
                                                                                    func=AF.Exp),
                                      reads=[bk], writes=[pk])
                                if jj != 1:
                                    side = 0 if jj == 0 else 1
                                    if side == 0:
                                        em.op("dve", lambda px=px, side=side: V.tensor_tensor(
                                            out=px[:], in0=px[:], in1=bc_mid(maskb[s][:, side, :], 4), op=ALU.mult),
                                              reads=[pk, ("maskb", s, side)], writes=[pk])
                                    else:
                                        em.op("pool", lambda px=px, side=side: G.tensor_tensor(
                                            out=px[:], in0=px[:], in1=bc_mid(maskb[s][:, side, :], 4), op=ALU.mult),
                                              reads=[pk, ("maskb", s, side)], writes=[pk])
                                pl.append((jj, px, pk))
                            tiles[gp] = pl

                        def pv_norm(gp):
                            g, par = gp // 2, gp % 2
                            pl = tiles[gp]
                            bank, bk = mm[4 + (gp % 2)]
                            for n_, (jj, px, pk) in enumerate(pl):
                                em.op("pe", lambda bank=bank, jj=jj, px=px, n_=n_: P.matmul(
                                    bank[:], lhsT=val[s][jj][:, gp, :], rhs=px[:].rearrange("p c t -> p (c t)"),
                                    start=(n_ == 0), stop=False),
                                      reads=[("val", s, jj), pk], writes=[bk])
                            em.op("pe", lambda bank=bank: P.matmul(bank[:], lhsT=sel[:, par, :],
                                                                   rhs=prow[:, gp, :].unsqueeze(2).broadcast_to([128, 4, 128]),
                                                                   start=False, stop=True), writes=[bk])
                            nr = slice(par * 64, par * 64 + 64)
                            dr = slice((1 - par) * 64, (1 - par) * 64 + 64)
                            rc, on = rec[gp % 2], onrm[gp % 2]
                            rk, ok_ = ("rec", gp % 2), ("onrm", gp % 2)
                            em.op("act", lambda: S.activation(out=rc[dr, :], in_=bank[dr, :], func=AF.Ln), reads=[bk], writes=[rk])
                            em.op("act", lambda: S.activation(out=rc[dr, :], in_=rc[dr, :], func=AF.Exp, scale=-1.0),
                                  reads=[rk], writes=[rk])
                            em.op("dve", lambda: V.tensor_tensor(out=on[nr, :], in0=bank[nr, :], in1=rc[dr, :], op=ALU.mult),
                                  reads=[bk, rk], writes=[ok_])
                            em.op("pool", lambda: G.tensor_tensor(
                                out=gatedT[nr, 4 * g:4 * g + 4, :].rearrange("p c t -> p (c t)"), in0=on[nr, :],
                                in1=sgT[s][nr, 4 * g:4 * g + 4, :].rearrange("p c t -> p (c t)"), op=ALU.mult),
                                  reads=[ok_, ("sgT", s)], writes=[("gatedT", g)])

                        LAG = 2
                        for gp in range(8 + LAG):
                            if gp < 8:
                                scores(gp)
                            if gp >= LAG:
                                pv_norm(gp - LAG)
                        epilogue(w, b, b % 3, gatedT, "per_group", Wo, [mm[4], mm[5]], dst, last)

                    pipeline(stage1, stage2, NT, prefetch=lambda t: load_x(w, t, src))
                    em.flush()

        def layer_B(li, j, src, dst, last):
            with ExitStack() as st:
                load_gain(li, st, last)
                lg = sbuf(st, "lg", [128, 8], F32)
                kd16 = sbuf(st, "kd16", [128, 8], F32)
                cdt = sbuf(st, "cdt", [128, 8], F32)
                Dm = sbuf(st, "Dm", [128, 4, 128], F32)
                qdf = sbuf(st, "qdf", [128, 4, 128], F32)
                qdb = sbuf(st, "qdb", [128, 4, 128], F32)
                flg = sbuf(st, "flagsB", [128, 2, NT], F32)
                Sm = sbuf(st, "Sm", [128, 8, 512], F32)
                Sbf = sbuf(st, "Sbf", [128, 2, 2, 512], BF16)
                with ExitStack() as s0:
                    retT = sbuf(s0, "retT", [128, 4, 128], F32)
                    retP = sbuf(s0, "retP", [128, 2], F32)
                    t1 = sbuf(s0, "rt1", [128, 128], F32)
                    em.dma(lg[:], b_decay[j:j + 1, :].partition_broadcast(128), writes=[("lg",)], semkey="lg")
                    em.dma(retT[:], tb["retT"], writes=[("retT",)], semkey="retT")
                    em.dma(retP[:], tb["retP"], writes=[("retP",)], semkey="retP")
                    em.dma(flg[:], tb["flagsB"], writes=[("flg",)], semkey="flg")
                    em.op("act", lambda: S.activation(out=lg[:], in_=lg[:], func=AF.Exp, scale=-1.0), reads=[("lg",)], writes=[("lg",)])
                    em.op("dve", lambda: V.tensor_scalar(out=lg[:], in0=lg[:], scalar1=1.0, scalar2=None, op0=ALU.add),
                          reads=[("lg",)], writes=[("lg",)])
                    em.op("act", lambda: S.activation(out=lg[:], in_=lg[:], func=AF.Ln), reads=[("lg",)], writes=[("lg",)])
                    em.op("dve", lambda: V.tensor_scalar(out=lg[:], in0=lg[:], scalar1=-1.0, scalar2=None, op0=ALU.mult),
                          reads=[("lg",)], writes=[("lg",)])
                    em.op("act", lambda: S.activation(out=kd16[:, 0:4], in_=lg[:, 0:4], func=AF.Exp, scale=retP[:, 0:1]),
                          reads=[("lg",), ("retP",)], writes=[("kd16", 0)])
                    em.op("act", lambda: S.activation(out=kd16[:, 4:8], in_=lg[:, 4:8], func=AF.Exp, scale=retP[:, 1:2]),
                          reads=[("lg",), ("retP",)], writes=[("kd16", 1)])
                    em.op("dve", lambda: V.tensor_scalar(out=kd16[:], in0=kd16[:], scalar1=1.0 / 16, scalar2=None, op0=ALU.mult),
                          reads=[("kd16", 0), ("kd16", 1)], writes=[("kd16", 2)])
                    em.op("act", lambda: S.activation(out=cdt[:], in_=lg[:], func=AF.Exp, scale=128.0), reads=[("lg",)], writes=[("cdt",)])
                    for h in range(4):
                        em.op("act", lambda h=h: S.activation(out=qdf[:, h, :], in_=retT[:, 2, :], func=AF.Exp, scale=lg[:, h:h + 1]),
                              reads=[("lg",), ("retT",)], writes=[("qdf", h)])
                        em.op("act", lambda h=h: S.activation(out=qdb[:, h, :], in_=retT[:, 3, :], func=AF.Exp, scale=lg[:, 4 + h:5 + h]),
                              reads=[("lg",), ("retT",)], writes=[("qdb", h)])
                        em.op("dve", lambda h=h: V.tensor_scalar(out=t1[:], in0=retT[:, 0, :], scalar1=lg[:, h:h + 1], scalar2=None,
                                                                 op0=ALU.mult), reads=[("lg",), ("retT",)], writes=[("rt1",)])
                        em.op("dve", lambda h=h: V.scalar_tensor_tensor(out=t1[:], in0=retT[:, 1, :], scalar=lg[:, 4 + h:5 + h], in1=t1[:],
                                                                        op0=ALU.mult, op1=ALU.add),
                              reads=[("lg",), ("retT",), ("rt1",)], writes=[("rt1",)])
                        em.op("act", lambda h=h: S.activation(out=Dm[:, h, :], in_=t1[:], func=AF.Exp), reads=[("rt1",)], writes=[("Dm", h)])
                        em.op("dve", lambda h=h: V.tensor_scalar(out=Dm[:, h, :], in0=Dm[:, h, :], scalar1=1.0 / 16, scalar2=None,
                                                                 op0=ALU.mult), reads=[("Dm", h)], writes=[("Dm", h)])
                    em.op("pool", lambda: G.memset(Sm[:], 0.0), writes=[("Sm", i) for i in range(8)])
                    em.flush()

                def rope_half(x3, csb, nh, ta, tb_, tc, td):
                    x1, x2 = x3[:, :, 0:128], x3[:, :, 128:256]
                    cosb, sinb = bc_mid(csb[:, 0:128], nh), bc_mid(csb[:, 128:256], nh)
                    return x1, x2, cosb, sinb

                def state_update(kk, vb, direction, c, cdk, mmu):
                    for h in range(4):
                        for dc in range(2):
                            i = h * 2 + dc
                            bank, bk = mmu[i % len(mmu)]
                            em.op("pe", lambda h=h, dc=dc, bank=bank: P.matmul(bank[:], lhsT=kk[:, h, dc * 128:(dc + 1) * 128],
                                                                              rhs=vb[:, h * 512:(h + 1) * 512], start=True, stop=True),
                                  reads=[("ksc",), ("vbf",)], writes=[bk])
                            em.op("dve", lambda h=h, i=i, bank=bank: V.scalar_tensor_tensor(out=Sm[:, i, :], in0=Sm[:, i, :],
                                                                                        scalar=cdk[:, h:h + 1], in1=bank[:],
                                                                                        op0=ALU.mult, op1=ALU.add),
                                  reads=[bk, ("Sm", i), ("cdk",)], writes=[("Sm", i)])

                if bstop == 1:
                    return
                with ExitStack() as s1:
                    Wi = load_weights(s1, b_w_in[j][:, 1024:6144], 5120, "WiB1")
                    w = Work()
                    alloc_common(s1, w, 2)
                    cs = [sbuf(s1, "csB%d" % i, [128, 256], F32) for i in range(2)]
                    kf = [sbuf(s1, "kf%d" % i, [128, 4, 256], F32) for i in range(2)]
                    tt = [sbuf(s1, "rtmp%d" % i, [128, 4, 128], F32) for i in range(4)]
                    ksc = sbuf(s1, "ksc", [128, 4, 256], BF16)
                    vbf = [sbuf(s1, "vbf%d" % i, [128, E], BF16) for i in range(2)]
                    sgb = [sbuf(s1, "sgb%d" % i, [128, E], BF16) for i in range(2)]
                    kbs = sbuf(s1, "kbs", [128, 4, 256], BF16)
                    sck = sbuf(s1, "sck", [128, 8], F32)
                    mm = [(psum(s1, "mmB%d" % i, [128, 512], F32), ("mm", i)) for i in range(6)]

                    def stage1(t):
                        c = NT - 1 - t
                        s = t % 2
                        sx = prologue(w, c, li, src)
                        hTc, hk = w.hT[sx], ("hT", sx)
                        em.dma(cs[s][:], tb["ropeB"][c * 128:(c + 1) * 128, :], writes=[("cs", s)], semkey=("cs", s))
                        nb = [0]

                        def proj(col):
                            bank, bk = mm[nb[0] % 3]
                            nb[0] += 1
                            for kc in range(8):
                                em.op("pe", lambda kc=kc, bank=bank: P.matmul(bank[:], lhsT=hTc[:, kc, :], rhs=Wi[:, kc, col:col + 512],
                                                                              start=(kc == 0), stop=(kc == 7)),
                                      reads=[hk], writes=[bk])
                            return bank, bk
                        for cb in range(2):
                            bank, bk = proj(cb * 512)
                            em.op("act", lambda cb=cb, bank=bank: S.copy(out=kf[s][:, 2 * cb:2 * cb + 2, :].rearrange("p h d -> p (h d)"),
                                                                        in_=bank[:]), reads=[bk], writes=[("kf", s, cb)])
                        for cb in range(4):
                            bank, bk = proj(1024 + cb * 512)
                            if cb % 2 == 0:
                                em.op("act", lambda cb=cb, bank=bank: S.copy(out=vbf[s][:, cb * 512:(cb + 1) * 512], in_=bank[:]),
                                      reads=[bk], writes=[("vbf", s)])
                            else:
                                em.op("dve", lambda cb=cb, bank=bank: V.tensor_copy(out=vbf[s][:, cb * 512:(cb + 1) * 512], in_=bank[:]),
                                      reads=[bk], writes=[("vbf", s)])
                        for cb in range(4):
                            bank, bk = proj(3072 + cb * 512)
                            em.op("act", lambda cb=cb, bank=bank: S.activation(out=sgb[s][:, cb * 512:(cb + 1) * 512], in_=bank[:],
                                                                                func=AF.Silu), reads=[bk], writes=[("sgb", s)])

                    def stage2(t):
                        c = NT - 1 - t
                        s = t % 2
                        x1, x2, cosb, sinb = rope_half(kf[s][:], cs[s], 4, *[None] * 4)
                        kfk = [("kf", s, 0), ("kf", s, 1), ("cs", s)]
                        em.op("pool", lambda: G.tensor_tensor(out=tt[0][:], in0=x1, in1=cosb, op=ALU.mult), reads=kfk, writes=[("tt", 0)])
                        em.op("pool", lambda: G.tensor_tensor(out=tt[1][:], in0=x2, in1=sinb, op=ALU.mult), reads=kfk, writes=[("tt", 1)])
                        em.op("dve", lambda: V.tensor_tensor(out=tt[2][:], in0=x2, in1=cosb, op=ALU.mult), reads=kfk, writes=[("tt", 2)])
                        em.op("dve", lambda: V.tensor_tensor(out=tt[3][:], in0=x1, in1=sinb, op=ALU.mult), reads=kfk, writes=[("tt", 3)])
                        em.op("pool", lambda: G.tensor_tensor(out=tt[0][:], in0=tt[0][:], in1=tt[1][:], op=ALU.subtract),
                              reads=[("tt", 0), ("tt", 1)], writes=[("tt", 0)])
                        em.op("dve", lambda: V.tensor_tensor(out=tt[2][:], in0=tt[2][:], in1=tt[3][:], op=ALU.add),
                              reads=[("tt", 2), ("tt", 3)], writes=[("tt", 2)])
                        em.op("dve", lambda: V.tensor_scalar(out=sck[:, 0:4], in0=kd16[:, 4:8], scalar1=flg[:, 1, c:c + 1], scalar2=None,
                                                             op0=ALU.mult), writes=[("sck",)])
                        em.op("dve", lambda: V.tensor_scalar(out=sck[:, 4:8], in0=cdt[:, 4:8], scalar1=flg[:, 1, c:c + 1], scalar2=None,
                                                             op0=ALU.mult), writes=[("cdk",)])
                        em.op("act", lambda: S.copy(out=kbs[:, :, 0:128], in_=tt[0][:]), reads=[("tt", 0)], writes=[("kbs",)])
                        em.op("act", lambda: S.copy(out=kbs[:, :, 128:256], in_=tt[2][:]), reads=[("tt", 2)], writes=[("kbs",)])
                        em.dma(KB[c].rearrange("p (h d) -> p h d", h=4), kbs[:], reads=[("kbs",)], writes=[("D_KB", c)], semkey="kbst")
                        sb_ = sck[:, 0:4].unsqueeze(2).broadcast_to([128, 4, 128])
                        em.op("pool", lambda: G.tensor_tensor(out=ksc[:, :, 0:128], in0=tt[0][:], in1=sb_, op=ALU.mult),
                              reads=[("tt", 0), ("sck",)], writes=[("ksc",)])
                        em.op("dve", lambda: V.tensor_tensor(out=ksc[:, :, 128:256], in0=tt[2][:], in1=sb_, op=ALU.mult),
                              reads=[("tt", 2), ("sck",)], writes=[("ksc",)])
                        em.dma(SG[c], sgb[s][:], reads=[("sgb", s)], writes=[("D_SG", c)], semkey=("sgbst", s))
                        em.dma(VB[c], vbf[s][:], reads=[("vbf", s)], writes=[("D_VB", c)], semkey=("vbst", s))
                        for h in range(4):
                            em.op("act", lambda h=h: S.copy(out=Sbf[:, h % 2], in_=Sm[:, 2 * h:2 * h + 2, :]),
                                  reads=[("Sm", 2 * h), ("Sm", 2 * h + 1)], writes=[("Sbf", h % 2)])
                            em.dma(Sscr[c][:, h * 1024:(h + 1) * 1024].rearrange("p (i e) -> p i e", i=2), Sbf[:, h % 2],
                                   reads=[("Sbf", h % 2)], writes=[("D_S", c, h)], semkey=("Sst", h % 2))
                        for h in range(4):
                            for dc in range(2):
                                i = h * 2 + dc
                                bank, bk = mm[4 + i % 2]
                                em.op("pe", lambda h=h, dc=dc, bank=bank: P.matmul(bank[:], lhsT=ksc[:, h, dc * 128:(dc + 1) * 128],
                                                                                  rhs=vbf[s][:, h * 512:(h + 1) * 512], start=True, stop=True),
                                      reads=[("ksc",), ("vbf", s)], writes=[bk])
                                em.op("dve", lambda h=h, i=i, bank=bank: V.scalar_tensor_tensor(out=Sm[:, i, :], in0=Sm[:, i, :],
                                                                                            scalar=sck[:, 4 + h:5 + h], in1=bank[:],
                                                                                            op0=ALU.mult, op1=ALU.add),
                                      reads=[bk, ("Sm", i), ("cdk",)], writes=[("Sm", i)])

                    pipeline(stage1, stage2, NT)
                    em.flush()

                if bstop == 2:
                    return
                with ExitStack() as s2:
                    Wi = load_weights(s2, b_w_in[j][:, 0:1024], 1024, "WiB2")
                    Wo = load_weights(s2, b_w_out[j], D, "WoB")
                    w = Work()
                    alloc_common(s2, w, 3)
                    gjunk = sbuf(s2, "junkB", [128, 512], BF16)
                    cs = [sbuf(s2, "csB%d" % i, [128, 256], F32) for i in range(2)]
                    qkf = sbuf(s2, "qkf", [128, 4, 256], F32)
                    tt = [sbuf(s2, "rtmp%d" % i, [128, 4, 128], F32) for i in range(4)]
                    qbf = sbuf(s2, "qbf", [128, 4, 256], BF16)
                    kbf = [sbuf(s2, "kbf%d" % i, [128, 4, 256], BF16) for i in range(2)]
                    ksc = [sbuf(s2, "ksc%d" % i, [128, 4, 256], BF16) for i in range(2)]
                    qTs = [sbuf(s2, "qTs%d" % i, [128, 4, 8, 128], BF16) for i in range(2)]
                    vbf = [sbuf(s2, "vbf%d" % i, [128, E], BF16) for i in range(2)]
                    sgh = [sbuf(s2, "sgh%d" % i, [128, 512], BF16) for i in range(2)]
                    PT = [sbuf(s2, "PT%d" % i, [128, 128], BF16) for i in range(4)]
                    gtok = sbuf(s2, "gtok", [128, E], BF16)
                    Sbl = sbuf(s2, "Sbl", [128, 2, 2, 512], BF16)
                    sck = [sbuf(s2, "sck%d" % i, [128, 8], F32) for i in range(2)]
                    gst = sbuf(s2, "gst", [128, 4, 4], F32)
                    mm = [(psum(s2, "mmB%d" % i, [128, 512], F32), ("mm", i)) for i in range(5)]
                    tpB = psum(s2, "tpB", [128, 8, 128], BF16)
                    em.op("pool", lambda: G.memset(Sm[:], 0.0), writes=[("Sm", i) for i in range(8)])

                    def stage1(c):
                        sx = prologue(w, c, li, src)
                        s = c % 2
                        hTc, hk = w.hT[sx % 2], ("hT", sx % 2)
                        em.dma(cs[s][:], tb["ropeB"][c * 128:(c + 1) * 128, :], writes=[("cs", s)], semkey=("cs", s))
                        em.op("dve", lambda: V.tensor_scalar(out=sck[s][:, 0:4], in0=kd16[:, 0:4], scalar1=flg[:, 0, c:c + 1], scalar2=None,
                                                             op0=ALU.mult), writes=[("sck", s)])
                        em.op("dve", lambda: V.tensor_scalar(out=sck[s][:, 4:8], in0=cdt[:, 0:4], scalar1=flg[:, 0, c:c + 1], scalar2=None,
                                                             op0=ALU.mult), writes=[("cdk", s)])
                        for cb in range(2):
                            bank, bk = mm[0]
                            col = cb * 512
                            for kc in range(8):
                                em.op("pe", lambda kc=kc, col=col, bank=bank: P.matmul(bank[:], lhsT=hTc[:, kc, :],
                                                                                     rhs=Wi[:, kc, col:col + 512],
                                                                                     start=(kc == 0), stop=(kc == 7)),
                                      reads=[hk], writes=[bk])
                            em.op("act", lambda cb=cb, bank=bank: S.copy(out=qkf[:, 2 * cb:2 * cb + 2, :].rearrange("p h d -> p (h d)"),
                                                                        in_=bank[:]), reads=[bk], writes=[("qkf", cb)])
                        x1, x2, cosb, sinb = rope_half(qkf[:], cs[s], 4, *[None] * 4)
                        kfk = [("qkf", 0), ("qkf", 1), ("cs", s)]
                        em.op("pool", lambda: G.tensor_tensor(out=tt[0][:], in0=x1, in1=cosb, op=ALU.mult), reads=kfk, writes=[("tt", 0)])
                        em.op("pool", lambda: G.tensor_tensor(out=tt[1][:], in0=x2, in1=sinb, op=ALU.mult), reads=kfk, writes=[("tt", 1)])
                        em.op("dve", lambda: V.tensor_tensor(out=tt[2][:], in0=x2, in1=cosb, op=ALU.mult), reads=kfk, writes=[("tt", 2)])
                        em.op("dve", lambda: V.tensor_tensor(out=tt[3][:], in0=x1, in1=sinb, op=ALU.mult), reads=kfk, writes=[("tt", 3)])
                        em.op("pool", lambda: G.tensor_tensor(out=qbf[:, :, 0:128], in0=tt[0][:], in1=tt[1][:], op=ALU.subtract),
                              reads=[("tt", 0), ("tt", 1)], writes=[("qbf",)])
                        em.op("dve", lambda: V.tensor_tensor(out=qbf[:, :, 128:256], in0=tt[2][:], in1=tt[3][:], op=ALU.add),
                              reads=[("tt", 2), ("tt", 3)], writes=[("qbf",)])
                        em.dma(kbf[s][:], KB[c].rearrange("p (h d) -> p h d", h=4), writes=[("kbf", s)], semkey=("kbf", s))
                        em.dma(vbf[s][:], VB[c], writes=[("vbf", s)], semkey=("vbf", s))
                        em.op("pool", lambda: G.tensor_tensor(out=ksc[s][:], in0=kbf[s][:],
                                                              in1=sck[s][:, 0:4].unsqueeze(2).broadcast_to([128, 4, 256]), op=ALU.mult),
                              reads=[("kbf", s), ("sck", s)], writes=[("ksc", s)])
                        qb2 = qbf[:].rearrange("p h d -> p (h d)")
                        kb2 = kbf[s][:].rearrange("p h d -> p (h d)")
                        for m in range(8):
                            em.op("pe", lambda m=m: P.transpose(out=w.tp[:, m, :], in_=qb2[:, m * 128:(m + 1) * 128], identity=ident[:]),
                                  reads=[("qbf",)], writes=[("tp",)])
                        for m in range(8):
                            em.op("pe", lambda m=m: P.transpose(out=w.tp[:, 8 + m, :], in_=kb2[:, m * 128:(m + 1) * 128], identity=ident[:]),
                                  reads=[("kbf", s)], writes=[("tp2",)])
                        q4 = qTs[s]
                        em.op("act", lambda: S.copy(out=q4[:, 0], in_=w.tp[:, 0:8, :]), reads=[("tp",)], writes=[("qT", s)])
                        for h in range(4):
                            em.op("dve", lambda h=h: V.tensor_tensor(out=q4[:, 1, 2 * h:2 * h + 2, :], in0=q4[:, 0, 2 * h:2 * h + 2, :],
                                                                     in1=bc_mid(qdf[:, h, :], 2), op=ALU.mult),
                                  reads=[("qT", s)], writes=[("qTf", s)])
                            em.op("pool", lambda h=h: G.tensor_tensor(out=q4[:, 2, 2 * h:2 * h + 2, :], in0=q4[:, 0, 2 * h:2 * h + 2, :],
                                                                      in1=bc_mid(qdb[:, h, :], 2), op=ALU.mult),
                                  reads=[("qT", s)], writes=[("qTb", s)])
                        em.op("act", lambda: S.copy(out=q4[:, 3], in_=w.tp[:, 8:16, :]), reads=[("tp2",)], writes=[("kT", s)])

                    def stage2(c):
                        s = c % 2
                        q4 = qTs[s]
                        gatedT = q4[:, 0:2].rearrange("p a c t -> p (a c) t")
                        sb, sk_ = mm[1]
                        for h in range(4):
                            for dc in range(2):
                                em.op("pe", lambda dc=dc, h=h: P.matmul(sb[:, h * 128:(h + 1) * 128], lhsT=q4[:, 3, h * 2 + dc, :],
                                                                        rhs=q4[:, 0, h * 2 + dc, :], start=(dc == 0), stop=(dc == 1)),
                                      reads=[("kT", s), ("qT", s)], writes=[sk_])
                        for h in range(4):
                            em.op("dve", lambda h=h: V.tensor_tensor(out=PT[h][:], in0=sb[:, h * 128:(h + 1) * 128], in1=Dm[:, h, :],
                                                                     op=ALU.mult), reads=[sk_], writes=[("PT", h)])
                        for h in range(4):
                            em.dma(sgh[h % 2][:], SG[c][:, h * 512:(h + 1) * 512], writes=[("sgh", h % 2)], semkey=("sgh", h % 2))
                            em.op("act", lambda h=h: S.copy(out=Sbf[:, h % 2], in_=Sm[:, 2 * h:2 * h + 2, :]),
                                  reads=[("Sm", 2 * h), ("Sm", 2 * h + 1)], writes=[("Sbf", h % 2)])
                            em.dma(Sbl[:, h % 2], Sscr[c][:, h * 1024:(h + 1) * 1024].rearrange("p (i e) -> p i e", i=2),
                                   writes=[("Sbl", h % 2)], semkey=("Sbl", h % 2))
                            pt = PT[h]
                            ob, ok_ = mm[2 + h % 2]
                            em.op("pe", lambda h=h, ob=ob, pt=pt: P.matmul(ob[:], lhsT=pt[:], rhs=vbf[s][:, h * 512:(h + 1) * 512],
                                                                           start=True, stop=False),
                                  reads=[("PT", h), ("vbf", s)], writes=[ok_])
                            for dc in range(2):
                                i = h * 2 + dc
                                em.op("pe", lambda i=i, ob=ob, h=h, dc=dc: P.matmul(ob[:], lhsT=q4[:, 1, i, :], rhs=Sbf[:, h % 2, dc, :],
                                                                               start=False, stop=False),
                                      reads=[("qTf", s), ("Sbf", h % 2)], writes=[ok_])
                            for dc in range(2):
                                i = h * 2 + dc
                                em.op("pe", lambda i=i, ob=ob, dc=dc, h=h: P.matmul(ob[:], lhsT=q4[:, 2, i, :], rhs=Sbl[:, h % 2, dc, :],
                                                                               start=False, stop=(dc == 1)),
                                      reads=[("qTb", s), ("Sbl", h % 2)], writes=[ok_])
                            gs = gst[:, h, :]
                            em.op("act", lambda ob=ob, gs=gs: S.activation(out=gjunk[:], in_=ob[:], func=AF.Square, accum_out=gs[:, 0:1]),
                                  reads=[ok_], writes=[("gjunk",), ("gst", h)])
                            em.op("act", lambda gs=gs: S.activation(out=gs[:, 2:3], in_=gs[:, 0:1], func=AF.Ln, scale=1.0 / 512,
                                                                    bias=epsb[:, 0:1]),
                                  reads=[("gst", h)], writes=[("gst", h)])
                            em.op("act", lambda gs=gs: S.activation(out=gs[:, 3:4], in_=gs[:, 2:3], func=AF.Exp, scale=-0.5),
                                  reads=[("gst", h)], writes=[("gst", h)])
                            em.op("dve", lambda h=h, ob=ob, gs=gs: V.scalar_tensor_tensor(out=gtok[:, h * 512:(h + 1) * 512], in0=ob[:],
                                                                                        scalar=gs[:, 3:4], in1=sgh[h % 2][:], op0=ALU.mult,
                                                                                        op1=ALU.mult),
                                  reads=[ok_, ("gst", h), ("sgh", h % 2)], writes=[("gtok", h)])
                        for h in range(4):
                            for dc in range(2):
                                i = h * 2 + dc
                                bank, bk = mm[4] if i % 2 == 0 else mm[1]
                                em.op("pe", lambda h=h, dc=dc, bank=bank: P.matmul(bank[:], lhsT=ksc[s][:, h, dc * 128:(dc + 1) * 128],
                                                                                  rhs=vbf[s][:, h * 512:(h + 1) * 512], start=True, stop=True),
                                      reads=[("ksc", s), ("vbf", s)], writes=[bk])
                                em.op("dve", lambda h=h, i=i, bank=bank: V.scalar_tensor_tensor(out=Sm[:, i, :], in0=Sm[:, i, :],
                                                                                            scalar=sck[s][:, 4 + h:5 + h], in1=bank[:],
                                                                                            op0=ALU.mult, op1=ALU.add),
                                      reads=[bk, ("Sm", i), ("cdk", s)], writes=[("Sm", i)])
                        for half in range(2):
                            for m in range(8):
                                mm_ = half * 8 + m
                                em.op("pe", lambda m=m, mm_=mm_: P.transpose(out=tpB[:, m, :], in_=gtok[:, mm_ * 128:(mm_ + 1) * 128],
                                                                             identity=ident[:]),
                                      reads=[("gtok", mm_ // 4)], writes=[("tpB",)])
                            if half == 0:
                                em.op("act", lambda: S.copy(out=gatedT[:, 0:8, :], in_=tpB[:]), reads=[("tpB",)], writes=[("qT", s)])
                            else:
                                em.op("dve", lambda: V.tensor_copy(out=gatedT[:, 8:16, :], in_=tpB[:]), reads=[("tpB",)], writes=[("qTf", s)])
                        epilogue(w, c, c % 3, gatedT, [("qT", s), ("qTf", s)], Wo, [mm[2], mm[3]], dst, last)

                    pipeline(stage1, stage2, NT)
                    em.flush()

        def layer_C(li, j, src, dst, last):
            with ExitStack() as st:
                load_gain(li, st, False)
                Wi = load_weights(st, c_w_in[j][:, 0:2048], 2048, "WiCu")
                w = Work()
                alloc_common(st, w, 2)
                FC = sbuf(st, "FC", [128, 4, 2, 512], BF16)
                zb = [sbuf(st, "zb%d" % i, [128, 2, 2048], BF16) for i in range(2)]
                mm = [(psum(st, "mmC%d" % i, [128, 512], F32), ("mm", i)) for i in range(6)]
                em.dma(FC[:], tb["FC"], writes=[("FC",)], semkey="FC")

                hT4 = [sbuf(st, "hT4C_%d" % i, [128, 8, 512], BF16) for i in range(2)]
                uT4s = [sbuf(st, "uT4_%d" % i, [128, 16, 512], BF16) for i in range(2)]

                def uproj4(ss):
                    uT4 = uT4s[ss]
                    hks = [("hT4", ss, tl) for tl in range(4)]
                    for m in range(16):
                        bank, bk = mm[m % 2]
                        for kc in range(8):
                            em.op("pe", lambda kc=kc, m=m, bank=bank: P.matmul(bank[:], lhsT=Wi[:, kc, m * 128:(m + 1) * 128],
                                                                              rhs=hT4[ss][:, kc, :], start=(kc == 0), stop=(kc == 7)),
                                  reads=hks, writes=[bk])
                        if m % 2 == 0:
                            em.op("act", lambda m=m, bank=bank: S.copy(out=uT4[:, m, :], in_=bank[:]), reads=[bk], writes=[("uT", ss, m // 4)])
                        else:
                            em.op("dve", lambda m=m, bank=bank: V.tensor_copy(out=uT4[:, m, :], in_=bank[:]), reads=[bk],
                                  writes=[("uT", ss, m // 4)])

                def body1(t):
                    tl = t % 4
                    uT4 = uT4s[(t // 4) % 2]
                    ss = (t // 4) % 2
                    zs = t % 2
                    for grp in range(4):
                        for ri in range(2):
                            n_ = grp * 2 + ri
                            bank, bk = mm[2 + n_ % 4]
                            for cc in range(4):
                                em.op("pe", lambda cc=cc, grp=grp, ri=ri, bank=bank: P.matmul(
                                    bank[:], lhsT=uT4[:, grp * 4 + cc, tl * 128:(tl + 1) * 128], rhs=FC[:, cc, ri, :],
                                    start=(cc == 0), stop=(cc == 3)),
                                      reads=[("uT", ss, grp), ("FC",)], writes=[bk])
                            if n_ % 2 == 0:
                                em.op("act", lambda grp=grp, ri=ri, bank=bank: S.copy(out=zb[zs][:, ri, grp * 512:(grp + 1) * 512], in_=bank[:]),
                                      reads=[bk], writes=[("zb", zs)])
                            else:
                                em.op("dve", lambda grp=grp, ri=ri, bank=bank: V.tensor_copy(out=zb[zs][:, ri, grp * 512:(grp + 1) * 512],
                                                                                            in_=bank[:]), reads=[bk], writes=[("zb", zs)])
                    em.dma(Zscr[t * 128:(t + 1) * 128, :].rearrange("p (r c) -> p r c", r=2), zb[zs][:], reads=[("zb", zs)],
                           writes=[("D_Z", t)], semkey=("zst", zs))

                def c1s1(sb):
                    ss = sb % 2
                    for tl in range(4):
                        prologue(w, 4 * sb + tl, li, src, hT4[ss][:, :, tl * 128:(tl + 1) * 128], ("hT4", ss, tl))
                    uproj4(ss)

                def c1s2(sb):
                    for tl in range(4):
                        body1(4 * sb + tl)

                pipeline(c1s1, c1s2, NT // 4)
                em.flush()
            with ExitStack() as st:
                Gt = sbuf(st, "Gt", [NT, 128, 3, NT], BF16)
                zr = [sbuf(st, "zr%d" % i, [NT, 2, 2048], BF16) for i in range(2)]
                Bs = [sbuf(st, "Bs%d" % i, [NT, 2, 2048], BF16) for i in range(2)]
                mm = [(psum(st, "mmC%d" % i, [128, 512], F32), ("mm", i)) for i in range(8)]
                em.dma(Gt[:], tb["G"], writes=[("Gt",)], semkey="Gt")

                def body2a(n2):
                    s = n2 % 2
                    em.dma(zr[s][:], Zscr[n2::128, :].rearrange("p (r c) -> p r c", r=2), writes=[("zr", s)], semkey=("zr", s))
                    for cb in range(4):
                        (bre, kre), (bim, kim) = mm[(2 * cb) % 8], mm[(2 * cb + 1) % 8]
                        cs_ = slice(cb * 512, (cb + 1) * 512)
                        for (bank, bk, parts) in ((bre, kre, ((0, 0), (2, 1))), (bim, kim, ((1, 0), (0, 1)))):
                            for n_, (gi, ri) in enumerate(parts):
                                em.op("pe", lambda bank=bank, gi=gi, ri=ri, n_=n_, cs_=cs_: P.matmul(
                                    bank[0:NT, :], lhsT=Gt[:, n2, gi, :], rhs=zr[s][:, ri, cs_], start=(n_ == 0), stop=(n_ == 1)),
                                      reads=[("zr", s), ("Gt",)], writes=[bk])
                        em.op("act", lambda bre=bre, cs_=cs_: S.copy(out=Bs[s][:, 0, cs_], in_=bre[0:NT, :]), reads=[kre], writes=[("Bs", s)])
                        em.op("dve", lambda bim=bim, cs_=cs_: V.tensor_copy(out=Bs[s][:, 1, cs_], in_=bim[0:NT, :]), reads=[kim],
                              writes=[("Bs", s)])
                    em.dma(Bscr[:, n2, :].rearrange("p (r c) -> p r c", r=2), Bs[s][:], reads=[("Bs", s)], writes=[("D_B", n2)],
                           semkey=("bst", s))

                for n2 in range(128):
                    body2a(n2)
                em.flush()
            with ExitStack() as st:
                Ht = sbuf(st, "Ht", [128, 2, 128], BF16)
                Br = [sbuf(st, "Br%d" % i, [128, 2, 2048], BF16) for i in range(2)]
                Ms = [sbuf(st, "Ms%d" % i, [128, 2048], BF16) for i in range(2)]
                mm = [(psum(st, "mmC%d" % i, [128, 512], F32), ("mm", i)) for i in range(8)]
                em.dma(Ht[:], tb["H"], writes=[("Ht",)], semkey="Ht")

                def body2b(q):
                    s = q % 2
                    em.dma(Br[s][:], Bscr[q].rearrange("p (r c) -> p r c", r=2), writes=[("Br", s)], semkey=("Br", s))
                    for cb in range(4):
                        bank, bk = mm[(q * 4 + cb) % 8]
                        cs_ = slice(cb * 512, (cb + 1) * 512)
                        for ri in range(2):
                            em.op("pe", lambda bank=bank, ri=ri, cs_=cs_: P.matmul(bank[:], lhsT=Ht[:, ri, :], rhs=Br[s][:, ri, cs_],
                                                                                  start=(ri == 0), stop=(ri == 1)),
                                  reads=[("Br", s), ("Ht",)], writes=[bk])
                        if cb % 2 == 0:
                            em.op("act", lambda bank=bank, cs_=cs_: S.copy(out=Ms[s][:, cs_], in_=bank[:]), reads=[bk], writes=[("Ms", s)])
                        else:
                            em.op("dve", lambda bank=bank, cs_=cs_: V.tensor_copy(out=Ms[s][:, cs_], in_=bank[:]), reads=[bk],
                                  writes=[("Ms", s)])
                    em.dma(Mscr[q * 128:(q + 1) * 128, :], Ms[s][:], reads=[("Ms", s)], writes=[("D_M", q)], semkey=("mst", s))

                for q in range(NT):
                    body2b(q)
                em.flush()
            with ExitStack() as st:
                load_gain(li, st, last)
                Wi = load_weights(st, c_w_in[j][:, 2048:4096], 2048, "WiCg")
                Wo = load_weights(st, c_w_out[j], D, "WoC")
                w = Work()
                alloc_common(st, w, 8)
                hT4 = [sbuf(st, "hT4C3_%d" % i, [128, 8, 512], BF16) for i in range(2)]
                PAB = sbuf(st, "PAB", [128, 2, 128], BF16)
                MA = [sbuf(st, "MA%d" % i, [128, E], BF16) for i in range(2)]
                MB = [sbuf(st, "MB%d" % i, [128, E], BF16) for i in range(2)]
                sgT = [sbuf(st, "sgTC%d" % i, [128, 16, 512], BF16) for i in range(2)]
                gatedT = sbuf(st, "gatedTC", [128, 16, 128], BF16)
                mm = [(psum(st, "mmC%d" % i, [128, 512], F32), ("mm", i)) for i in range(6)]
                em.dma(PAB[:], tb["PAB"], writes=[("PAB",)], semkey="PAB")

                def gate4(ss):
                    hks = [("hT4", ss, tl) for tl in range(4)]
                    for m in range(16):
                        bank, bk = mm[m % 2]
                        for kc in range(8):
                            em.op("pe", lambda kc=kc, m=m, bank=bank: P.matmul(bank[:], lhsT=Wi[:, kc, m * 128:(m + 1) * 128],
                                                                              rhs=hT4[ss][:, kc, :], start=(kc == 0), stop=(kc == 7)),
                                  reads=hks, writes=[bk])
                        em.op("act", lambda m=m, bank=bank: S.activation(out=sgT[ss][:, m, :], in_=bank[:], func=AF.Silu),
                              reads=[bk], writes=[("sgT", ss, m // 4)])

                def body3(t, ss):
                    s = t % 8
                    tl = t % 4
                    em.dma(MA[t % 2][:], Mscr[t:t + TSA * 127 + 1:TSA, :], writes=[("MA", t % 2)], semkey=("MA", t % 2))
                    bB = 128 * TSB * (t // TSB) + t % TSB
                    em.dma(MB[t % 2][:], Mscr[bB:bB + TSB * 127 + 1:TSB, :], writes=[("MB", t % 2)], semkey=("MB", t % 2))
                    for m4 in range(4):
                        bank, bk = mm[2 + m4 % 2]
                        for mi in range(4):
                            m = m4 * 4 + mi
                            em.op("pe", lambda m=m, mi=mi, bank=bank: P.matmul(bank[:, mi * 128:(mi + 1) * 128],
                                                                              lhsT=MA[t % 2][:, m * 128:(m + 1) * 128], rhs=PAB[:, 0, :],
                                                                              start=True, stop=False),
                                  reads=[("MA", t % 2), ("PAB",)], writes=[bk])
                            em.op("pe", lambda m=m, mi=mi, bank=bank: P.matmul(bank[:, mi * 128:(mi + 1) * 128],
                                                                              lhsT=MB[t % 2][:, m * 128:(m + 1) * 128], rhs=PAB[:, 1, :],
                                                                              start=False, stop=True),
                                  reads=[("MB", t % 2), ("PAB",)], writes=[bk])
                        em.op("dve", lambda m4=m4, bank=bank: V.tensor_tensor(
                            out=gatedT[:, m4 * 4:(m4 + 1) * 4, :], in0=bank[:].rearrange("p (c t) -> p c t", c=4),
                            in1=sgT[ss][:, m4 * 4:(m4 + 1) * 4, tl * 128:(tl + 1) * 128], op=ALU.mult),
                              reads=[bk, ("sgT", ss, m4)], writes=[("gatedT",)])
                    epilogue(w, t, s, gatedT, ("gatedT",), Wo, [mm[4], mm[5]], dst, last)

                def c3s1(sb):
                    ss = sb % 2
                    for tl in range(4):
                        prologue(w, 4 * sb + tl, li, src, hT4[ss][:, :, tl * 128:(tl + 1) * 128], ("hT4", ss, tl))
                    gate4(ss)

                def c3s2(sb):
                    for tl in range(4):
                        body3(4 * sb + tl, sb % 2)

                pipeline(c3s1, c3s2, NT // 4)
                em.flush()

        cnt = {"A": 0, "B": 0, "C": 0}
        src = x_in
        for li, kind in enumerate(layers):
            last = final_norm and (li == len(layers) - 1)
            dst = y_out if li == len(layers) - 1 else xscr
            j = cnt[kind]
            cnt[kind] += 1
            if kind == "A":
                layer_A(li, j, src, dst, last)
            elif kind == "B":
                layer_B(li, j, src, dst, last)
            else:
                layer_C(li, j, src, dst, last)
            src = dst
        nc._em_n_instr = em.n_instr
    return nc


def kernel(x_prompt, x_sample, norm_g, final_norm_g, a_w_in, a_sink, a_w_out,
           b_w_in, b_decay, b_w_out, c_w_in, c_w_out):
    NT, TSA, TSB = 128, 128, 16
    nc = build(NT, TSA, TSB)
    f = lambda a: np.ascontiguousarray(np.asarray(a, dtype=np.float32))
    common = {"norm_g": f(norm_g), "final_norm_g": f(final_norm_g).reshape(1, D), "a_w_in": f(a_w_in), "a_sink": f(a_sink),
              "a_w_out": f(a_w_out), "b_w_in": f(b_w_in), "b_decay": f(b_decay).reshape(1, 8), "b_w_out": f(b_w_out),
              "c_w_in": f(c_w_in), "c_w_out": f(c_w_out)}
    tabA = const_tables(NT, TSA, 1.0, 0.0, TSA, TSB)
    tabB = const_tables(NT, TSB, 0.0, 1.0, TSA, TSB)
    xp = f(x_prompt)
    xs = f(x_sample)
    in_maps = []
    for c in range(8):
        if c < 2:
            m = dict(common, **tabA)
            m["x"] = xs[c]
        else:
            m = dict(common, **tabB)
            xx = np.zeros((8, 2048, D), np.float32)
            seqs = list(range((c - 2) * 6, min((c - 2) * 6 + 6, 32)))
            xx[:len(seqs)] = xp[seqs]
            m["x"] = xx.reshape(NT * 128, D)
        in_maps.append(m)
    res = run_bass_kernel_spmd(nc, in_maps, core_ids=list(range(8)))
    y_s = np.stack([res.results[c]["y"] for c in range(2)], 0).astype(np.float32)
    y_p = np.zeros((32, 2048, D), np.float32)
    for c in range(2, 8):
        seqs = list(range((c - 2) * 6, min((c - 2) * 6 + 6, 32)))
        yy = res.results[c]["y"].reshape(8, 2048, D)
        y_p[seqs] = yy[:len(seqs)]
    return (y_p, y_s)
```

```python
import math
from contextlib import ExitStack

import numpy as np
import ml_dtypes

import concourse.bass as bass
import concourse.mybir as mybir
from concourse.bass_utils import run_bass_kernel_spmd

F32 = mybir.dt.float32
BF16 = mybir.dt.bfloat16
AF = mybir.ActivationFunctionType
ALU = mybir.AluOpType
NPBF = ml_dtypes.bfloat16

D = 1024
E = 2048
EPS = 1e-6
A_IN = 4608
B_IN = 6144
C_IN = 4096
LAYERS = ("A", "B", "C", "A")


class Em:
    ENGS = ("pe", "act", "dve", "pool", "sp")

    def __init__(self, nc, stack):
        self.nc = nc
        self.eng = {"pe": nc.tensor, "act": nc.scalar, "dve": nc.vector, "pool": nc.gpsimd, "sp": nc.sync}
        self.stack = stack
        self.sem = {e: stack.enter_context(nc.semaphore("s_" + e)) for e in self.ENGS}
        self.semval = {e: 0 for e in self.ENGS}
        self.dsem = {}
        self.dval = {}
        self.ops = []
        self.n_instr = 0

    def op(self, e, fn, reads=(), writes=()):
        self.ops.append(("c", e, fn, tuple(reads), tuple(writes), None))

    def dma(self, out, in_, reads=(), writes=(), semkey=None, q="sp", **kw):
        assert semkey is not None
        self.ops.append(("d", q, (out, in_, kw), tuple(reads), tuple(writes), semkey))

    @staticmethod
    def _hoist(ops):
        pos = [0.0] * len(ops)
        last_touch = {}
        last_sem = {}
        nh = 0
        for i, o in enumerate(ops):
            kind, e, fn, reads, writes, semkey = o
            p = float(i)
            if kind == "d" and not reads and writes and not str(writes[0][0]).startswith("D_"):
                q = max([last_touch.get(k, -1.0) for k in writes] + [last_sem.get(semkey, -1.0)])
                nh += 1
                p = min(p, q + 1e-7 * nh)
            pos[i] = p
            for k in reads:
                last_touch[k] = max(last_touch.get(k, -1.0), p)
            for k in writes:
                last_touch[k] = max(last_touch.get(k, -1.0), p)
            if kind == "d":
                last_sem[semkey] = max(last_sem.get(semkey, -1.0), p)
        order = sorted(range(len(ops)), key=lambda i: (pos[i], i))
        return [ops[i] for i in order]

    def flush(self):
        ops = self._hoist(self.ops)
        self.ops = []
        n = len(ops)
        last_w = {}
        readers = {}
        eidx = {e: 0 for e in self.ENGS}
        op_eidx = [0] * n
        known = {e: {} for e in self.ENGS}
        waits = [None] * n
        needed = [False] * n
        for i, o in enumerate(ops):
            kind, e, fn, reads, writes, semkey = o
            eidx[e] += 1
            op_eidx[i] = eidx[e]
            deps = set()
            for k in reads:
                j = last_w.get(k)
                if j is not None:
                    deps.add(j)
            for k in writes:
                j = last_w.get(k)
                if j is not None:
                    deps.add(j)
                for r in readers.get(k, ()):
                    deps.add(r)
            deps.discard(i)
            w = []
            kn = known[e]
            for j in sorted(deps):
                oj = ops[j]
                if oj[0] == "c":
                    ej = oj[1]
                    if ej == e and e == "pe":
                        continue
                    kk = ej
                    if kn.get(kk, -1) >= op_eidx[j]:
                        continue
                    kn[kk] = op_eidx[j]
                else:
                    kk = ("d", oj[5])
                    if kn.get(kk, -1) >= j:
                        continue
                    kn[kk] = j
                w.append(j)
                needed[j] = True
            waits[i] = w
            for k in writes:
                last_w[k] = i
                readers[k] = []
            for k in reads:
                readers.setdefault(k, []).append(i)
        last_of = {}
        for i, o in enumerate(ops):
            if o[0] == "c":
                last_of[o[1]] = i
            else:
                needed[i] = True
        for i in last_of.values():
            needed[i] = True
        val = [None] * n
        for i, o in enumerate(ops):
            if not needed[i]:
                continue
            if o[0] == "c":
                self.semval[o[1]] += 1
                val[i] = (self.sem[o[1]], self.semval[o[1]])
            else:
                sk = o[5]
                if sk not in self.dsem:
                    self.dsem[sk] = self.stack.enter_context(self.nc.semaphore("d%d" % len(self.dsem)))
                    self.dval[sk] = 0
                self.dval[sk] += 16
                val[i] = (self.dsem[sk], self.dval[sk])
        for i, o in enumerate(ops):
            kind, e, fn, reads, writes, semkey = o
            eng = self.eng[e]
            for j in waits[i]:
                s, v = val[j]
                eng.wait_ge(s, v)
            if kind == "c":
                ins = fn()
                if needed[i]:
                    ins.then_inc(val[i][0], 1)
            else:
                out, in_, kw = fn
                eng.dma_start(out=out, in_=in_, **kw).then_inc(val[i][0], 16)
            self.n_instr += 1 + len(waits[i])
        self.barrier()

    def barrier(self):
        sp = self.eng["sp"]
        for sk, s in self.dsem.items():
            if self.dval[sk] > 0:
                sp.wait_ge(s, self.dval[sk])
        for e in ("pe", "act", "dve", "pool"):
            if self.semval[e] > 0:
                sp.wait_ge(self.sem[e], self.semval[e])
        self.semval["sp"] += 1
        sp.nop().then_inc(self.sem["sp"], 1)
        for e in ("pe", "act", "dve", "pool"):
            self.eng[e].wait_ge(self.sem["sp"], self.semval["sp"])


def const_tables(NT, TS, fa, fb, TSA, TSB):
    T = NT * 128
    pos = (np.arange(T) % (TS * 128)).astype(np.float32)
    t = {}
    t["ident"] = np.eye(128, dtype=np.float32).astype(NPBF)
    invA = np.exp(-(np.arange(8, dtype=np.float32) * (2.0 / 16)) * np.float32(math.log(500000.0))).astype(np.float32)
    angA = (pos[:, None] * invA[None, :]).astype(np.float32)
    t["ropeA"] = np.concatenate([np.cos(angA), np.sin(angA)], 1).astype(np.float32)
    invB = np.exp(-(np.arange(128, dtype=np.float32) * (2.0 / 256)) * np.float32(math.log(10000.0))).astype(np.float32)
    angB = (pos[:, None] * invB[None, :]).astype(np.float32)
    t["ropeB"] = np.concatenate([np.cos(angB), np.sin(angB)], 1).astype(np.float32)
    tiles = np.arange(NT)
    fl = np.zeros((2, NT), np.float32)
    fl[0] = (tiles % TS != 0)
    fl[1] = (tiles % TS != TS - 1)
    t["flagsA"] = np.broadcast_to(fl[None], (128, 2, NT)).astype(np.float32).copy()
    kk = np.arange(128)
    mk = np.zeros((128, 2, 128), np.float32)
    mk[:, 0, :] = (kk[:, None] >= kk[None, :])
    mk[:, 1, :] = (kk[:, None] <= kk[None, :])
    t["maskLR"] = mk.astype(NPBF)
    jj = kk[:, None].astype(np.float32)
    ii = kk[None, :].astype(np.float32)
    rt = np.zeros((128, 4, 128), np.float32)
    rt[:, 0, :] = np.where(jj <= ii, ii - jj, 0.0)
    rt[:, 1, :] = np.where(jj > ii, jj - ii, 0.0)
    rt[:, 2, :] = ii + 1.0
    rt[:, 3, :] = 128.0 - ii
    t["retT"] = rt
    rp = np.zeros((128, 2), np.float32)
    rp[:, 0] = 127.0 - kk
    rp[:, 1] = kk
    t["retP"] = rp
    fb_ = np.zeros((2, NT), np.float32)
    fb_[0] = ((tiles + 1) % TS != 0)
    fb_[1] = (tiles % TS != 0)
    t["flagsB"] = np.broadcast_to(fb_[None], (128, 2, NT)).astype(np.float32).copy()
    c = np.arange(512)
    ang = 2 * np.pi * np.outer(c, c) / 512.0
    fc = np.stack([np.cos(ang), -np.sin(ang)], 1) / math.sqrt(512.0)
    t["FC"] = fc.reshape(4, 128, 2, 512).transpose(1, 0, 2, 3).astype(NPBF).copy()
    NS = NT // TS
    n1 = np.arange(NT)
    s1, m1 = n1 // TS, n1 % TS
    q = np.arange(NT)
    sq, k1 = q // TS, q % TS
    n2 = np.arange(128)
    ph = (m1[:, None, None] * k1[None, None, :] / TS) + (n2[None, :, None] * k1[None, None, :] / (128.0 * TS))
    Gc = np.exp(-2j * np.pi * ph) * (s1[:, None, None] == sq[None, None, :])
    G = np.stack([Gc.real, Gc.imag, -Gc.imag], 2)
    t["G"] = G.astype(NPBF)
    R = 128 // TS
    k2 = np.arange(128)
    sig = TS * (k2 % R) + k2 // R
    Hc = np.exp(-2j * np.pi * np.outer(n2, k2) / 128.0) / math.sqrt(128.0 * TS)
    H = np.zeros((128, 2, 128), np.float64)
    H[:, 0, sig] = Hc.real
    H[:, 1, sig] = -Hc.imag
    t["H"] = H.astype(NPBF)
    pi = np.arange(128)
    P = np.zeros((128, 2, 128), np.float32)
    for v, (f, ts) in enumerate(((fa, TSA), (fb, TSB))):
        r = 128 // ts
        w = ts * (pi % r) + pi // r
        P[pi, v, w] = f
    t["PAB"] = P.astype(NPBF)
    return t


TABLE_SPECS = lambda NT: {
    "ident": ([128, 128], BF16), "ropeA": ([NT * 128, 16], F32), "ropeB": ([NT * 128, 256], F32),
    "flagsA": ([128, 2, NT], F32), "maskLR": ([128, 2, 128], BF16), "retT": ([128, 4, 128], F32),
    "retP": ([128, 2], F32), "flagsB": ([128, 2, NT], F32), "FC": ([128, 4, 2, 512], BF16),
    "G": ([NT, 128, 3, NT], BF16), "H": ([128, 2, 128], BF16), "PAB": ([128, 2, 128], BF16),
}


def build(NT, TSA, TSB, layers=LAYERS, final_norm=True, debug=False, bstop=0):
    T = NT * 128
    nc = bass.Bass("TRN2", target_bir_lowering=False)
    din = lambda name, shape, dt=F32: nc.dram_tensor(name, shape, dt, kind="ExternalInput").ap()
    x_in = din("x", [T, D])
    norm_g = din("norm_g", [4, D])
    fin_g = din("final_norm_g", [1, D])
    a_w_in = din("a_w_in", [2, D, A_IN])
    a_sink = din("a_sink", [2, 32])
    a_w_out = din("a_w_out", [2, E, D])
    b_w_in = din("b_w_in", [1, D, B_IN])
    b_decay = din("b_decay", [1, 8])
    b_w_out = din("b_w_out", [1, E, D])
    c_w_in = din("c_w_in", [1, D, C_IN])
    c_w_out = din("c_w_out", [1, E, D])
    tb = {k: din(k, sh, dt) for k, (sh, dt) in TABLE_SPECS(NT).items()}
    y_out = nc.dram_tensor("y", [T, D], F32, kind="ExternalOutput").ap()
    dscr = lambda name, shape, dt: nc.dram_tensor(name, shape, dt, kind="Internal").ap()
    xscr = dscr("xscr", [T, D], F32)
    KT = dscr("KT", [NT, 128, 1024], BF16)
    VA = dscr("VA", [NT, 128, 1024], BF16)
    SG = dscr("SG", [NT, 128, 2048], BF16)
    KB = dscr("KBs", [NT, 128, 1024], BF16)
    VB = dscr("VBs", [NT, 128, 2048], BF16)
    Sscr = dscr("Sscr", [NT, 128, 4096], BF16)
    Zscr = dscr("Zscr", [T, 4096], BF16)
    Bscr = dscr("Bscr", [NT, 128, 4096], BF16)
    Mscr = dscr("Mscr", [T, E], BF16)

    with ExitStack() as top:
        em = Em(nc, top)
        V, S, G, P = nc.vector, nc.scalar, nc.gpsimd, nc.tensor

        uid = [0]

        def sbuf(st, name, shape, dt):
            uid[0] += 1
            return st.enter_context(nc.sbuf_tensor("sb%d_%s" % (uid[0], name), shape, dt))

        def psum(st, name, shape, dt):
            uid[0] += 1
            return st.enter_context(nc.psum_tensor("ps%d_%s" % (uid[0], name), shape, dt))

        ident = sbuf(top, "ident", [128, 128], BF16)
        gl = sbuf(top, "gl", [128, D], F32)
        gfin_box = [None]
        epsb = sbuf(top, "epsb", [128, 1], F32)
        em.op("pool", lambda: G.memset(epsb[:], EPS), writes=[("epsb",)])
        em.dma(ident[:], tb["ident"], writes=[("ident",)], semkey="c0")
        em.flush()

        def load_weights(st, Wd, ncols, name):
            K = Wd.shape[0]
            Wb = sbuf(st, name, [128, K // 128, ncols], BF16)
            with ExitStack() as st2:
                stg = [sbuf(st2, "stg%d" % i, [128, ncols], F32) for i in range(2)]
                h = ncols // 2
                for kc in range(K // 128):
                    s = kc % 2
                    em.dma(stg[s][:], Wd[kc * 128:(kc + 1) * 128, :], writes=[("stg", s)], semkey=("stg", s))
                    em.op("act", lambda kc=kc, s=s: S.copy(out=Wb[:, kc, 0:h], in_=stg[s][:, 0:h]),
                          reads=[("stg", s)], writes=[(name, kc, 0)])
                    em.op("dve", lambda kc=kc, s=s: V.tensor_copy(out=Wb[:, kc, h:ncols], in_=stg[s][:, h:ncols]),
                          reads=[("stg", s)], writes=[(name, kc, 1)])
                em.flush()
            return Wb

        def load_gain(li, st=None, last=False):
            em.dma(gl[:], norm_g[li:li + 1, :].partition_broadcast(128), writes=[("gl",)], semkey="gl")
            if last:
                gfin_box[0] = sbuf(st, "gfin", [128, D], F32)
                em.dma(gfin_box[0][:], fin_g.partition_broadcast(128), writes=[("gfin",)], semkey="c2")

        class Work:
            pass

        def alloc_common(st, w, nslot=2):
            w.nslot = nslot
            w.xt = [sbuf(st, "xt%d" % i, [128, D], F32) for i in range(nslot)]
            w.hb = sbuf(st, "hb", [128, D], BF16)
            w.junk = w.hb
            w.st = [sbuf(st, "stat%d" % i, [128, 4], F32) for i in range(nslot)]
            w.hT = [sbuf(st, "hT%d" % i, [128, 8, 128], BF16) for i in range(min(nslot, 2))]
            w.tp = psum(st, "tp", [128, 16, 128], BF16)

        def load_x(w, t, src):
            s = t % w.nslot
            em.dma(w.xt[s][:], src[t * 128:(t + 1) * 128, :], writes=[("xt", s)], semkey=("xt", s))

        def prologue(w, t, li, src, hdst=None, hkey=None, preloaded=False):
            s = t % w.nslot
            xt, stt = w.xt[s], w.st[s]
            sh = s % len(w.hT)
            hT = w.hT[sh][:] if hdst is None else hdst
            hkey = ("hT", sh) if hkey is None else hkey
            if not preloaded:
                em.dma(xt[:], src[t * 128:(t + 1) * 128, :], writes=[("xt", s)], semkey=("xt", s))
            em.op("act", lambda: S.activation(out=w.junk[:], in_=xt[:], func=AF.Square, accum_out=stt[:, 0:1]),
                  reads=[("xt", s)], writes=[("hb",), ("st", s)])
            em.op("act", lambda: S.activation(out=stt[:, 2:3], in_=stt[:, 0:1], func=AF.Ln, scale=1.0 / D, bias=epsb[:, 0:1]),
                  reads=[("st", s)], writes=[("st", s)])
            em.op("act", lambda: S.activation(out=stt[:, 3:4], in_=stt[:, 2:3], func=AF.Exp, scale=-0.5),
                  reads=[("st", s)], writes=[("st", s)])
            em.op("dve", lambda: V.scalar_tensor_tensor(out=w.hb[:], in0=xt[:], scalar=stt[:, 3:4], in1=gl[:],
                                                        op0=ALU.mult, op1=ALU.mult),
                  reads=[("xt", s), ("st", s)], writes=[("hb",)])
            split_at[0] = len(em.ops)
            for c in range(8):
                em.op("pe", lambda c=c: P.transpose(out=w.tp[:, c, :], in_=w.hb[:, c * 128:(c + 1) * 128], identity=ident[:]),
                      reads=[("hb",)], writes=[("tp",)])
            em.op("dve", lambda: V.tensor_copy(out=hT[:, 0:4, :], in_=w.tp[:, 0:4, :]), reads=[("tp",)], writes=[hkey])
            em.op("dve", lambda: V.tensor_copy(out=hT[:, 4:8, :], in_=w.tp[:, 4:8, :]), reads=[("tp",)], writes=[hkey])
            return s

        def epilogue(w, t, s, gatedT, gkey, Wo, mm, dst, last):
            xt, stt = w.xt[s], w.st[s]
            def gk(m):
                if gkey == "per_group":
                    return [("gatedT", m // 4)]
                return list(gkey) if isinstance(gkey, list) else [gkey]
            for (m0, m1) in ((0, 12), (12, 16)):
                for dh in range(2):
                    bank, bk = mm[dh]
                    for m in range(m0, m1):
                        em.op("pe", lambda m=m, dh=dh, bank=bank: P.matmul(bank[:], lhsT=gatedT[:, m, :],
                                                                          rhs=Wo[:, m, dh * 512:(dh + 1) * 512],
                                                                          start=(m == 0), stop=(m == 15)),
                              reads=gk(m), writes=[bk])
            for dh in range(2):
                bank, bk = mm[dh]
                em.op("dve", lambda dh=dh, bank=bank: V.tensor_tensor(out=xt[:, dh * 512:(dh + 1) * 512], in0=bank[:],
                                                                     in1=xt[:, dh * 512:(dh + 1) * 512], op=ALU.add),
                      reads=[bk, ("xt", s)], writes=[("xt", s)])
            if last:
                em.op("act", lambda: S.activation(out=w.junk[:], in_=xt[:], func=AF.Square, accum_out=stt[:, 0:1]),
                      reads=[("xt", s)], writes=[("hb",) if w.junk is w.hb else ("junk",), ("st", s)])
                em.op("act", lambda: S.activation(out=stt[:, 2:3], in_=stt[:, 0:1], func=AF.Ln, scale=1.0 / D, bias=epsb[:, 0:1]),
                      reads=[("st", s)], writes=[("st", s)])
                em.op("act", lambda: S.activation(out=stt[:, 3:4], in_=stt[:, 2:3], func=AF.Exp, scale=-0.5),
                      reads=[("st", s)], writes=[("st", s)])
                em.op("dve", lambda: V.scalar_tensor_tensor(out=xt[:], in0=xt[:], scalar=stt[:, 3:4], in1=gfin_box[0][:],
                                                            op0=ALU.mult, op1=ALU.mult),
                      reads=[("xt", s), ("st", s)], writes=[("xt", s)])
            em.dma(dst[t * 128:(t + 1) * 128, :], xt[:], reads=[("xt", s)], writes=[("D_x", t)], semkey=("xst", s))

        def dump(name, ap, key):
            if not debug:
                return
            d = nc.dram_tensor("dbg_" + name, list(ap.shape), ap.dtype, kind="ExternalOutput").ap()
            em.dma(d, ap, reads=[key], writes=[("D_dbg", name)], semkey=("dbg", name))

        split_at = [0]

        def capture(fn, *args):
            saved = em.ops
            em.ops = []
            fn(*args)
            out = em.ops
            em.ops = saved
            return out

        def merge(a, b):
            out, i, j = [], 0, 0
            na, nb = len(a), len(b)
            while i < na or j < nb:
                if j >= nb or (i < na and i * nb <= j * na):
                    out.append(a[i]); i += 1
                else:
                    out.append(b[j]); j += 1
            return out

        def pipeline(stage1, stage2, n, prefetch=None):
            if prefetch is not None:
                prefetch(0)
                if n > 1:
                    prefetch(1)
            prev = capture(stage1, 0)
            em.ops.extend(prev)
            for t in range(n):
                if prefetch is not None and t + 2 < n:
                    prefetch(t + 2)
                s2 = capture(stage2, t)
                if t + 1 < n:
                    s1 = capture(stage1, t + 1)
                    k = split_at[0]
                    em.ops.extend(s1[:k])
                    em.ops.extend(merge(s1[k:], s2))
                else:
                    em.ops.extend(s2)

        def bc_mid(ap, n):
            return ap.unsqueeze(1).broadcast_to([ap.shape[0], n, ap.shape[1]])

        def layer_A(li, j, src, dst, last):
            with ExitStack() as st:
                load_gain(li, st, last)
                Wi = load_weights(st, a_w_in[j], A_IN, "Wi")
                Wo = load_weights(st, a_w_out[j], D, "Wo")
                with ExitStack() as s1:
                    w = Work()
                    alloc_common(s1, w)
                    kvfs = [sbuf(s1, "kvf%d" % i, [128, 512], F32) for i in range(2)]
                    cs = [sbuf(s1, "csA%d" % i, [128, 16], F32) for i in range(2)]
                    tmp = sbuf(s1, "ropetmp", [128, 4, 4, 8], F32)
                    ktz = [sbuf(s1, "ktz%d" % i, [128, 8, 128], BF16) for i in range(2)]
                    vau = [sbuf(s1, "vau%d" % i, [128, 8, 128], BF16) for i in range(2)]
                    kTs = [sbuf(s1, "kTs%d" % i, [128, 8, 128], BF16) for i in range(2)]
                    mmb = psum(s1, "mmA1", [128, 512], F32)
                    mmg = [(psum(s1, "mmA1g%d" % i, [128, 512], F32), ("mmg", i)) for i in range(2)]
                    sgs = [sbuf(s1, "sgs%d" % i, [128, 16, 512], BF16) for i in range(2)]
                    hT4 = [sbuf(s1, "hT4_%d" % i, [128, 8, 512], BF16) for i in range(2)]
                    for i in range(2):
                        em.op("pool", lambda i=i: G.memset(ktz[i][:], 0.0), writes=[("ktz", i)])
                        em.op("pool", lambda i=i: G.memset(vau[i][:], 1.0), writes=[("vau", i)])
                    def stage1(b):
                        sb, tl = b // 4, b % 4
                        ss = sb % 2
                        hTv, hk = hT4[ss][:, :, tl * 128:(tl + 1) * 128], ("hT4", ss, tl)
                        s = prologue(w, b, li, src, hTv, hk)
                        em.dma(cs[s][:], tb["ropeA"][b * 128:(b + 1) * 128, :], writes=[("cs", s)], semkey=("cs", s))
                        for kc in range(8):
                            em.op("pe", lambda kc=kc, s=s: P.matmul(mmb[:], lhsT=hTv[:, kc, :], rhs=Wi[:, kc, 2048:2560],
                                                                   start=(kc == 0), stop=(kc == 7)),
                                  reads=[hk], writes=[("mmb",)])
                        em.op("act", lambda: S.copy(out=kvfs[s][:], in_=mmb[:]), reads=[("mmb",)], writes=[("kvf", s)])

                    def stage2(b):
                        s = b % 2
                        kvf = kvfs[s]
                        kv4 = kvf[:, 0:256].rearrange("p (g d) -> p g d", g=4)
                        x1, x2 = kv4[:, :, 0:8], kv4[:, :, 8:16]
                        cosb, sinb = bc_mid(cs[s][:, 0:8], 4), bc_mid(cs[s][:, 8:16], 4)
                        for ti, (xa, cb_) in enumerate(((x1, cosb), (x2, sinb), (x2, cosb), (x1, sinb))):
                            em.op("pool", lambda ti=ti, xa=xa, cb_=cb_: G.tensor_tensor(out=tmp[:, ti], in0=xa, in1=cb_, op=ALU.mult),
                                  reads=[("kvf", s), ("cs", s)], writes=[("tmp", ti)])
                        kz = ktz[s][:].rearrange("p (g r) c -> p g r c", r=2)
                        va = vau[s][:].rearrange("p (g r) c -> p g r c", r=2)
                        for par in range(2):
                            o = par * 64
                            em.op("pool", lambda par=par, o=o, kz=kz: G.tensor_tensor(out=kz[:, :, par, o:o + 8], in0=tmp[:, 0], in1=tmp[:, 1],
                                                                          op=ALU.subtract),
                                  reads=[("tmp", 0), ("tmp", 1)], writes=[("ktz", s)])
                            em.op("pool", lambda par=par, o=o, kz=kz: G.tensor_tensor(out=kz[:, :, par, o + 8:o + 16], in0=tmp[:, 2], in1=tmp[:, 3],
                                                                          op=ALU.add),
                                  reads=[("tmp", 2), ("tmp", 3)], writes=[("ktz", s)])
                            em.op("dve", lambda par=par, o=o, kz=kz: V.tensor_copy(out=kz[:, :, par, o + 16:o + 64], in_=kv4[:, :, 16:64]),
                                  reads=[("kvf", s)], writes=[("ktz", s)])
                            em.op("dve", lambda par=par, o=o, va=va: V.tensor_copy(
                                out=va[:, :, par, o:o + 64], in_=kvf[:, 256:512].rearrange("p (g d) -> p g d", g=4)),
                                  reads=[("kvf", s)], writes=[("vau", s)])
                        for c in range(8):
                            em.op("pe", lambda c=c, s=s: P.transpose(out=w.tp[:, 8 + c, :], in_=ktz[s][:, c, :], identity=ident[:]),
                                  reads=[("ktz", s)], writes=[("tp2",)])
                        em.op("act", lambda s=s: S.copy(out=kTs[s][:], in_=w.tp[:, 8:16, :]), reads=[("tp2",)], writes=[("kTs", s)])
                        em.dma(KT[b].rearrange("p (c t) -> p c t", c=8), kTs[s][:], reads=[("kTs", s)], writes=[("D_KT", b)],
                               semkey=("kTst", s))
                        em.dma(VA[b].rearrange("p (c t) -> p c t", c=8), vau[s][:], reads=[("vau", s)], writes=[("D_VA", b)],
                               semkey=("vast", s))
                        if b % 4 == 3:
                            gate4(b // 4, (b // 4) % 2)

                    def gate4(sb, ss):
                        hks = [("hT4", ss, tl) for tl in range(4)]
                        for m in range(16):
                            bank, bk = mmg[m % 2]
                            for kc in range(8):
                                em.op("pe", lambda kc=kc, m=m, bank=bank: P.matmul(
                                    bank[:], lhsT=Wi[:, kc, 2560 + m * 128:2560 + (m + 1) * 128], rhs=hT4[ss][:, kc, :],
                                    start=(kc == 0), stop=(kc == 7)), reads=hks, writes=[bk])
                            em.op("act", lambda m=m, bank=bank: S.activation(out=sgs[ss][:, m, :], in_=bank[:], func=AF.Silu),
                                  reads=[bk], writes=[("sgs", ss)])
                        for tl in range(4):
                            em.dma(SG[4 * sb + tl].rearrange("p (c t) -> p c t", c=16), sgs[ss][:, :, tl * 128:(tl + 1) * 128],
                                   reads=[("sgs", ss)], writes=[("D_SG", sb, tl)], semkey=("sgst", ss))

                    pipeline(stage1, stage2, NT)
                    em.flush()
                with ExitStack() as s2:
                    w = Work()
                    alloc_common(s2, w, 3)
                    if last:
                        w.junk = sbuf(s2, "junkA", [128, D], BF16)
                    sk = sbuf(s2, "sinkbc", [128, 32], F32)
                    sinksm = sbuf(s2, "sinksm", [128, 8, 4], F32)
                    flags = sbuf(s2, "flagsA", [128, 2, NT], F32)
                    mask = sbuf(s2, "maskLR", [128, 2, 128], BF16)
                    maskb = [sbuf(s2, "maskb%d" % i, [128, 2, 128], BF16) for i in range(2)]
                    cs = [sbuf(s2, "csA%d" % i, [128, 16], F32) for i in range(2)]
                    kTl = [[sbuf(s2, "kTl%d_%d" % (i, jj), [128, 8, 128], BF16) for jj in range(3)] for i in range(2)]
                    val = [[sbuf(s2, "val%d_%d" % (i, jj), [128, 8, 128], BF16) for jj in range(3)] for i in range(2)]
                    qf = sbuf(s2, "qf", [128, 32, 16], F32)
                    tmp = sbuf(s2, "ropetmpq", [128, 4, 32, 8], F32)
                    qb = sbuf(s2, "qb", [128, E], BF16)
                    qT = [sbuf(s2, "qT%d" % i, [128, 16, 128], BF16) for i in range(2)]
                    sgT = [sbuf(s2, "sgT%d" % i, [128, 16, 128], BF16) for i in range(2)]
                    pex = [sbuf(s2, "pex%d" % i, [128, 4, 128], BF16) for i in range(10)]
                    rec = [sbuf(s2, "rec%d" % i, [128, 512], F32) for i in range(2)]
                    onrm = rec
                    gatedT = sbuf(s2, "gatedT", [128, 16, 128], BF16)
                    prow = sbuf(s2, "prow", [128, 8, 4], BF16)
                    sel = sbuf(s2, "sel", [128, 2, 128], BF16)
                    mm = [(psum(s2, "mmA%d" % i, [128, 512], F32), ("mm", i)) for i in range(6)]
                    em.dma(sk[:], a_sink[j:j + 1, :].partition_broadcast(128), writes=[("sk",)], semkey="sk")
                    em.dma(flags[:], tb["flagsA"], writes=[("flags",)], semkey="fl")
                    em.dma(mask[:], tb["maskLR"], writes=[("mask",)], semkey="mk")
                    em.op("act", lambda: S.activation(out=sk[:], in_=sk[:], func=AF.Exp), reads=[("sk",)], writes=[("sk",)])
                    em.op("pool", lambda: G.tensor_copy(out=sinksm[:].rearrange("p (g r) c -> p g r c", r=2),
                                                        in_=sk[:].rearrange("p (g c r) -> p g r c", g=4, c=4, r=2)),
                          reads=[("sk",)], writes=[("sinksm",)])
                    em.op("pool", lambda: G.memset(prow[:], 0.0), writes=[("prow",)])
                    em.op("pool", lambda: G.memset(sel[:], 0.0), writes=[("sel",)])
                    em.op("pool", lambda: G.memset(sel[:, 0, 64:128], 1.0), reads=[("sel",)], writes=[("sel",)])
                    em.op("pool", lambda: G.memset(sel[:, 1, 0:64], 1.0), reads=[("sel",)], writes=[("sel",)])
                    for gp in range(8):
                        em.op("pool", lambda gp=gp: G.tensor_copy(out=prow[0:1, gp, :], in_=sinksm[0:1, gp, :]),
                              reads=[("prow",), ("sinksm",)], writes=[("prow",)])
                    pexi = [0]

                    def stage1(b):
                        sx = prologue(w, b, li, src, preloaded=True)
                        s = b % 2
                        em.dma(cs[s][:], tb["ropeA"][b * 128:(b + 1) * 128, :], writes=[("cs", s)], semkey=("cs", s))
                        js = [jj for jj in range(3) if 0 <= b - 1 + jj < NT]
                        for side in range(2):
                            em.op("pool", lambda side=side: G.tensor_scalar(out=maskb[s][:, side, :], in0=mask[:, side, :],
                                                                            scalar1=flags[:, side, b:b + 1], scalar2=None, op0=ALU.mult),
                                  reads=[("mask",), ("flags",)], writes=[("maskb", s, side)])
                        for jj in js:
                            bb = b - 1 + jj
                            em.dma(kTl[s][jj][:], KT[bb].rearrange("p (c t) -> p c t", c=8), writes=[("kTl", s, jj)],
                                   semkey=("kTl", s, jj))
                            em.dma(val[s][jj][:], VA[bb].rearrange("p (c t) -> p c t", c=8), writes=[("val", s, jj)],
                                   semkey=("val", s, jj))
                        for cb in range(4):
                            bank, bk = mm[0]
                            for kc in range(8):
                                em.op("pe", lambda kc=kc, cb=cb, bank=bank: P.matmul(bank[:], lhsT=w.hT[sx % 2][:, kc, :],
                                                                                   rhs=Wi[:, kc, cb * 512:(cb + 1) * 512],
                                                                                   start=(kc == 0), stop=(kc == 7)),
                                      reads=[("hT", sx % 2)], writes=[bk])
                            em.op("dve", lambda cb=cb, bank=bank: V.tensor_scalar(out=qb[:, cb * 512:(cb + 1) * 512], in0=bank[:],
                                                                                 scalar1=0.125, scalar2=None, op0=ALU.mult),
                                  reads=[bk], writes=[("qb", cb)])
                            em.op("dve", lambda cb=cb, bank=bank: V.tensor_scalar(
                                out=qf[:, cb * 8:(cb + 1) * 8, :], in0=bank[:].rearrange("p (h d) -> p h d", h=8)[:, :, 0:16],
                                scalar1=0.125, scalar2=None, op0=ALU.mult), reads=[bk], writes=[("qf", cb)])
                        qb3 = qb[:].rearrange("p (h d) -> p h d", h=32)
                        x1, x2 = qf[:, :, 0:8], qf[:, :, 8:16]
                        cosb, sinb = bc_mid(cs[s][:, 0:8], 32), bc_mid(cs[s][:, 8:16], 32)
                        qfk = [("qf", cb) for cb in range(4)]
                        qbk = [("qb", cb) for cb in range(4)]
                        for ti, (xa, cb_) in enumerate(((x1, cosb), (x2, sinb), (x2, cosb), (x1, sinb))):
                            em.op("dve", lambda ti=ti, xa=xa, cb_=cb_: V.tensor_tensor(out=tmp[:, ti], in0=xa, in1=cb_, op=ALU.mult),
                                  reads=qfk + [("cs", s)], writes=[("tmpq", ti)])
                        em.op("dve", lambda: V.tensor_tensor(out=qb3[:, :, 0:8], in0=tmp[:, 0], in1=tmp[:, 1], op=ALU.subtract),
                              reads=[("tmpq", 0), ("tmpq", 1)], writes=qbk)
                        em.op("dve", lambda: V.tensor_tensor(out=qb3[:, :, 8:16], in0=tmp[:, 2], in1=tmp[:, 3], op=ALU.add),
                              reads=[("tmpq", 2), ("tmpq", 3)], writes=qbk)
                        for m in range(16):
                            em.op("pe", lambda m=m: P.transpose(out=w.tp[:, m, :], in_=qb[:, m * 128:(m + 1) * 128], identity=ident[:]),
                                  reads=[("qb", m // 4)], writes=[("tp" if m < 8 else "tp2",)])
                        em.op("dve", lambda: V.tensor_copy(out=qT[s][:, 0:8, :], in_=w.tp[:, 0:8, :]), reads=[("tp",)], writes=[("qT", s, 0)])
                        em.op("dve", lambda: V.tensor_copy(out=qT[s][:, 8:16, :], in_=w.tp[:, 8:16, :]), reads=[("tp2",)],
                              writes=[("qT", s, 1)])
                        em.dma(sgT[s][:], SG[b].rearrange("p (c t) -> p c t", c=16), writes=[("sgT", s)], semkey=("sgT", s))

                    def stage2(b):
                        s = b % 2
                        js = [jj for jj in range(3) if 0 <= b - 1 + jj < NT]
                        tiles = {}

                        def scores(gp):
                            g = gp // 2
                            pl = []
                            for jj in js:
                                bank, bk = mm[1 + (pexi[0] % 3)]
                                px = pex[pexi[0] % len(pex)]
                                pk = ("pex", pexi[0] % len(pex))
                                pexi[0] += 1
                                em.op("pe", lambda bank=bank, jj=jj: P.matmul(
                                    bank[:], lhsT=kTl[s][jj][:, gp, :], rhs=qT[s][:, 4 * g:4 * g + 4, :], start=True, stop=True),
                                      reads=[("kTl", s, jj), ("qT", s, g // 2)], writes=[bk])
                                em.op("act", lambda bank=bank, px=px: S.activation(out=px[:].rearrange("p c t -> p (c t)"), in_=bank[:],
                                                                                    func=AF.Exp),
                                      reads=[bk], writes=[pk])
                                if jj != 1:
                                    side = 0 if jj == 0 else 1
                                    if side == 0:
                                        em.op("dve", lambda px=px, side=side: V.tensor_tensor(
                                            out=px[:], in0=px[:], in1=bc_mid(maskb[s][:, side, :], 4), op=ALU.mult),
                                              reads=[pk, ("maskb", s, side)], writes=[pk])
                                    else:
                                        em.op("pool", lambda px=px, side=side: G.tensor_tensor(
                                            out=px[:], in0=px[:], in1=bc_mid(maskb[s][:, side, :], 4), op=ALU.mult),
                                              reads=[pk, ("maskb", s, side)], writes=[pk])
                                pl.append((jj, px, pk))
                            tiles[gp] = pl

                        def pv_norm(gp):
                            g, par = gp // 2, gp % 2
                            pl = tiles[gp]
                            bank, bk = mm[4 + (gp % 2)]
                            for n_, (jj, px, pk) in enumerate(pl):
                                em.op("pe", lambda bank=bank, jj=jj, px=px, n_=n_: P.matmul(
                                    bank[:], lhsT=val[s][jj][:, gp, :], rhs=px[:].rearrange("p c t -> p (c t)"),
                                    start=(n_ == 0), stop=False),
                                      reads=[("val", s, jj), pk], writes=[bk])
                            em.op("pe", lambda bank=bank: P.matmul(bank[:], lhsT=sel[:, par, :],
                                                                   rhs=prow[:, gp, :].unsqueeze(2).broadcast_to([128, 4, 128]),
                                                                   start=False, stop=True), writes=[bk])
                            nr = slice(par * 64, par * 64 + 64)
                            dr = slice((1 - par) * 64, (1 - par) * 64 + 64)
                            rc, on = rec[gp % 2], onrm[gp % 2]
                            rk, ok_ = ("rec", gp % 2), ("onrm", gp % 2)
                            em.op("act", lambda: S.activation(out=rc[dr, :], in_=bank[dr, :], func=AF.Ln), reads=[bk], writes=[rk])
                            em.op("act", lambda: S.activation(out=rc[dr, :], in_=rc[dr, :], func=AF.Exp, scale=-1.0),
                                  reads=[rk], writes=[rk])
                            em.op("dve", lambda: V.tensor_tensor(out=on[nr, :], in0=bank[nr, :], in1=rc[dr, :], op=ALU.mult),
                                  reads=[bk, rk], writes=[ok_])
                            em.op("pool", lambda: G.tensor_tensor(
                                out=gatedT[nr, 4 * g:4 * g + 4, :].rearrange("p c t -> p (c t)"), in0=on[nr, :],
                                in1=sgT[s][nr, 4 * g:4 * g + 4, :].rearrange("p c t -> p (c t)"), op=ALU.mult),
                                  reads=[ok_, ("sgT", s)], writes=[("gatedT", g)])

                        LAG = 2
                        for gp in range(8 + LAG):
                            if gp < 8:
                                scores(gp)
                            if gp >= LAG:
                                pv_norm(gp - LAG)
                        epilogue(w, b, b % 3, gatedT, "per_group", Wo, [mm[4], mm[5]], dst, last)

                    pipeline(stage1, stage2, NT, prefetch=lambda t: load_x(w, t, src))
                    em.flush()

        def layer_B(li, j, src, dst, last):
            with ExitStack() as st:
                load_gain(li, st, last)
                lg = sbuf(st, "lg", [128, 8], F32)
                kd16 = sbuf(st, "kd16", [128, 8], F32)
                cdt = sbuf(st, "cdt", [128, 8], F32)
                Dm = sbuf(st, "Dm", [128, 4, 128], F32)
                qdf = sbuf(st, "qdf", [128, 4, 128], F32)
                qdb = sbuf(st, "qdb", [128, 4, 128], F32)
                flg = sbuf(st, "flagsB", [128, 2, NT], F32)
                Sm = sbuf(st, "Sm", [128, 8, 512], F32)
                Sbf = sbuf(st, "Sbf", [128, 2, 2, 512], BF16)
                with ExitStack() as s0:
                    retT = sbuf(s0, "retT", [128, 4, 128], F32)
                    retP = sbuf(s0, "retP", [128, 2], F32)
                    t1 = sbuf(s0, "rt1", [128, 128], F32)
                    em.dma(lg[:], b_decay[j:j + 1, :].partition_broadcast(128), writes=[("lg",)], semkey="lg")
                    em.dma(retT[:], tb["retT"], writes=[("retT",)], semkey="retT")
                    em.dma(retP[:], tb["retP"], writes=[("retP",)], semkey="retP")
                    em.dma(flg[:], tb["flagsB"], writes=[("flg",)], semkey="flg")
                    em.op("act", lambda: S.activation(out=lg[:], in_=lg[:], func=AF.Exp, scale=-1.0), reads=[("lg",)], writes=[("lg",)])
                    em.op("dve", lambda: V.tensor_scalar(out=lg[:], in0=lg[:], scalar1=1.0, scalar2=None, op0=ALU.add),
                          reads=[("lg",)], writes=[("lg",)])
                    em.op("act", lambda: S.activation(out=lg[:], in_=lg[:], func=AF.Ln), reads=[("lg",)], writes=[("lg",)])
                    em.op("dve", lambda: V.tensor_scalar(out=lg[:], in0=lg[:], scalar1=-1.0, scalar2=None, op0=ALU.mult),
                          reads=[("lg",)], writes=[("lg",)])
                    em.op("act", lambda: S.activation(out=kd16[:, 0:4], in_=lg[:, 0:4], func=AF.Exp, scale=retP[:, 0:1]),
                          reads=[("lg",), ("retP",)], writes=[("kd16", 0)])
                    em.op("act", lambda: S.activation(out=kd16[:, 4:8], in_=lg[:, 4:8], func=AF.Exp, scale=retP[:, 1:2]),
                          reads=[("lg",), ("retP",)], writes=[("kd16", 1)])
                    em.op("dve", lambda: V.tensor_scalar(out=kd16[:], in0=kd16[:], scalar1=1.0 / 16, scalar2=None, op0=ALU.mult),
                          reads=[("kd16", 0), ("kd16", 1)], writes=[("kd16", 2)])
                    em.op("act", lambda: S.activation(out=cdt[:], in_=lg[:], func=AF.Exp, scale=128.0), reads=[("lg",)], writes=[("cdt",)])
                    for h in range(4):
                        em.op("act", lambda h=h: S.activation(out=qdf[:, h, :], in_=retT[:, 2, :], func=AF.Exp, scale=lg[:, h:h + 1]),
                              reads=[("lg",), ("retT",)], writes=[("qdf", h)])
                        em.op("act", lambda h=h: S.activation(out=qdb[:, h, :], in_=retT[:, 3, :], func=AF.Exp, scale=lg[:, 4 + h:5 + h]),
                              reads=[("lg",), ("retT",)], writes=[("qdb", h)])
                        em.op("dve", lambda h=h: V.tensor_scalar(out=t1[:], in0=retT[:, 0, :], scalar1=lg[:, h:h + 1], scalar2=None,
                                                                 op0=ALU.mult), reads=[("lg",), ("retT",)], writes=[("rt1",)])
                        em.op("dve", lambda h=h: V.scalar_tensor_tensor(out=t1[:], in0=retT[:, 1, :], scalar=lg[:, 4 + h:5 + h], in1=t1[:],
                                                                        op0=ALU.mult, op1=ALU.add),
                              reads=[("lg",), ("retT",), ("rt1",)], writes=[("rt1",)])
                        em.op("act", lambda h=h: S.activation(out=Dm[:, h, :], in_=t1[:], func=AF.Exp), reads=[("rt1",)], writes=[("Dm", h)])
                        em.op("dve", lambda h=h: V.tensor_scalar(out=Dm[:, h, :], in0=Dm[:, h, :], scalar1=1.0 / 16, scalar2=None,
                                                                 op0=ALU.mult), reads=[("Dm", h)], writes=[("Dm", h)])
                    em.op("pool", lambda: G.memset(Sm[:], 0.0), writes=[("Sm", i) for i in range(8)])
                    em.flush()

                def rope_half(x3, csb, nh, ta, tb_, tc, td):
                    x1, x2 = x3[:, :, 0:128], x3[:, :, 128:256]
                    cosb, sinb = bc_mid(csb[:, 0:128], nh), bc_mid(csb[:, 128:256], nh)
                    return x1, x2, cosb, sinb

                def state_update(kk, vb, direction, c, cdk, mmu):
                    for h in range(4):
                        for dc in range(2):
                            i = h * 2 + dc
                            bank, bk = mmu[i % len(mmu)]
                            em.op("pe", lambda h=h, dc=dc, bank=bank: P.matmul(bank[:], lhsT=kk[:, h, dc * 128:(dc + 1) * 128],
                                                                              rhs=vb[:, h * 512:(h + 1) * 512], start=True, stop=True),
                                  reads=[("ksc",), ("vbf",)], writes=[bk])
                            em.op("dve", lambda h=h, i=i, bank=bank: V.scalar_tensor_tensor(out=Sm[:, i, :], in0=Sm[:, i, :],
                                                                                        scalar=cdk[:, h:h + 1], in1=bank[:],
                                                                                        op0=ALU.mult, op1=ALU.add),
                                  reads=[bk, ("Sm", i), ("cdk",)], writes=[("Sm", i)])

                if bstop == 1:
                    return
                with ExitStack() as s1:
                    Wi = load_weights(s1, b_w_in[j][:, 1024:6144], 5120, "WiB1")
                    w = Work()
                    alloc_common(s1, w, 2)
                    cs = [sbuf(s1, "csB%d" % i, [128, 256], F32) for i in range(2)]
                    kf = [sbuf(s1, "kf%d" % i, [128, 4, 256], F32) for i in range(2)]
                    tt = [sbuf(s1, "rtmp%d" % i, [128, 4, 128], F32) for i in range(4)]
                    ksc = sbuf(s1, "ksc", [128, 4, 256], BF16)
                    vbf = [sbuf(s1, "vbf%d" % i, [128, E], BF16) for i in range(2)]
                    sgb = [sbuf(s1, "sgb%d" % i, [128, E], BF16) for i in range(2)]
                    kbs = sbuf(s1, "kbs", [128, 4, 256], BF16)
                    sck = sbuf(s1, "sck", [128, 8], F32)
                    mm = [(psum(s1, "mmB%d" % i, [128, 512], F32), ("mm", i)) for i in range(6)]

                    def stage1(t):
                        c = NT - 1 - t
                        s = t % 2
                        sx = prologue(w, c, li, src)
                        hTc, hk = w.hT[sx], ("hT", sx)
                        em.dma(cs[s][:], tb["ropeB"][c * 128:(c + 1) * 128, :], writes=[("cs", s)], semkey=("cs", s))
                        nb = [0]

                        def proj(col):
                            bank, bk = mm[nb[0] % 3]
                            nb[0] += 1
                            for kc in range(8):
                                em.op("pe", lambda kc=kc, bank=bank: P.matmul(bank[:], lhsT=hTc[:, kc, :], rhs=Wi[:, kc, col:col + 512],
                                                                              start=(kc == 0), stop=(kc == 7)),
                                      reads=[hk], writes=[bk])
                            return bank, bk
                        for cb in range(2):
                            bank, bk = proj(cb * 512)
                            em.op("act", lambda cb=cb, bank=bank: S.copy(out=kf[s][:, 2 * cb:2 * cb + 2, :].rearrange("p h d -> p (h d)"),
                                                                        in_=bank[:]), reads=[bk], writes=[("kf", s, cb)])
                        for cb in range(4):
                            bank, bk = proj(1024 + cb * 512)
                            if cb % 2 == 0:
                                em.op("act", lambda cb=cb, bank=bank: S.copy(out=vbf[s][:, cb * 512:(cb + 1) * 512], in_=bank[:]),
                                      reads=[bk], writes=[("vbf", s)])
                            else:
                                em.op("dve", lambda cb=cb, bank=bank: V.tensor_copy(out=vbf[s][:, cb * 512:(cb + 1) * 512], in_=bank[:]),
                                      reads=[bk], writes=[("vbf", s)])
                        for cb in range(4):
                            bank, bk = proj(3072 + cb * 512)
                            em.op("act", lambda cb=cb, bank=bank: S.activation(out=sgb[s][:, cb * 512:(cb + 1) * 512], in_=bank[:],
                                                                                func=AF.Silu), reads=[bk], writes=[("sgb", s)])

                    def stage2(t):
                        c = NT - 1 - t
                        s = t % 2
                        x1, x2, cosb, sinb = rope_half(kf[s][:], cs[s], 4, *[None] * 4)
                        kfk = [("kf", s, 0), ("kf", s, 1), ("cs", s)]
                        em.op("pool", lambda: G.tensor_tensor(out=tt[0][:], in0=x1, in1=cosb, op=ALU.mult), reads=kfk, writes=[("tt", 0)])
                        em.op("pool", lambda: G.tensor_tensor(out=tt[1][:], in0=x2, in1=sinb, op=ALU.mult), reads=kfk, writes=[("tt", 1)])
                        em.op("dve", lambda: V.tensor_tensor(out=tt[2][:], in0=x2, in1=cosb, op=ALU.mult), reads=kfk, writes=[("tt", 2)])
                        em.op("dve", lambda: V.tensor_tensor(out=tt[3][:], in0=x1, in1=sinb, op=ALU.mult), reads=kfk, writes=[("tt", 3)])
                        em.op("pool", lambda: G.tensor_tensor(out=tt[0][:], in0=tt[0][:], in1=tt[1][:], op=ALU.subtract),
                              reads=[("tt", 0), ("tt", 1)], writes=[("tt", 0)])
                        em.op("dve", lambda: V.tensor_tensor(out=tt[2][:], in0=tt[2][:], in1=tt[3][:], op=ALU.add),
                              reads=[("tt", 2), ("tt", 3)], writes=[("tt", 2)])
                        em.op("dve", lambda: V.tensor_scalar(out=sck[:, 0:4], in0=kd16[:, 4:8], scalar1=flg[:, 1, c:c + 1], scalar2=None,
                                                             op0=ALU.mult), writes=[("sck",)])
                        em.op("dve", lambda: V.tensor_scalar(out=sck[:, 4:8], in0=cdt[:, 4:8], scalar1=flg[:, 1, c:c + 1], scalar2=None,
                                                             op0=ALU.mult), writes=[("cdk",)])
                        em.op("act", lambda: S.copy(out=kbs[:, :, 0:128], in_=tt[0][:]), reads=[("tt", 0)], writes=[("kbs",)])
                        em.op("act", lambda: S.copy(out=kbs[:, :, 128:256], in_=tt[2][:]), reads=[("tt", 2)], writes=[("kbs",)])
                        em.dma(KB[c].rearrange("p (h d) -> p h d", h=4), kbs[:], reads=[("kbs",)], writes=[("D_KB", c)], semkey="kbst")
                        sb_ = sck[:, 0:4].unsqueeze(2).broadcast_to([128, 4, 128])
                        em.op("pool", lambda: G.tensor_tensor(out=ksc[:, :, 0:128], in0=tt[0][:], in1=sb_, op=ALU.mult),
                              reads=[("tt", 0), ("sck",)], writes=[("ksc",)])
                        em.op("dve", lambda: V.tensor_tensor(out=ksc[:, :, 128:256], in0=tt[2][:], in1=sb_, op=ALU.mult),
                              reads=[("tt", 2), ("sck",)], writes=[("ksc",)])
                        em.dma(SG[c], sgb[s][:], reads=[("sgb", s)], writes=[("D_SG", c)], semkey=("sgbst", s))
                        em.dma(VB[c], vbf[s][:], reads=[("vbf", s)], writes=[("D_VB", c)], semkey=("vbst", s))
                        for h in range(4):
                            em.op("act", lambda h=h: S.copy(out=Sbf[:, h % 2], in_=Sm[:, 2 * h:2 * h + 2, :]),
                                  reads=[("Sm", 2 * h), ("Sm", 2 * h + 1)], writes=[("Sbf", h % 2)])
                            em.dma(Sscr[c][:, h * 1024:(h + 1) * 1024].rearrange("p (i e) -> p i e", i=2), Sbf[:, h % 2],
                                   reads=[("Sbf", h % 2)], writes=[("D_S", c, h)], semkey=("Sst", h % 2))
                        for h in range(4):
                            for dc in range(2):
                                i = h * 2 + dc
                                bank, bk = mm[4 + i % 2]
                                em.op("pe", lambda h=h, dc=dc, bank=bank: P.matmul(bank[:], lhsT=ksc[:, h, dc * 128:(dc + 1) * 128],
                                                                                  rhs=vbf[s][:, h * 512:(h + 1) * 512], start=True, stop=True),
                                      reads=[("ksc",), ("vbf", s)], writes=[bk])
                                em.op("dve", lambda h=h, i=i, bank=bank: V.scalar_tensor_tensor(out=Sm[:, i, :], in0=Sm[:, i, :],
                                                                                            scalar=sck[:, 4 + h:5 + h], in1=bank[:],
                                                                                            op0=ALU.mult, op1=ALU.add),
                                      reads=[bk, ("Sm", i), ("cdk",)], writes=[("Sm", i)])

                    pipeline(stage1, stage2, NT)
                    em.flush()

                if bstop == 2:
                    return
                with ExitStack() as s2:
                    Wi = load_weights(s2, b_w_in[j][:, 0:1024], 1024, "WiB2")
                    Wo = load_weights(s2, b_w_out[j], D, "WoB")
                    w = Work()
                    alloc_common(s2, w, 3)
                    gjunk = sbuf(s2, "junkB", [128, 512], BF16)
                    cs = [sbuf(s2, "csB%d" % i, [128, 256], F32) for i in range(2)]
                    qkf = sbuf(s2, "qkf", [128, 4, 256], F32)
                    tt = [sbuf(s2, "rtmp%d" % i, [128, 4, 128], F32) for i in range(4)]
                    qbf = sbuf(s2, "qbf", [128, 4, 256], BF16)
                    kbf = [sbuf(s2, "kbf%d" % i, [128, 4, 256], BF16) for i in range(2)]
                    ksc = [sbuf(s2, "ksc%d" % i, [128, 4, 256], BF16) for i in range(2)]
                    qTs = [sbuf(s2, "qTs%d" % i, [128, 4, 8, 128], BF16) for i in range(2)]
                    vbf = [sbuf(s2, "vbf%d" % i, [128, E], BF16) for i in range(2)]
                    sgh = [sbuf(s2, "sgh%d" % i, [128, 512], BF16) for i in range(2)]
                    PT = [sbuf(s2, "PT%d" % i, [128, 128], BF16) for i in range(4)]
                    gtok = sbuf(s2, "gtok", [128, E], BF16)
                    Sbl = sbuf(s2, "Sbl", [128, 2, 2, 512], BF16)
                    sck = [sbuf(s2, "sck%d" % i, [128, 8], F32) for i in range(2)]
                    gst = sbuf(s2, "gst", [128, 4, 4], F32)
                    mm = [(psum(s2, "mmB%d" % i, [128, 512], F32), ("mm", i)) for i in range(5)]
                    tpB = psum(s2, "tpB", [128, 8, 128], BF16)
                    em.op("pool", lambda: G.memset(Sm[:], 0.0), writes=[("Sm", i) for i in range(8)])

                    def stage1(c):
                        sx = prologue(w, c, li, src)
                        s = c % 2
                        hTc, hk = w.hT[sx % 2], ("hT", sx % 2)
                        em.dma(cs[s][:], tb["ropeB"][c * 128:(c + 1) * 128, :], writes=[("cs", s)], semkey=("cs", s))
                        em.op("dve", lambda: V.tensor_scalar(out=sck[s][:, 0:4], in0=kd16[:, 0:4], scalar1=flg[:, 0, c:c + 1], scalar2=None,
                                                             op0=ALU.mult), writes=[("sck", s)])
                        em.op("dve", lambda: V.tensor_scalar(out=sck[s][:, 4:8], in0=cdt[:, 0:4], scalar1=flg[:, 0, c:c + 1], scalar2=None,
                                                             op0=ALU.mult), writes=[("cdk", s)])
                        for cb in range(2):
                            bank, bk = mm[0]
                            col = cb * 512
                            for kc in range(8):
                                em.op("pe", lambda kc=kc, col=col, bank=bank: P.matmul(bank[:], lhsT=hTc[:, kc, :],
                                                                                     rhs=Wi[:, kc, col:col + 512],
                                                                                     start=(kc == 0), stop=(kc == 7)),
                                      reads=[hk], writes=[bk])
                            em.op("act", lambda cb=cb, bank=bank: S.copy(out=qkf[:, 2 * cb:2 * cb + 2, :].rearrange("p h d -> p (h d)"),
                                                                        in_=bank[:]), reads=[bk], writes=[("qkf", cb)])
                        x1, x2, cosb, sinb = rope_half(qkf[:], cs[s], 4, *[None] * 4)
                        kfk = [("qkf", 0), ("qkf", 1), ("cs", s)]
                        em.op("pool", lambda: G.tensor_tensor(out=tt[0][:], in0=x1, in1=cosb, op=ALU.mult), reads=kfk, writes=[("tt", 0)])
                        em.op("pool", lambda: G.tensor_tensor(out=tt[1][:], in0=x2, in1=sinb, op=ALU.mult), reads=kfk, writes=[("tt", 1)])
                        em.op("dve", lambda: V.tensor_tensor(out=tt[2][:], in0=x2, in1=cosb, op=ALU.mult), reads=kfk, writes=[("tt", 2)])
                        em.op("dve", lambda: V.tensor_tensor(out=tt[3][:], in0=x1, in1=sinb, op=ALU.mult), reads=kfk, writes=[("tt", 3)])
                        em.op("pool", lambda: G.tensor_tensor(out=qbf[:, :, 0:128], in0=tt[0][:], in1=tt[1][:], op=ALU.subtract),
                              reads=[("tt", 0), ("tt", 1)], writes=[("qbf",)])
                        em.op("dve", lambda: V.tensor_tensor(out=qbf[:, :, 128:256], in0=tt[2][:], in1=tt[3][:], op=ALU.add),
                              reads=[("tt", 2), ("tt", 3)], writes=[("qbf",)])
                        em.dma(kbf[s][:], KB[c].rearrange("p (h d) -> p h d", h=4), writes=[("kbf", s)], semkey=("kbf", s))
                        em.dma(vbf[s][:], VB[c], writes=[("vbf", s)], semkey=("vbf", s))
                        em.op("pool", lambda: G.tensor_tensor(out=ksc[s][:], in0=kbf[s][:],
                                                              in1=sck[s][:, 0:4].unsqueeze(2).broadcast_to([128, 4, 256]), op=ALU.mult),
                              reads=[("kbf", s), ("sck", s)], writes=[("ksc", s)])
                        qb2 = qbf[:].rearrange("p h d -> p (h d)")
                        kb2 = kbf[s][:].rearrange("p h d -> p (h d)")
                        for m in range(8):
                            em.op("pe", lambda m=m: P.transpose(out=w.tp[:, m, :], in_=qb2[:, m * 128:(m + 1) * 128], identity=ident[:]),
                                  reads=[("qbf",)], writes=[("tp",)])
                        for m in range(8):
                            em.op("pe", lambda m=m: P.transpose(out=w.tp[:, 8 + m, :], in_=kb2[:, m * 128:(m + 1) * 128], identity=ident[:]),
                                  reads=[("kbf", s)], writes=[("tp2",)])
                        q4 = qTs[s]
                        em.op("act", lambda: S.copy(out=q4[:, 0], in_=w.tp[:, 0:8, :]), reads=[("tp",)], writes=[("qT", s)])
                        for h in range(4):
                            em.op("dve", lambda h=h: V.tensor_tensor(out=q4[:, 1, 2 * h:2 * h + 2, :], in0=q4[:, 0, 2 * h:2 * h + 2, :],
                                                                     in1=bc_mid(qdf[:, h, :], 2), op=ALU.mult),
                                  reads=[("qT", s)], writes=[("qTf", s)])
                            em.op("pool", lambda h=h: G.tensor_tensor(out=q4[:, 2, 2 * h:2 * h + 2, :], in0=q4[:, 0, 2 * h:2 * h + 2, :],
                                                                      in1=bc_mid(qdb[:, h, :], 2), op=ALU.mult),
                                  reads=[("qT", s)], writes=[("qTb", s)])
                        em.op("act", lambda: S.copy(out=q4[:, 3], in_=w.tp[:, 8:16, :]), reads=[("tp2",)], writes=[("kT", s)])

                    def stage2(c):
                        s = c % 2
                        q4 = qTs[s]
                        gatedT = q4[:, 0:2].rearrange("p a c t -> p (a c) t")
                        sb, sk_ = mm[1]
                        for h in range(4):
                            for dc in range(2):
                                em.op("pe", lambda dc=dc, h=h: P.matmul(sb[:, h * 128:(h + 1) * 128], lhsT=q4[:, 3, h * 2 + dc, :],
                                                                        rhs=q4[:, 0, h * 2 + dc, :], start=(dc == 0), stop=(dc == 1)),
                                      reads=[("kT", s), ("qT", s)], writes=[sk_])
                        for h in range(4):
                            em.op("dve", lambda h=h: V.tensor_tensor(out=PT[h][:], in0=sb[:, h * 128:(h + 1) * 128], in1=Dm[:, h, :],
                                                                     op=ALU.mult), reads=[sk_], writes=[("PT", h)])
                        for h in range(4):
                            em.dma(sgh[h % 2][:], SG[c][:, h * 512:(h + 1) * 512], writes=[("sgh", h % 2)], semkey=("sgh", h % 2))
                            em.op("act", lambda h=h: S.copy(out=Sbf[:, h % 2], in_=Sm[:, 2 * h:2 * h + 2, :]),
                                  reads=[("Sm", 2 * h), ("Sm", 2 * h + 1)], writes=[("Sbf", h % 2)])
                            em.dma(Sbl[:, h % 2], Sscr[c][:, h * 1024:(h + 1) * 1024].rearrange("p (i e) -> p i e", i=2),
                                   writes=[("Sbl", h % 2)], semkey=("Sbl", h % 2))
                            pt = PT[h]
                            ob, ok_ = mm[2 + h % 2]
                            em.op("pe", lambda h=h, ob=ob, pt=pt: P.matmul(ob[:], lhsT=pt[:], rhs=vbf[s][:, h * 512:(h + 1) * 512],
                                                                           start=True, stop=False),
                                  reads=[("PT", h), ("vbf", s)], writes=[ok_])
                            for dc in range(2):
                                i = h * 2 + dc
                                em.op("pe", lambda i=i, ob=ob, h=h, dc=dc: P.matmul(ob[:], lhsT=q4[:, 1, i, :], rhs=Sbf[:, h % 2, dc, :],
                                                                               start=False, stop=False),
                                      reads=[("qTf", s), ("Sbf", h % 2)], writes=[ok_])
                            for dc in range(2):
                                i = h * 2 + dc
                                em.op("pe", lambda i=i, ob=ob, dc=dc, h=h: P.matmul(ob[:], lhsT=q4[:, 2, i, :], rhs=Sbl[:, h % 2, dc, :],
                                                                               start=False, stop=(dc == 1)),
                                      reads=[("qTb", s), ("Sbl", h % 2)], writes=[ok_])
                            for dc in range(2):
                                i = h * 2 + dc
                                bank, bk = mm[4] if dc == 0 else mm[1]
                                em.op("pe", lambda h=h, dc=dc, bank=bank: P.matmul(bank[:], lhsT=ksc[s][:, h, dc * 128:(dc + 1) * 128],
                                                                                  rhs=vbf[s][:, h * 512:(h + 1) * 512], start=True, stop=True),
                                      reads=[("ksc", s), ("vbf", s), ("PT", 0), ("PT", 1), ("PT", 2), ("PT", 3)], writes=[bk])
                                em.op("dve", lambda h=h, i=i, bank=bank: V.scalar_tensor_tensor(out=Sm[:, i, :], in0=Sm[:, i, :],
                                                                                            scalar=sck[s][:, 4 + h:5 + h], in1=bank[:],
                                                                                            op0=ALU.mult, op1=ALU.add),
                                      reads=[bk, ("Sm", i), ("cdk", s)], writes=[("Sm", i)])
                            gs = gst[:, h, :]
                            em.op("act", lambda ob=ob, gs=gs: S.activation(out=gjunk[:], in_=ob[:], func=AF.Square, accum_out=gs[:, 0:1]),
                                  reads=[ok_], writes=[("gjunk",), ("gst", h)])
                            em.op("act", lambda gs=gs: S.activation(out=gs[:, 2:3], in_=gs[:, 0:1], func=AF.Ln, scale=1.0 / 512,
                                                                    bias=epsb[:, 0:1]),
                                  reads=[("gst", h)], writes=[("gst", h)])
                            em.op("act", lambda gs=gs: S.activation(out=gs[:, 3:4], in_=gs[:, 2:3], func=AF.Exp, scale=-0.5),
                                  reads=[("gst", h)], writes=[("gst", h)])
                            em.op("dve", lambda h=h, ob=ob, gs=gs: V.scalar_tensor_tensor(out=gtok[:, h * 512:(h + 1) * 512], in0=ob[:],
                                                                                        scalar=gs[:, 3:4], in1=sgh[h % 2][:], op0=ALU.mult,
                                                                                        op1=ALU.mult),
                                  reads=[ok_, ("gst", h), ("sgh", h % 2)], writes=[("gtok", h)])
                        for half in range(2):
                            for m in range(8):
                                mm_ = half * 8 + m
                                em.op("pe", lambda m=m, mm_=mm_: P.transpose(out=tpB[:, m, :], in_=gtok[:, mm_ * 128:(mm_ + 1) * 128],
                                                                             identity=ident[:]),
                                      reads=[("gtok", mm_ // 4)], writes=[("tpB",)])
                            if half == 0:
                                em.op("act", lambda: S.copy(out=gatedT[:, 0:8, :], in_=tpB[:]), reads=[("tpB",)], writes=[("qT", s)])
                            else:
                                em.op("dve", lambda: V.tensor_copy(out=gatedT[:, 8:16, :], in_=tpB[:]), reads=[("tpB",)], writes=[("qTf", s)])
                        epilogue(w, c, c % 3, gatedT, [("qT", s), ("qTf", s)], Wo, [mm[2], mm[3]], dst, last)

                    pipeline(stage1, stage2, NT)
                    em.flush()

        def layer_C(li, j, src, dst, last):
            with ExitStack() as st:
                load_gain(li, st, False)
                Wi = load_weights(st, c_w_in[j][:, 0:2048], 2048, "WiCu")
                w = Work()
                alloc_common(st, w, 2)
                FC = sbuf(st, "FC", [128, 4, 2, 512], BF16)
                zb = [sbuf(st, "zb%d" % i, [128, 2, 2048], BF16) for i in range(2)]
                mm = [(psum(st, "mmC%d" % i, [128, 512], F32), ("mm", i)) for i in range(6)]
                em.dma(FC[:], tb["FC"], writes=[("FC",)], semkey="FC")

                hT4 = [sbuf(st, "hT4C_%d" % i, [128, 8, 512], BF16) for i in range(2)]
                uT4s = [sbuf(st, "uT4_%d" % i, [128, 16, 512], BF16) for i in range(2)]

                def uproj4(ss):
                    uT4 = uT4s[ss]
                    hks = [("hT4", ss, tl) for tl in range(4)]
                    for m in range(16):
                        bank, bk = mm[m % 2]
                        for kc in range(8):
                            em.op("pe", lambda kc=kc, m=m, bank=bank: P.matmul(bank[:], lhsT=Wi[:, kc, m * 128:(m + 1) * 128],
                                                                              rhs=hT4[ss][:, kc, :], start=(kc == 0), stop=(kc == 7)),
                                  reads=hks, writes=[bk])
                        if m % 2 == 0:
                            em.op("act", lambda m=m, bank=bank: S.copy(out=uT4[:, m, :], in_=bank[:]), reads=[bk], writes=[("uT", ss, m // 4)])
                        else:
                            em.op("dve", lambda m=m, bank=bank: V.tensor_copy(out=uT4[:, m, :], in_=bank[:]), reads=[bk],
                                  writes=[("uT", ss, m // 4)])

                def body1(t):
                    tl = t % 4
                    uT4 = uT4s[(t // 4) % 2]
                    ss = (t // 4) % 2
                    zs = t % 2
                    for grp in range(4):
                        for ri in range(2):
                            n_ = grp * 2 + ri
                            bank, bk = mm[2 + n_ % 4]
                            for cc in range(4):
                                em.op("pe", lambda cc=cc, grp=grp, ri=ri, bank=bank: P.matmul(
                                    bank[:], lhsT=uT4[:, grp * 4 + cc, tl * 128:(tl + 1) * 128], rhs=FC[:, cc, ri, :],
                                    start=(cc == 0), stop=(cc == 3)),
                                      reads=[("uT", ss, grp), ("FC",)], writes=[bk])
                            if n_ % 2 == 0:
                                em.op("act", lambda grp=grp, ri=ri, bank=bank: S.copy(out=zb[zs][:, ri, grp * 512:(grp + 1) * 512], in_=bank[:]),
                                      reads=[bk], writes=[("zb", zs)])
                            else:
                                em.op("dve", lambda grp=grp, ri=ri, bank=bank: V.tensor_copy(out=zb[zs][:, ri, grp * 512:(grp + 1) * 512],
                                                                                            in_=bank[:]), reads=[bk], writes=[("zb", zs)])
                    em.dma(Zscr[t * 128:(t + 1) * 128, :].rearrange("p (r c) -> p r c", r=2), zb[zs][:], reads=[("zb", zs)],
                           writes=[("D_Z", t)], semkey=("zst", zs))

                def c1s1(sb):
                    ss = sb % 2
                    for tl in range(4):
                        prologue(w, 4 * sb + tl, li, src, hT4[ss][:, :, tl * 128:(tl + 1) * 128], ("hT4", ss, tl))
                    uproj4(ss)

                def c1s2(sb):
                    for tl in range(4):
                        body1(4 * sb + tl)

                pipeline(c1s1, c1s2, NT // 4)
                em.flush()
            with ExitStack() as st:
                Gt = sbuf(st, "Gt", [NT, 128, 3, NT], BF16)
                zr = [sbuf(st, "zr%d" % i, [NT, 2, 2048], BF16) for i in range(2)]
                Bs = [sbuf(st, "Bs%d" % i, [NT, 2, 2048], BF16) for i in range(2)]
                mm = [(psum(st, "mmC%d" % i, [128, 512], F32), ("mm", i)) for i in range(8)]
                em.dma(Gt[:], tb["G"], writes=[("Gt",)], semkey="Gt")

                def body2a(n2):
                    s = n2 % 2
                    em.dma(zr[s][:], Zscr[n2::128, :].rearrange("p (r c) -> p r c", r=2), writes=[("zr", s)], semkey=("zr", s))
                    for cb in range(4):
                        (bre, kre), (bim, kim) = mm[(2 * cb) % 8], mm[(2 * cb + 1) % 8]
                        cs_ = slice(cb * 512, (cb + 1) * 512)
                        for (bank, bk, parts) in ((bre, kre, ((0, 0), (2, 1))), (bim, kim, ((1, 0), (0, 1)))):
                            for n_, (gi, ri) in enumerate(parts):
                                em.op("pe", lambda bank=bank, gi=gi, ri=ri, n_=n_, cs_=cs_: P.matmul(
                                    bank[0:NT, :], lhsT=Gt[:, n2, gi, :], rhs=zr[s][:, ri, cs_], start=(n_ == 0), stop=(n_ == 1)),
                                      reads=[("zr", s), ("Gt",)], writes=[bk])
                        em.op("act", lambda bre=bre, cs_=cs_: S.copy(out=Bs[s][:, 0, cs_], in_=bre[0:NT, :]), reads=[kre], writes=[("Bs", s)])
                        em.op("dve", lambda bim=bim, cs_=cs_: V.tensor_copy(out=Bs[s][:, 1, cs_], in_=bim[0:NT, :]), reads=[kim],
                              writes=[("Bs", s)])
                    em.dma(Bscr[:, n2, :].rearrange("p (r c) -> p r c", r=2), Bs[s][:], reads=[("Bs", s)], writes=[("D_B", n2)],
                           semkey=("bst", s))

                for n2 in range(128):
                    body2a(n2)
                em.flush()
            with ExitStack() as st:
                Ht = sbuf(st, "Ht", [128, 2, 128], BF16)
                Br = [sbuf(st, "Br%d" % i, [128, 2, 2048], BF16) for i in range(2)]
                Ms = [sbuf(st, "Ms%d" % i, [128, 2048], BF16) for i in range(2)]
                mm = [(psum(st, "mmC%d" % i, [128, 512], F32), ("mm", i)) for i in range(8)]
                em.dma(Ht[:], tb["H"], writes=[("Ht",)], semkey="Ht")

                def body2b(q):
                    s = q % 2
                    em.dma(Br[s][:], Bscr[q].rearrange("p (r c) -> p r c", r=2), writes=[("Br", s)], semkey=("Br", s))
                    for cb in range(4):
                        bank, bk = mm[(q * 4 + cb) % 8]
                        cs_ = slice(cb * 512, (cb + 1) * 512)
                        for ri in range(2):
                            em.op("pe", lambda bank=bank, ri=ri, cs_=cs_: P.matmul(bank[:], lhsT=Ht[:, ri, :], rhs=Br[s][:, ri, cs_],
                                                                                  start=(ri == 0), stop=(ri == 1)),
                                  reads=[("Br", s), ("Ht",)], writes=[bk])
                        if cb % 2 == 0:
                            em.op("act", lambda bank=bank, cs_=cs_: S.copy(out=Ms[s][:, cs_], in_=bank[:]), reads=[bk], writes=[("Ms", s)])
                        else:
                            em.op("dve", lambda bank=bank, cs_=cs_: V.tensor_copy(out=Ms[s][:, cs_], in_=bank[:]), reads=[bk],
                                  writes=[("Ms", s)])
                    em.dma(Mscr[q * 128:(q + 1) * 128, :], Ms[s][:], reads=[("Ms", s)], writes=[("D_M", q)], semkey=("mst", s))

                for q in range(NT):
                    body2b(q)
                em.flush()
            with ExitStack() as st:
                load_gain(li, st, last)
                Wi = load_weights(st, c_w_in[j][:, 2048:4096], 2048, "WiCg")
                Wo = load_weights(st, c_w_out[j], D, "WoC")
                w = Work()
                alloc_common(st, w, 8)
                hT4 = [sbuf(st, "hT4C3_%d" % i, [128, 8, 512], BF16) for i in range(2)]
                PAB = sbuf(st, "PAB", [128, 2, 128], BF16)
                MA = [sbuf(st, "MA%d" % i, [128, E], BF16) for i in range(2)]
                MB = [sbuf(st, "MB%d" % i, [128, E], BF16) for i in range(2)]
                sgT = [sbuf(st, "sgTC%d" % i, [128, 16, 512], BF16) for i in range(2)]
                gatedT = sbuf(st, "gatedTC", [128, 16, 128], BF16)
                mm = [(psum(st, "mmC%d" % i, [128, 512], F32), ("mm", i)) for i in range(6)]
                em.dma(PAB[:], tb["PAB"], writes=[("PAB",)], semkey="PAB")

                def gate4(ss):
                    hks = [("hT4", ss, tl) for tl in range(4)]
                    for m in range(16):
                        bank, bk = mm[m % 2]
                        for kc in range(8):
                            em.op("pe", lambda kc=kc, m=m, bank=bank: P.matmul(bank[:], lhsT=Wi[:, kc, m * 128:(m + 1) * 128],
                                                                              rhs=hT4[ss][:, kc, :], start=(kc == 0), stop=(kc == 7)),
                                  reads=hks, writes=[bk])
                        em.op("act", lambda m=m, bank=bank: S.activation(out=sgT[ss][:, m, :], in_=bank[:], func=AF.Silu),
                              reads=[bk], writes=[("sgT", ss, m // 4)])

                def body3(t, ss):
                    s = t % 8
                    tl = t % 4
                    em.dma(MA[t % 2][:], Mscr[t:t + TSA * 127 + 1:TSA, :], writes=[("MA", t % 2)], semkey=("MA", t % 2))
                    bB = 128 * TSB * (t // TSB) + t % TSB
                    em.dma(MB[t % 2][:], Mscr[bB:bB + TSB * 127 + 1:TSB, :], writes=[("MB", t % 2)], semkey=("MB", t % 2))
                    for m4 in range(4):
                        bank, bk = mm[2 + m4 % 2]
                        for mi in range(4):
                            m = m4 * 4 + mi
                            em.op("pe", lambda m=m, mi=mi, bank=bank: P.matmul(bank[:, mi * 128:(mi + 1) * 128],
                                                                              lhsT=MA[t % 2][:, m * 128:(m + 1) * 128], rhs=PAB[:, 0, :],
                                                                              start=True, stop=False),
                                  reads=[("MA", t % 2), ("PAB",)], writes=[bk])
                            em.op("pe", lambda m=m, mi=mi, bank=bank: P.matmul(bank[:, mi * 128:(mi + 1) * 128],
                                                                              lhsT=MB[t % 2][:, m * 128:(m + 1) * 128], rhs=PAB[:, 1, :],
                                                                              start=False, stop=True),
                                  reads=[("MB", t % 2), ("PAB",)], writes=[bk])
                        em.op("dve", lambda m4=m4, bank=bank: V.tensor_tensor(
                            out=gatedT[:, m4 * 4:(m4 + 1) * 4, :], in0=bank[:].rearrange("p (c t) -> p c t", c=4),
                            in1=sgT[ss][:, m4 * 4:(m4 + 1) * 4, tl * 128:(tl + 1) * 128], op=ALU.mult),
                              reads=[bk, ("sgT", ss, m4)], writes=[("gatedT",)])
                    epilogue(w, t, s, gatedT, ("gatedT",), Wo, [mm[4], mm[5]], dst, last)

                def c3s1(sb):
                    ss = sb % 2
                    for tl in range(4):
                        prologue(w, 4 * sb + tl, li, src, hT4[ss][:, :, tl * 128:(tl + 1) * 128], ("hT4", ss, tl))
                    gate4(ss)

                def c3s2(sb):
                    for tl in range(4):
                        body3(4 * sb + tl, sb % 2)

                pipeline(c3s1, c3s2, NT // 4)
                em.flush()

        cnt = {"A": 0, "B": 0, "C": 0}
        src = x_in
        for li, kind in enumerate(layers):
            last = final_norm and (li == len(layers) - 1)
            dst = y_out if li == len(layers) - 1 else xscr
            j = cnt[kind]
            cnt[kind] += 1
            if kind == "A":
                layer_A(li, j, src, dst, last)
            elif kind == "B":
                layer_B(li, j, src, dst, last)
            else:
                layer_C(li, j, src, dst, last)
            src = dst
        nc._em_n_instr = em.n_instr
    return nc


def kernel(x_prompt, x_sample, norm_g, final_norm_g, a_w_in, a_sink, a_w_out,
           b_w_in, b_decay, b_w_out, c_w_in, c_w_out):
    NT, TSA, TSB = 128, 128, 16
    nc = build(NT, TSA, TSB)
    f = lambda a: np.ascontiguousarray(np.asarray(a, dtype=np.float32))
    common = {"norm_g": f(norm_g), "final_norm_g": f(final_norm_g).reshape(1, D), "a_w_in": f(a_w_in), "a_sink": f(a_sink),
              "a_w_out": f(a_w_out), "b_w_in": f(b_w_in), "b_decay": f(b_decay).reshape(1, 8), "b_w_out": f(b_w_out),
              "c_w_in": f(c_w_in), "c_w_out": f(c_w_out)}
    tabA = const_tables(NT, TSA, 1.0, 0.0, TSA, TSB)
    tabB = const_tables(NT, TSB, 0.0, 1.0, TSA, TSB)
    xp = f(x_prompt)
    xs = f(x_sample)
    in_maps = []
    for c in range(8):
        if c < 2:
            m = dict(common, **tabA)
            m["x"] = xs[c]
        else:
            m = dict(common, **tabB)
            xx = np.zeros((8, 2048, D), np.float32)
            seqs = list(range((c - 2) * 6, min((c - 2) * 6 + 6, 32)))
            xx[:len(seqs)] = xp[seqs]
            m["x"] = xx.reshape(NT * 128, D)
        in_maps.append(m)
    res = run_bass_kernel_spmd(nc, in_maps, core_ids=list(range(8)))
    y_s = np.stack([res.results[c]["y"] for c in range(2)], 0).astype(np.float32)
    y_p = np.zeros((32, 2048, D), np.float32)
    for c in range(2, 8):
        seqs = list(range((c - 2) * 6, min((c - 2) * 6 + 6, 32)))
        yy = res.results[c]["y"].reshape(8, 2048, D)
        y_p[seqs] = yy[:len(seqs)]
    return (y_p, y_s)
```
